# Optimizing a Trainium2 kernel written in Bass

```python
import math
import jax, jax.numpy as jnp
from jax import lax
import numpy as np

D_MODEL = 1024
BATCH = 32
SEQ = 256
DEPTH = 4
DEC_BATCH = 4
DEC_SEQ = 1024
PAST_LEN = 512

GRID_W = 64
N_MIXERS = 2
N_NA_LAYERS = (DEPTH + N_MIXERS - 1) // N_MIXERS
N_SSD_LAYERS = DEPTH // N_MIXERS
NA_HEADS = 16
NA_HEAD_DIM = D_MODEL // NA_HEADS
NA_WIN_H = 8
NA_WIN_W = 16
DENSE_BLOCK_KEYS = 2048
Q_BLOCK = 128
SSD_EXPAND = 2
SSD_D_INNER = SSD_EXPAND * D_MODEL
SSD_HEAD_DIM = 64
SSD_HEADS = SSD_D_INNER // SSD_HEAD_DIM
SSD_GROUPS = 8
SSD_D_STATE = 128
SSD_CONV = 3
SSD_CHUNK = 128
SSD_CONV_DIM = SSD_D_INNER + 2 * SSD_GROUPS * SSD_D_STATE
SSD_IN_DIM = SSD_D_INNER + SSD_CONV_DIM + 2 * SSD_HEADS
D_FF = -(-8 * D_MODEL // (3 * 256)) * 256
RMS_EPS = 1e-6

kernel_name = 'hybrid_na_ssd_dit_step'


def rmsnorm(x, g):
    xf = x.astype(jnp.float32)
    y = xf * lax.rsqrt(jnp.mean(xf * xf, axis=-1, keepdims=True) + RMS_EPS)
    return (y * g.astype(jnp.float32)).astype(x.dtype)


def ada_params(cond, w, b):
    m = jax.nn.silu(cond) @ w + b
    return jnp.split(m[:, None, :], 6, axis=-1)


def modulate(h, shift, scale):
    return h * (1 + scale) + shift


def swiglu(h, w_gate_up, w_down):
    g, u = jnp.split(h @ w_gate_up, 2, axis=-1)
    return (jax.nn.silu(g) * u) @ w_down


def na_qkv(h, w_qkv):
    n, l, _ = h.shape
    qkv = (h @ w_qkv).reshape(n, l, 3, NA_HEADS, NA_HEAD_DIM)
    return qkv[:, :, 0], qkv[:, :, 1], qkv[:, :, 2]


def dense_attention(q, k, v):
    scale = NA_HEAD_DIM ** -0.5

    def attend(qb):
        s = jnp.einsum('bqhd,bhkd->bhqk', qb, k).astype(jnp.float32) * scale
        p = jax.nn.softmax(s, axis=-1).astype(v.dtype)
        return jnp.einsum('bhqk,bhkd->bqhd', p, v)

    n, lq = q.shape[:2]
    if k.shape[2] < DENSE_BLOCK_KEYS:
        return attend(q)
    qb = q.reshape(n, lq // Q_BLOCK, Q_BLOCK, NA_HEADS, NA_HEAD_DIM).swapaxes(0, 1)
    out = lax.map(attend, qb)
    return out.swapaxes(0, 1).reshape(q.shape)


def neighbourhood_attention(q, k, v, k_ctx, v_ctx, rpb):
    n, t, nh, hd = q.shape
    rows = t // GRID_W
    kh = min(NA_WIN_H, rows)
    kw = NA_WIN_W
    scale = hd ** -0.5
    qg = q.reshape(n, rows, GRID_W, nh, hd)
    kg = k.reshape(n, rows, GRID_W, nh, hd)
    vg = v.reshape(n, rows, GRID_W, nh, hd)
    cols = jnp.arange(GRID_W)
    col_start = jnp.clip(cols - kw // 2, 0, GRID_W - kw)
    col_mask = (cols[None, :] >= col_start[:, None]) & (cols[None, :] < col_start[:, None] + kw)
    dc_idx = jnp.clip(cols[None, :] - cols[:, None] + kw - 1, 0, 2 * kw - 2)

    def row_block(args):
        r, q_r = args
        r0 = jnp.clip(r - kh // 2, 0, rows - kh)
        k_r = lax.dynamic_slice_in_dim(kg, r0, kh, axis=1)
        v_r = lax.dynamic_slice_in_dim(vg, r0, kh, axis=1)
        dr_idx = r0 + jnp.arange(kh) - r + NA_WIN_H - 1
        bias = rpb[:, dr_idx[None, :, None], dc_idx[:, None, :]].astype(jnp.float32)
        s_loc = jnp.einsum('bqhd,bkwhd->bhqkw', q_r, k_r).astype(jnp.float32) * scale + bias
        s_loc = jnp.where(col_mask[:, None, :], s_loc, -jnp.inf).reshape(n, nh, GRID_W, kh * GRID_W)
        s_ctx = jnp.einsum('bqhd,bhcd->bhqc', q_r, k_ctx).astype(jnp.float32) * scale
        p = jax.nn.softmax(jnp.concatenate([s_loc, s_ctx], axis=-1), axis=-1).astype(v.dtype)
        p_loc = p[..., :kh * GRID_W].reshape(n, nh, GRID_W, kh, GRID_W)
        p_ctx = p[..., kh * GRID_W:]
        return (jnp.einsum('bhqkw,bkwhd->bqhd', p_loc, v_r)
                + jnp.einsum('bhqc,bhcd->bqhd', p_ctx, v_ctx))

    out = lax.map(row_block, (jnp.arange(rows), qg.swapaxes(0, 1)))
    return out.swapaxes(0, 1).reshape(n, t, nh, hd)


def na_context_mixer(h, w_qkv, w_o):
    n, l, _ = h.shape
    q, k, v = na_qkv(h, w_qkv)
    k = k.transpose(0, 2, 1, 3)
    v = v.transpose(0, 2, 1, 3)
    o = dense_attention(q, k, v)
    return o.reshape(n, l, D_MODEL) @ w_o, k, v


def na_latent_mixer(h, k_ctx, v_ctx, w_qkv, w_o, rpb):
    n, l, _ = h.shape
    q, k, v = na_qkv(h, w_qkv)
    o = neighbourhood_attention(q, k, v, k_ctx, v_ctx, rpb)
    return o.reshape(n, l, D_MODEL) @ w_o


def segsum(a):
    cs = jnp.cumsum(a, axis=-1)
    t = a.shape[-1]
    d = cs[..., :, None] - cs[..., None, :]
    return jnp.where(jnp.tril(jnp.ones((t, t), dtype=bool)), d, -jnp.inf)


def ssd_chunked(x, da, bm, cm, init):
    n, l, nh, p = x.shape
    g, ns = bm.shape[2], bm.shape[3]
    r = nh // g
    q = SSD_CHUNK
    nc = l // q
    x = x.reshape(n, nc, q, g, r, p)
    bm = bm.reshape(n, nc, q, g, ns)
    cm = cm.reshape(n, nc, q, g, ns)
    a = da.reshape(n, nc, q, g, r).transpose(0, 3, 4, 1, 2)
    a_cs = jnp.cumsum(a, axis=-1)
    lmat = jnp.exp(segsum(a))
    cb = jnp.einsum('bclgn,bcsgn->bcgls', cm, bm)
    y_diag = jnp.einsum('bcgls,bgrcls,bcsgrp->bclgrp', cb, lmat, x)
    decay_states = jnp.exp(a_cs[..., -1:] - a_cs)
    states = jnp.einsum('bclgn,bgrcl,bclgrp->bcgrpn', bm, decay_states, x)
    states = jnp.concatenate([init.reshape(n, 1, g, r, p, ns), states], axis=1)
    chunk_a = jnp.pad(a_cs[..., -1], ((0, 0), (0, 0), (0, 0), (1, 0)))
    decay_chunk = jnp.exp(segsum(chunk_a))
    new_states = jnp.einsum('bgrzc,bcgrpn->bzgrpn', decay_chunk, states)
    states, final = new_states[:, :-1], new_states[:, -1]
    y_off = jnp.einsum('bclgn,bcgrpn,bgrcl->bclgrp', cm, states, jnp.exp(a_cs))
    y = (y_diag + y_off).reshape(n, l, nh, p)
    return y, final.reshape(n, nh, p, ns)


def depthwise_conv_centred(u, w, b):
    kw = w.shape[0]
    pad = kw // 2
    out = lax.conv_general_dilated(u, w[:, None, :].astype(u.dtype), window_strides=(1,),
                                   padding=[(pad, kw - 1 - pad)],
                                   dimension_numbers=('NWC', 'WIO', 'NWC'),
                                   feature_group_count=u.shape[-1])
    return out + b


def ssd_project(h, w_in, conv_w, conv_b):
    zxbcdt = h @ w_in
    z = zxbcdt[..., :SSD_D_INNER]
    xbc = zxbcdt[..., SSD_D_INNER:SSD_D_INNER + SSD_CONV_DIM]
    dt_raw = zxbcdt[..., SSD_D_INNER + SSD_CONV_DIM:]
    xbc = jax.nn.silu(depthwise_conv_centred(xbc, conv_w, conv_b))
    return z, xbc, dt_raw


def ssd_bidir_scan(xbc, dt_raw, dt_bias, a_log, init_f, init_b):
    n, l, _ = xbc.shape
    gn = SSD_GROUPS * SSD_D_STATE
    f32 = jnp.float32
    x = xbc[..., :SSD_D_INNER].astype(f32).reshape(n, l, SSD_HEADS, SSD_HEAD_DIM)
    bm = xbc[..., SSD_D_INNER:SSD_D_INNER + gn].astype(f32).reshape(n, l, SSD_GROUPS, SSD_D_STATE)
    cm = xbc[..., SSD_D_INNER + gn:].astype(f32).reshape(n, l, SSD_GROUPS, SSD_D_STATE)
    dt = jax.nn.softplus(dt_raw.astype(f32).reshape(n, l, 2, SSD_HEADS) + dt_bias.astype(f32))
    da = dt * (-jnp.exp(a_log.astype(f32)))
    xdt = x[:, :, None] * dt[..., None]
    y_f, s_f = ssd_chunked(xdt[:, :, 0], da[:, :, 0], bm, cm, init_f)
    flip = lambda a: jnp.flip(a, axis=1)
    y_b, s_b = ssd_chunked(flip(xdt[:, :, 1]), flip(da[:, :, 1]), flip(bm), flip(cm), init_b)
    return y_f + flip(y_b), x, s_f, s_b


def ssd_output(y, x, z, d_skip, norm_g, w_out):
    n, l = z.shape[:2]
    y = (y + x * d_skip.astype(jnp.float32)[:, None]).reshape(n, l, SSD_D_INNER)
    y = rmsnorm(y * jax.nn.silu(z.astype(jnp.float32)), norm_g)
    return y.astype(z.dtype) @ w_out


def ssd_context_mixer(h, w_in, conv_w, conv_b, dt_bias, a_log, d_skip, norm_g, w_out):
    n = h.shape[0]
    z, xbc, dt_raw = ssd_project(h, w_in, conv_w, conv_b)
    zeros = jnp.zeros((n, SSD_HEADS, SSD_HEAD_DIM, SSD_D_STATE), jnp.float32)
    y, x, s_f, s_b = ssd_bidir_scan(xbc, dt_raw, dt_bias, a_log, zeros, zeros)
    out = ssd_output(y, x, z, d_skip, norm_g, w_out)
    return out, jnp.stack([s_f, s_b], axis=1).astype(h.dtype)


def ssd_latent_mixer(h, state, w_in, conv_w, conv_b, dt_bias, a_log, d_skip, norm_g, w_out):
    z, xbc, dt_raw = ssd_project(h, w_in, conv_w, conv_b)
    init_f = state[:, 0].astype(jnp.float32)
    init_b = state[:, 1].astype(jnp.float32)
    y, x, _, _ = ssd_bidir_scan(xbc, dt_raw, dt_bias, a_log, init_f, init_b)
    return ssd_output(y, x, z, d_skip, norm_g, w_out)


def setup_inputs(seed: int = 0) -> dict:
    key = jax.random.key(seed)
    ks = jax.random.split(key, 26)
    f32 = jnp.float32
    D = D_MODEL

    def nrm(k, shape, s):
        return jax.random.normal(k, shape, f32) * s

    dt0 = jnp.exp(jax.random.uniform(ks[18], (N_SSD_LAYERS, 2, SSD_HEADS), f32,
                                     math.log(1e-3), math.log(1e-1)))
    return {
        'x_prompt': nrm(ks[0], (BATCH, SEQ, D), 1.0),
        'x_sample': nrm(ks[1], (DEC_BATCH, DEC_SEQ, D), 1.0),
        'cache_k': nrm(ks[2], (DEC_BATCH, N_NA_LAYERS, NA_HEADS, PAST_LEN, NA_HEAD_DIM), 1.0),
        'cache_v': nrm(ks[3], (DEC_BATCH, N_NA_LAYERS, NA_HEADS, PAST_LEN, NA_HEAD_DIM), 1.0),
        'state_ssm': nrm(ks[4], (DEC_BATCH, N_SSD_LAYERS, 2, SSD_HEADS, SSD_HEAD_DIM, SSD_D_STATE), 0.1),
        'c': nrm(ks[5], (DEC_BATCH, D), 1.0),
        'c_ctx': nrm(ks[6], (D,), 1.0),
        'ada_w': nrm(ks[7], (DEPTH, D, 6 * D), 0.5 * D ** -0.5),
        'ada_b': nrm(ks[8], (DEPTH, 6 * D), 0.01),
        'norm_mix_g': 1.0 + nrm(ks[9], (DEPTH, D), 0.02),
        'norm_ffn_g': 1.0 + nrm(ks[10], (DEPTH, D), 0.02),
        'ffn_w_gate_up': nrm(ks[11], (DEPTH, D, 2 * D_FF), D ** -0.5),
        'ffn_w_down': nrm(ks[12], (DEPTH, D_FF, D), D_FF ** -0.5),
        'na_w_qkv': nrm(ks[13], (N_NA_LAYERS, D, 3 * D), D ** -0.5),
        'na_w_o': nrm(ks[14], (N_NA_LAYERS, D, D), D ** -0.5),
        'na_rpb': nrm(ks[15], (N_NA_LAYERS, NA_HEADS, 2 * NA_WIN_H - 1, 2 * NA_WIN_W - 1), 0.1),
        'ssd_w_in': nrm(ks[16], (N_SSD_LAYERS, D, SSD_IN_DIM), D ** -0.5),
        'ssd_conv_w': nrm(ks[17], (N_SSD_LAYERS, SSD_CONV, SSD_CONV_DIM), SSD_CONV ** -0.5),
        'ssd_conv_b': nrm(ks[19], (N_SSD_LAYERS, SSD_CONV_DIM), 0.01),
        'ssd_dt_bias': dt0 + jnp.log(-jnp.expm1(-dt0)),
        'ssd_a_log': jnp.log(jax.random.uniform(ks[20], (N_SSD_LAYERS, 2, SSD_HEADS), f32, 1.0, 16.0)),
        'ssd_d': 1.0 + nrm(ks[21], (N_SSD_LAYERS, SSD_HEADS), 0.02),
        'ssd_norm_g': 1.0 + nrm(ks[22], (N_SSD_LAYERS, SSD_D_INNER), 0.02),
        'ssd_w_out': nrm(ks[23], (N_SSD_LAYERS, SSD_D_INNER, D), SSD_D_INNER ** -0.5),
        'final_norm_g': 1.0 + nrm(ks[24], (D,), 0.02),
    }


def reference(x_prompt, x_sample, cache_k, cache_v, state_ssm, c, c_ctx,
              ada_w, ada_b, norm_mix_g, norm_ffn_g, ffn_w_gate_up, ffn_w_down,
              na_w_qkv, na_w_o, na_rpb,
              ssd_w_in, ssd_conv_w, ssd_conv_b, ssd_dt_bias, ssd_a_log, ssd_d, ssd_norm_g, ssd_w_out,
              final_norm_g):
    xp, xs = x_prompt, x_sample
    new_k, new_v, new_s = [], [], []
    for i in range(DEPTH):
        j = i // N_MIXERS
        mp = ada_params(c_ctx[None, :], ada_w[i], ada_b[i])
        ms = ada_params(c, ada_w[i], ada_b[i])
        hp = modulate(rmsnorm(xp, norm_mix_g[i]), mp[0], mp[1])
        hs = modulate(rmsnorm(xs, norm_mix_g[i]), ms[0], ms[1])
        if i % N_MIXERS == 0:
            op, k_p, v_p = na_context_mixer(hp, na_w_qkv[j], na_w_o[j])
            os_ = na_latent_mixer(hs, cache_k[:, j], cache_v[:, j], na_w_qkv[j], na_w_o[j], na_rpb[j])
            new_k.append(k_p)
            new_v.append(v_p)
        else:
            op, s_p = ssd_context_mixer(hp, ssd_w_in[j], ssd_conv_w[j], ssd_conv_b[j], ssd_dt_bias[j],
                                        ssd_a_log[j], ssd_d[j], ssd_norm_g[j], ssd_w_out[j])
            os_ = ssd_latent_mixer(hs, state_ssm[:, j], ssd_w_in[j], ssd_conv_w[j], ssd_conv_b[j],
                                   ssd_dt_bias[j], ssd_a_log[j], ssd_d[j], ssd_norm_g[j], ssd_w_out[j])
            new_s.append(s_p)
        xp = xp + mp[2] * op
        xs = xs + ms[2] * os_
        hp = modulate(rmsnorm(xp, norm_ffn_g[i]), mp[3], mp[4])
        hs = modulate(rmsnorm(xs, norm_ffn_g[i]), ms[3], ms[4])
        xp = xp + mp[5] * swiglu(hp, ffn_w_gate_up[i], ffn_w_down[i])
        xs = xs + ms[5] * swiglu(hs, ffn_w_gate_up[i], ffn_w_down[i])
    y_prompt = rmsnorm(xp, final_norm_g)
    y_sample = rmsnorm(xs, final_norm_g)
    new_state = (jnp.stack(new_k, axis=1), jnp.stack(new_v, axis=1), jnp.stack(new_s, axis=1))
    return (y_prompt, y_sample, *new_state)
```

```python
import os
import numpy as np
from contextlib import ExitStack
import concourse.bass as bass
import concourse.mybir as mybir
from concourse.bass_types import AP
from concourse.bass_utils import run_bass_kernel_spmd

F32 = mybir.dt.float32
BF16 = mybir.dt.bfloat16
AF = mybir.ActivationFunctionType
ALU = mybir.AluOpType

D = 1024
KD = 8
T = 2048
TG = 1024
DEPTH = 4
DFF = 2816
NH = 16
EPS = 1e-6
NCORES = 8
ARENA = 36608

ENGS = ["pe", "act", "dve", "pool", "sp"]
SAME_SYNC = {"pe": False, "act": True, "dve": True, "pool": False, "sp": False}


class Op:
    __slots__ = ("eng", "fn", "deps", "slot", "signal", "tok")

    def __init__(self, eng, fn, deps, slot):
        self.eng = eng
        self.fn = fn
        self.deps = deps
        self.slot = slot
        self.signal = False
        self.tok = None


class Sched:
    def __init__(self):
        self.ops = []
        self.lw = {}
        self.rd = {}
        self.final_slots = set()
        self.last_se = {}
        self.pending = {}

    def barrier(self):
        last = set(self.last_se.values())
        for e in ENGS:
            self.pending[e] = set(self.pending.get(e, ())) | last

    def add(self, eng, fn, reads=(), writes=(), slot=None, final=False, strict=True):
        i = len(self.ops)
        psr = [k for k in reads if isinstance(k, tuple) and k[0] == "ps"]
        if psr:
            reads = [k for k in reads if not (isinstance(k, tuple) and k[0] == "ps")]
            writes = list(writes) + psr
        deps = set(self.pending.pop(eng, ())) if strict else set()
        for k in reads:
            if k in self.lw:
                deps.add(self.lw[k])
        weak = set()
        for k in writes:
            if k in self.lw:
                weak.add(self.lw[k])
            r = self.rd.get(k)
            if r:
                weak.update(r.values())
        deps.update(weak)
        se = slot if slot is not None else eng
        self.last_se[se] = i
        for k in reads:
            self.rd.setdefault(k, {})[se] = i
        for k in writes:
            self.lw[k] = i
            self.rd[k] = {}
        deps.discard(i)
        self.ops.append(Op(eng, fn, deps, slot))
        if final:
            self.final_slots.add(slot)
        return i

    def barrier_keys(self, keys):
        pass

    def finalize(self):
        ops = self.ops
        for op in ops:
            for d in op.deps:
                dop = ops[d]
                if dop.slot is None and (dop.eng != op.eng or SAME_SYNC[op.eng]):
                    dop.signal = True
        cnt = {}
        for op in ops:
            if op.slot is not None:
                cnt[op.slot] = cnt.get(op.slot, 0) + 16
                op.tok = (op.slot, cnt[op.slot])
            elif op.signal:
                cnt[op.eng] = cnt.get(op.eng, 0) + 1
                op.tok = (op.eng, cnt[op.eng])
        self.cnt = cnt
        self.per_eng = {e: [] for e in ENGS}
        for op in ops:
            self.per_eng[op.eng].append(op)
        return sorted(cnt.keys(), key=str)

    def replay(self, engname, e, sems):
        ops = self.ops
        waited = {}
        for op in self.per_eng[engname]:
            need = {}
            for d in op.deps:
                dop = ops[d]
                if dop.tok is None:
                    continue
                if dop.slot is None and dop.eng == engname and not SAME_SYNC[engname]:
                    continue
                se, v = dop.tok
                if v > need.get(se, 0):
                    need[se] = v
            for se, v in need.items():
                if waited.get(se, 0) < v:
                    e.wait_ge(sems[se], v)
                    waited[se] = v
            ins = op.fn(e)
            if op.tok is not None:
                ins.then_inc(sems[op.tok[0]], 16 if op.slot is not None else 1)
        if engname == "sp":
            for se in sorted(self.final_slots, key=str):
                e.wait_ge(sems[se], self.cnt[se])


def _env_int(name, default):
    v = os.environ.get(name)
    return default if v is None else int(v)


class Builder:
    def __init__(self, nlayers=DEPTH, do_na=True, do_ssd=True, do_ffn=True):
        self.nl = nlayers
        self.do_na = do_na
        self.do_ssd = do_ssd
        self.do_ffn = do_ffn
        self.nc = bass.Bass("TRN2", target_bir_lowering=False)
        self.S = Sched()
        self.es = ExitStack()
        self.wslot = 0
        self.stg = {}
        self.uid = 0
        self.sqi = 0

    def dram_in(self, name, shape):
        return self.nc.dram_tensor(name, list(shape), F32, kind="ExternalInput").ap()

    def dram_out(self, name, shape):
        return self.nc.dram_tensor(name, list(shape), F32, kind="ExternalOutput").ap()

    def sb(self, name, shape, dt):
        return self.es.enter_context(self.nc.sbuf_tensor(name, list(shape), dt))

    def op(self, eng, fn, reads=(), writes=(), slot=None, final=False, strict=True):
        return self.S.add(eng, fn, reads, writes, slot, final, strict)

    def mm(self, out, lhsT, rhs, start, stop, reads, writes):
        self.op("pe", lambda e: e.matmul(out, lhsT, rhs, start=start, stop=stop), reads, writes)

    def act(self, out, in_, func, reads, writes, bias=None, scale=None):
        kw = {}
        if bias is not None:
            kw["bias"] = bias
        if scale is not None:
            kw["scale"] = scale
        self.op("act", lambda e: e.activation(out=out, in_=in_, func=func, **kw), reads, writes)

    def tt(self, out, in0, in1, op, reads, writes, eng="dve"):
        self.op(eng, lambda e: e.tensor_tensor(out=out, in0=in0, in1=in1, op=op), reads, writes)

    def ts(self, out, in0, s1, op0, reads, writes, s2=None, op1=None, eng="dve"):
        if op1 is None:
            self.op(eng, lambda e: e.tensor_scalar(out, in0, s1, scalar2=None, op0=op0), reads, writes)
        else:
            self.op(eng, lambda e: e.tensor_scalar(out, in0, s1, scalar2=s2, op0=op0, op1=op1), reads, writes)

    def stt(self, out, in0, scalar, in1, op0, op1, reads, writes):
        self.op("dve", lambda e: e.scalar_tensor_tensor(out=out, in0=in0, scalar=scalar, in1=in1, op0=op0, op1=op1),
                reads, writes)

    def copy(self, out, in_, reads, writes, eng="dve"):
        if eng == "act":
            self.op("act", lambda e: e.copy(out, in_), reads, writes)
        else:
            self.op(eng, lambda e: e.tensor_copy(out, in_), reads, writes)

    def dma(self, q, out, in_, reads, writes, slot, final=False, strict=True, **kw):
        self.op(q, lambda e: e.dma_start(out=out, in_=in_, **kw), reads, writes, slot=slot, final=final,
                strict=strict)

    def pbank(self, i):
        return self.ps[:, i, :]

    def psum_alloc(self, pool):
        lst = self.pools[pool]
        k = self.pool_idx.get(pool, 0)
        self.pool_idx[pool] = k + 1
        return lst[k % len(lst)]

    def load_w(self, src, kd, cb):
        s = self.wslot % self.NW
        self.wslot += 1
        view = self.wring[:, s, 0:kd * cb].rearrange("p (k c) -> p k c", k=kd)
        self.dma("pool", view, src, [], [("w", s)], slot=("wsem", s), strict=False)
        return view, ("w", s)

    def build(self):
        nc = self.nc
        nl = self.nl
        self.d_xT = self.dram_in("xT", [D, T])
        self.d_cond = self.dram_in("cond", [128, KD, 2])
        self.d_adaw = self.dram_in("ada_w", [DEPTH, 12, 128, KD, 512])
        self.d_adab = self.dram_in("ada_b", [DEPTH, 128, 48])
        self.d_gmix = self.dram_in("gmix", [DEPTH, 128, KD])
        self.d_gffn = self.dram_in("gffn", [DEPTH, 128, KD])
        self.d_gfin = self.dram_in("gfin", [128, KD])
        self.d_wgu = self.dram_in("w_gu", [DEPTH, 11, 128, KD, 512])
        self.d_wdn = self.dram_in("w_dn", [DEPTH, 4, 128, 22, 256])
        self.d_yT = self.dram_out("yT", [D, T])
        self.d_wqkv = self.dram_in("w_qkv", [2, 6, 128, KD, 512])
        self.d_wo = self.dram_in("w_o", [2, 2, 128, KD, 512])
        self.d_braw = self.dram_in("braw", [2, NH, 128, 1792])
        self.d_kctx = self.dram_in("kctxT", [2, D, 512])
        self.d_vctx = self.dram_in("vctx", [2, 512, D])
        self.d_kout = self.dram_out("kT_out", [2, D, 1024])
        self.d_winA = self.dram_in("w_inA", [2, 8, 128, KD, 512])
        self.d_winB = self.dram_in("w_inB", [2, 8, 128, KD, 256])
        self.d_wdt = self.dram_in("w_dt", [2, 128, KD, 64])
        self.d_convp = self.dram_in("convp", [2, 8, 128, 16])
        self.d_dtp = self.dram_in("dtp", [2, 64, 2])
        self.d_dvec = self.dram_in("dvec", [2, 128, 16])
        self.d_ngain = self.dram_in("ngain", [2, 128, 16])
        self.d_wout = self.dram_in("w_out", [2, 4, 128, 16, 256])
        self.d_state0 = self.dram_in("state0", [2, 2, 128, 2048])
        self.d_consts = self.dram_in("consts", [3, 128, 128])
        self.d_stout = self.dram_out("st_out", [2, 4, 2, 128, 2048])
        self.d_vout = self.dram_out("v_out", [2, 1024, D])

        self.xT = self.sb("xT_sb", [128, KD, T], F32)
        self.hT = self.sb("hT_sb", [128, KD, T], BF16)
        self.NW = 3
        self.wring = self.sb("wring", [128, self.NW, 4096], BF16)
        self.arena = self.sb("arena", [128, ARENA], BF16)
        self.mod = self.sb("mod", [128, DEPTH, 48, 2], F32)
        self.Amix = self.sb("Amix", [128, DEPTH, KD, 2], F32)
        self.Affn = self.sb("Affn", [128, DEPTH, KD, 2], F32)
        self.gmix = self.sb("gmix_sb", [128, DEPTH, KD], F32)
        self.gffn = self.sb("gffn_sb", [128, DEPTH, KD], F32)
        self.gfin = self.sb("gfin_sb", [128, KD], F32)
        self.adab = self.sb("adab_sb", [128, DEPTH, 48], F32)
        self.cond = self.sb("cond_sb", [128, KD, 2], F32)
        self.scond = self.sb("scond_sb", [128, KD, 2], BF16)
        self.ones_bf = self.sb("ones_bf", [128, 128], BF16)
        self.epsc = self.sb("epsc", [128, 1], F32)
        self.zero8 = self.sb("zero8", [128, KD], F32)
        self.sq = self.sb("sq", [128, 2, 512], BF16)
        self.rstd = self.sb("rstd", [128, 2, 512], F32)
        self.tmpn = self.sb("tmpn", [128, 2, 512], F32)
        self.identF = self.sb("identF", [128, 128], F32)
        self.identB = self.sb("identB", [128, 128], BF16)
        self.maskLU = self.sb("maskLU", [128, 2, 128], BF16)
        self.onesF = self.sb("onesF", [128, 128], F32)
        self.onec = self.sb("onec", [128, 1], F32)
        self.ps = self.es.enter_context(nc.psum_tensor("ps", [128, 8, 512], F32))
        self.pools = {}
        self.pool_idx = {}

        self.dma("sp", self.xT[:, :, :], self.d_xT.rearrange("(k p) t -> p k t", p=128), [], [("xall",)], slot="xin")
        self.dma("sp", self.cond[:, :, :], self.d_cond, [], [("cond",)], slot=("cst", 0))
        self.dma("sp", self.adab[:, :, :], self.d_adab.rearrange("l p c -> p l c"), [], [("adab",)], slot=("cst", 1))
        self.dma("sp", self.gmix[:, :, :], self.d_gmix.rearrange("l p c -> p l c"), [], [("gmix",)], slot=("cst", 2))
        self.dma("sp", self.gffn[:, :, :], self.d_gffn.rearrange("l p c -> p l c"), [], [("gffn",)], slot=("cst", 3))
        self.dma("sp", self.gfin[:, :], self.d_gfin, [], [("gfin",)], slot=("cst", 4))
        self.op("dve", lambda e: e.memset(self.ones_bf[:, :], 1.0), [], [("ones",)])
        self.op("dve", lambda e: e.memset(self.epsc[:, :], EPS), [], [("eps",)])
        self.op("dve", lambda e: e.memset(self.zero8[:, :], 0.0), [], [("zero8",)])
        self.op("dve", lambda e: e.memset(self.onesF[:, :], 1.0), [], [("onesF",)])
        self.op("dve", lambda e: e.memset(self.onec[:, :], 1.0), [], [("onec",)])
        self.dma("sp", self.identF[:, :], self.d_consts[0], [], [("identF",)], slot=("cst", 5))
        self.copy(self.identB[:, :], self.identF[:, :], [("identF",)], [("identB",)])
        self.dma("pool", self.maskLU[:, :, :], self.d_consts[1:3].rearrange("m p c -> p m c"), [], [("maskLU",)],
                 slot="mlu")
        self.act(self.scond[:, :, :], self.cond[:, :, :], AF.Silu, [("cond",)], [("scond",)])
        self.pools = {"ada": [0, 1]}
        for l in range(nl):
            b = self.psum_alloc("ada")
            pk = ("ps", b)
            pt = self.ps[:, b, 0:96]
            for cb in range(12):
                wt, wk = self.load_w(self.d_adaw[l, cb], KD, 512)
                for j in range(4):
                    oc = cb * 4 + j
                    for kd in range(KD):
                        self.mm(pt[:, oc * 2:oc * 2 + 2], wt[:, kd, j * 128:(j + 1) * 128], self.scond[:, kd, :],
                                kd == 0, kd == KD - 1, [wk, ("scond",)], [pk])
            self.tt(self.mod[:, l, :, :], pt.rearrange("p (c i) -> p c i", i=2),
                    self.adab[:, l, :].unsqueeze(2).to_broadcast([128, 48, 2]), ALU.add,
                    [pk, ("adab",)], [("mod", l)])
            for (A, g, gk, part) in ((self.Amix, self.gmix, ("gmix",), 1), (self.Affn, self.gffn, ("gffn",), 4)):
                self.stt(A[:, l, :, :], self.mod[:, l, part * 8:(part + 1) * 8, :], 1.0,
                         g[:, l, :].unsqueeze(2).to_broadcast([128, KD, 2]), ALU.add, ALU.mult,
                         [("mod", l), gk], [("A", l, part)])

        for l in range(nl):
            if l % 2 == 0:
                if self.do_na:
                    self.na_layer(l)
            else:
                if self.do_ssd:
                    self.ssd_layer(l)
            if self.do_ffn:
                self.ffn(l)
        self.final()

    def xkeys(self, tb):
        return [("x", kd, tb) for kd in range(KD)]

    def norm_mod(self, tb, A_of_kd, B_of_kd, pkeys, out_fn, out_keys_fn, pool="norm"):
        t0 = tb * 512
        xk = self.xkeys(tb) + [("xall",)]
        b = self.psum_alloc(pool)
        pk = ("ps", b)
        for kd in range(KD):
            sr = self.sqi % 2
            self.sqi += 1
            self.act(self.sq[:, sr, :], self.xT[:, kd, t0:t0 + 512], AF.Square, [("x", kd, tb), ("xall",)],
                     [("sq", sr)])
            self.mm(self.ps[:, b, :], self.ones_bf[:, :], self.sq[:, sr, :], kd == 0, kd == KD - 1,
                    [("sq", sr), ("ones",)], [pk])
        r = self.uid % 2
        self.uid += 1
        rs = self.rstd[:, r, :]
        rk = ("rstd", r)
        self.act(rs, self.ps[:, b, :], AF.Sqrt, [pk, ("eps",)], [rk], bias=self.epsc[:, 0:1], scale=1.0 / D)
        self.op("dve", lambda e: e.reciprocal(rs, rs), [rk], [rk])
        for kd in range(KD):
            q = kd % 2
            tk = ("tmpn", q)
            self.stt(self.tmpn[:, q, :], self.xT[:, kd, t0:t0 + 512], A_of_kd(kd), rs, ALU.mult, ALU.mult,
                     [("x", kd, tb), ("xall",), rk] + pkeys, [tk])
            self.act(out_fn(kd), self.tmpn[:, q, :], AF.Identity, [tk] + pkeys, out_keys_fn(kd), bias=B_of_kd(kd))

    def ffn(self, l):
        self.S.barrier()
        self.pools.update({"norm": [6], "g": [0, 1], "u": [2, 3], "dn": [4, 5]})
        act_v = self.arena[:, 0:12 * T].rearrange("p (j t) -> p j t", j=12)
        sg = self.arena[:, 12 * T:12 * T + 4 * 512].bitcast(F32).rearrange("p (r t) -> p r t", r=2)
        for tb in range(4):
            ci = 0 if tb < 2 else 1
            self.norm_mod(tb,
                          lambda kd: self.Affn[:, l, kd, ci:ci + 1],
                          lambda kd: self.mod[:, l, 24 + kd, ci:ci + 1],
                          [("A", l, 4), ("mod", l)],
                          lambda kd: self.hT[:, kd, tb * 512:(tb + 1) * 512],
                          lambda kd: [("h", kd, tb)])
        halves = [(0, 6), (6, 11)]
        sgi = 0
        for (b0, b1) in halves:
            nj = (b1 - b0) * 2
            for bi in range(b0, b1):
                wt, wk = self.load_w(self.d_wgu[l, bi], KD, 512)
                for jj in range(2):
                    jl = (bi - b0) * 2 + jj
                    for tb in range(4):
                        bg = self.psum_alloc("g")
                        bu = self.psum_alloc("u")
                        for kd in range(KD):
                            self.mm(self.ps[:, bg, :], wt[:, kd, jj * 128:(jj + 1) * 128],
                                    self.hT[:, kd, tb * 512:(tb + 1) * 512], kd == 0, kd == KD - 1,
                                    [wk, ("h", kd, tb)], [("ps", bg)])
                        for kd in range(KD):
                            self.mm(self.ps[:, bu, :], wt[:, kd, 256 + jj * 128:256 + (jj + 1) * 128],
                                    self.hT[:, kd, tb * 512:(tb + 1) * 512], kd == 0, kd == KD - 1,
                                    [wk, ("h", kd, tb)], [("ps", bu)])
                        r = sgi % 2
                        sgi += 1
                        self.act(sg[:, r, :], self.ps[:, bg, :], AF.Silu, [("ps", bg)], [("sg", r)])
                        self.tt(act_v[:, jl, tb * 512:(tb + 1) * 512], sg[:, r, :], self.ps[:, bu, :], ALU.mult,
                                [("sg", r), ("ps", bu)], [("act", jl, tb)])
            k0 = b0 * 2
            for cbk in range(4):
                wt, wk = self.load_w(self.d_wdn[l, cbk, :, k0:k0 + nj, :], nj, 256)
                for m in range(2):
                    oc = cbk * 2 + m
                    for tb in range(4):
                        ci = 0 if tb < 2 else 1
                        bd = self.psum_alloc("dn")
                        for j in range(nj):
                            self.mm(self.ps[:, bd, :], wt[:, j, m * 128:(m + 1) * 128],
                                    act_v[:, j, tb * 512:(tb + 1) * 512], j == 0, j == nj - 1,
                                    [wk, ("act", j, tb)], [("ps", bd)])
                        xs = self.xT[:, oc, tb * 512:(tb + 1) * 512]
                        self.stt(xs, self.ps[:, bd, :], self.mod[:, l, 40 + oc, ci:ci + 1], xs, ALU.mult, ALU.add,
                                 [("ps", bd), ("mod", l), ("xall",)], [("x", oc, tb)])

    def final(self):
        self.S.barrier()
        self.pools.update({"norm": [6]})
        stg = self.arena[:, 0:2 * 2 * 512].bitcast(F32).rearrange("p (r t) -> p r t", r=2) if False else None
        ybuf = self.arena[:, 0:4 * 2 * 512].bitcast(F32).rearrange("p (r t) -> p r t", r=4)
        cnt = [0]
        for tb in range(4):
            def out_fn(kd, tb=tb):
                return ybuf[:, (tb * KD + kd) % 4, :]

            def keys_fn(kd, tb=tb):
                return [("ybuf", (tb * KD + kd) % 4)]
            t0 = tb * 512
            xk = self.xkeys(tb) + [("xall",)]
            b = self.psum_alloc("norm")
            pk = ("ps", b)
            for kd in range(KD):
                sr = self.sqi % 2
                self.sqi += 1
                self.act(self.sq[:, sr, :], self.xT[:, kd, t0:t0 + 512], AF.Square, [("x", kd, tb), ("xall",)],
                         [("sq", sr)])
                self.mm(self.ps[:, b, :], self.ones_bf[:, :], self.sq[:, sr, :], kd == 0, kd == KD - 1,
                        [("sq", sr), ("ones",)], [pk])
            r = self.uid % 2
            self.uid += 1
            rs = self.rstd[:, r, :]
            rk = ("rstd", r)
            self.act(rs, self.ps[:, b, :], AF.Sqrt, [pk, ("eps",)], [rk], bias=self.epsc[:, 0:1], scale=1.0 / D)
            self.op("dve", lambda e, rs=rs: e.reciprocal(rs, rs), [rk], [rk])
            for kd in range(KD):
                yi = (tb * KD + kd) % 4
                self.stt(ybuf[:, yi, :], self.xT[:, kd, t0:t0 + 512], self.gfin[:, kd:kd + 1], rs, ALU.mult, ALU.mult,
                         [("x", kd, tb), ("xall",), rk, ("gfin",)], [("ybuf", yi)])
                self.dma("sp", self.d_yT[kd * 128:(kd + 1) * 128, t0:t0 + 512], ybuf[:, yi, :],
                         [("ybuf", yi)], [], slot=("yout", yi), final=True)


    def pv(self, ab, cols, tile, h, rhs, start, stop, reads):
        lo, hi = slice(0, 64), slice(64, 128)
        nr, dr = (lo, hi) if h % 2 == 0 else (hi, lo)
        self.mm(self.ps[nr, ab, cols], self.vt[:, tile, h * 64:(h + 1) * 64], rhs, start, stop, reads, [("ps", ab)])
        self.mm(self.ps[dr, ab, cols], self.ones_bf[:, 0:64], rhs, start, stop, reads + [("ones",)], [("ps", ab)])

    def attn_norm(self, h, ab, tbl):
        c = h // 2
        if h % 2 == 0:
            nr, dr = slice(0, 64), slice(64, 128)
        else:
            nr, dr = slice(64, 128), slice(0, 64)
        r = self.rdi % 2
        self.rdi += 1
        rd = self.rden[:, r, :]
        self.act(rd[dr, :], self.ps[dr, ab, :], AF.Ln, [("ps", ab)], [("rstd", r)])
        self.act(rd[dr, :], rd[dr, :], AF.Exp, [("rstd", r)], [("rstd", r)], scale=-1.0)
        self.tt(self.hT[nr, c, tbl * 512:(tbl + 1) * 512], self.ps[nr, ab, :], rd[dr, :], ALU.mult,
                [("ps", ab), ("rstd", r)], [("h", c, tbl)])

    def na_layer(self, l):
        j = l // 2
        S = self.S
        S.barrier()
        A = self.arena
        kT = A[:, 0:8192].rearrange("p (k t) -> p k t", k=8)
        vt = A[:, 8192:8192 + 12 * 1024].rearrange("p (t c) -> p t c", t=12)
        self.vt = vt
        kctx = A[:, 20480:20480 + 4096].rearrange("p (k t) -> p k t", k=8)
        ostg = A[:, 20480:20480 + 4096].bitcast(F32).rearrange("p (r t) -> p r t", r=4)
        bands = A[:, 24576:24576 + 3584].rearrange("p (r c) -> p r c", r=2)
        brb = A[:, 28160:28160 + 3584].rearrange("p (r c) -> p r c", r=2)
        PT = A[:, 31744:31744 + 1280].rearrange("p (r c) -> p r c", r=2)
        self.rden = self.rstd
        self.rdi = 0
        self.pools.update({"norm": [0], "proj": [0, 1], "st": [2, 3], "st2": [2, 4], "acc": [6, 7]})
        oi = [0]
        pti = [0]

        def out_store(psb, dst):
            r = oi[0] % 4
            oi[0] += 1
            self.copy(ostg[:, r, :], self.ps[:, psb, :], [("ps", psb)], [("ostg", r)], eng="act")
            self.dma("sp", dst, ostg[:, r, :], [("ostg", r)], [], slot=("kvout", r), final=True)

        for grp in (0, 1):
            ci = grp
            if grp == 1:
                S.barrier()
            for tbl in range(2):
                self.norm_mod(grp * 2 + tbl,
                              lambda kd: self.Amix[:, l, kd, ci:ci + 1],
                              lambda kd: self.mod[:, l, kd, ci:ci + 1],
                              [("A", l, 1), ("mod", l)],
                              lambda kd, tbl=tbl: self.hT[:, kd, tbl * 512:(tbl + 1) * 512],
                              lambda kd, tbl=tbl: [("h", kd, tbl)])
            if grp == 0:
                self.dma("pool", kctx[:, :, :], self.d_kctx[j].rearrange("(k p) t -> p k t", p=128), [], [("kctx",)],
                         slot="kctx")
                for kb in range(4):
                    self.dma("pool", vt[:, 8 + kb, :], self.d_vctx[j, kb * 128:(kb + 1) * 128, :], [],
                             [("v", 8 + kb)], slot=("vctx", kb))
            for bi in range(4):
                wt, wk = self.load_w(self.d_wqkv[j, bi], KD, 512)
                for jj in range(4):
                    oc = (bi % 2) * 4 + jj
                    for tbl in range(2):
                        pb = self.psum_alloc("proj")
                        for kd in range(KD):
                            self.mm(self.ps[:, pb, :], wt[:, kd, jj * 128:(jj + 1) * 128],
                                    self.hT[:, kd, tbl * 512:(tbl + 1) * 512], kd == 0, kd == KD - 1,
                                    [wk, ("h", kd, tbl)], [("ps", pb)])
                        if bi < 2:
                            self.copy(self.hT[:, oc, 1024 + tbl * 512:1024 + (tbl + 1) * 512], self.ps[:, pb, :],
                                      [("ps", pb)], [("h", oc, 2 + tbl)], eng="act")
                        else:
                            self.copy(kT[:, oc, tbl * 512:(tbl + 1) * 512], self.ps[:, pb, :],
                                      [("ps", pb)], [("kT", oc, tbl)], eng="dve")
                            if grp == 1:
                                out_store(pb, self.d_kout[j, oc * 128:(oc + 1) * 128, tbl * 512:(tbl + 1) * 512])
            for bi in range(4, 6):
                wt, wk = self.load_w(self.d_wqkv[j, bi], KD, 512)
                for tile in range(8):
                    pb = self.psum_alloc("proj")
                    for kd in range(KD):
                        self.mm(self.ps[:, pb, :], self.hT[:, kd, tile * 128:(tile + 1) * 128], wt[:, kd, :],
                                kd == 0, kd == KD - 1, [wk, ("h", kd, tile // 4)], [("ps", pb)])
                    c0 = (bi - 4) * 512
                    self.copy(vt[:, tile, c0:c0 + 512], self.ps[:, pb, :], [("ps", pb)], [("v", tile, bi)], eng="dve")
                    if grp == 1:
                        out_store(pb, self.d_vout[j, tile * 128:(tile + 1) * 128, (bi - 4) * 512:(bi - 3) * 512])

            def vkeys(tile, h):
                if tile >= 8:
                    return [("v", tile)]
                return [("v", tile, 4 + h // 8)]

            units3 = [2, 4, 0]
            ui = [0]
            steps = []

            def next_unit():
                u_ = units3[ui[0] % 3]
                ui[0] += 1
                return u_

            if grp == 1:
                for sp in range(2):
                    for h in range(NH):
                        c, base = h // 2, (h % 2) * 64
                        stt_ = {}
                        for sq_ in range(2):
                            sidx = sp * 2 + sq_

                            def S_(h=h, c=c, base=base, sidx=sidx, sq_=sq_, stt_=stt_):
                                if sq_ == 0:
                                    stt_["ab"] = self.psum_alloc("acc")
                                st = next_unit()
                                stt_[("st", sq_)] = st
                                for kb in range(2):
                                    k0 = sidx * 256 + kb * 128
                                    self.mm(self.ps[:, st, kb * 256:(kb + 1) * 256], kT[base:base + 64, c, k0:k0 + 128],
                                            self.hT[base:base + 64, c, 1024 + sidx * 256:1024 + (sidx + 1) * 256],
                                            True, True, [("kT", c, sidx // 2), ("h", c, 2 + sidx // 2)], [("ps", st)])

                            def P_(sq_=sq_, stt_=stt_):
                                st = stt_[("st", sq_)]
                                r = pti[0] % 2
                                pti[0] += 1
                                stt_[("r", sq_)] = r
                                self.act(PT[:, r, 0:512], self.ps[:, st, :], AF.Exp, [("ps", st)], [("PT", r)],
                                         scale=0.125)

                            def V_(h=h, sidx=sidx, sq_=sq_, sp=sp, stt_=stt_):
                                ab, r = stt_["ab"], stt_[("r", sq_)]
                                for kb in range(2):
                                    tile = sidx * 2 + kb
                                    self.pv(ab, slice(sq_ * 256, (sq_ + 1) * 256), tile, h,
                                            PT[:, r, kb * 256:(kb + 1) * 256], kb == 0, kb == 1,
                                            [("PT", r)] + vkeys(tile, h))
                                if sq_ == 1:
                                    self.attn_norm(h, ab, sp)
                            steps.append((S_, P_, V_))
            else:
                qtiles = {0: [0, 1, 2, 3], 1: [0, 1, 2, 3], 2: [0, 1, 2, 3, 4], 3: [1, 2, 3, 4, 5],
                          4: [2, 3, 4, 5, 6], 5: [3, 4, 5, 6, 7], 6: [4, 5, 6, 7], 7: [4, 5, 6, 7]}
                for h in range(NH):
                    c, base = h // 2, (h % 2) * 64
                    br = h % 2
                    for qb in range(2):
                        stt_ = {}
                        for kb in range(4):
                            def S_(h=h, c=c, base=base, br=br, qb=qb, kb=kb, stt_=stt_):
                                if qb == 0 and kb == 0:
                                    self.dma("pool", brb[:, br, :], self.d_braw[j, h], [], [("brb", br)],
                                             slot=("brb", br))
                                    self.act(bands[:, br, :], brb[:, br, :], AF.Exp, [("brb", br)], [("bands", br)])
                                if kb == 0:
                                    stt_["ab"] = self.psum_alloc("acc")
                                st = next_unit()
                                stt_[("st", kb)] = st
                                self.mm(self.ps[:, st, :], kctx[base:base + 64, c, kb * 128:(kb + 1) * 128],
                                        self.hT[base:base + 64, c, 1024 + qb * 512:1024 + (qb + 1) * 512], True, True,
                                        [("kctx",), ("h", c, 2 + qb)], [("ps", st)])

                            def P_(kb=kb, stt_=stt_):
                                st = stt_[("st", kb)]
                                r = pti[0] % 2
                                pti[0] += 1
                                stt_[("r", kb)] = r
                                self.act(PT[:, r, 0:512], self.ps[:, st, :], AF.Exp, [("ps", st)], [("PT", r)],
                                         scale=0.125)

                            def V_(h=h, kb=kb, stt_=stt_):
                                ab, r = stt_["ab"], stt_[("r", kb)]
                                self.pv(ab, slice(0, 512), 8 + kb, h, PT[:, r, 0:512], kb == 0, False,
                                        [("PT", r)] + vkeys(8 + kb, h))
                            steps.append((S_, P_, V_))
                        for qi in range(qb * 4, qb * 4 + 4):
                            kjs = qtiles[qi][::-1]
                            nk = len(kjs)

                            def S_(c=c, base=base, qb=qb, qi=qi, kjs=kjs, stt_=stt_):
                                ub = next_unit()
                                stt_[("ub", qi)] = ub
                                for jx, kj in enumerate(kjs):
                                    bb, cc = (ub, jx * 128) if jx < 4 else (ub + 1, 0)
                                    self.mm(self.ps[:, bb, cc:cc + 128], kT[base:base + 64, c, kj * 128:(kj + 1) * 128],
                                            self.hT[base:base + 64, c, 1024 + qi * 128:1024 + (qi + 1) * 128],
                                            True, True, [("kT", c, kj // 4), ("h", c, 2 + qb)],
                                            [("ps", ub), ("ps", ub + 1)])

                            def P_(br=br, qi=qi, kjs=kjs, nk=nk, stt_=stt_):
                                ub = stt_[("ub", qi)]
                                r = pti[0] % 2
                                pti[0] += 1
                                stt_[("r", qi)] = r
                                src = self.ps[:, ub:ub + 2, :].rearrange("p b c -> p (b c)")[:, 0:nk * 128]
                                self.act(PT[:, r, 0:nk * 128], src, AF.Exp, [("ps", ub), ("ps", ub + 1)], [("PT", r)],
                                         scale=0.125)
                                var = 1 if qi in (2, 3, 4, 5) else 0
                                m0 = 6 - 2 * (kjs[0] - qi)
                                bsl = bands[:, br, var * 896 + m0 * 64:var * 896 + m0 * 64 + nk * 128]
                                self.tt(PT[:, r, 0:nk * 128], PT[:, r, 0:nk * 128], bsl, ALU.mult,
                                        [("PT", r), ("bands", br)], [("PT", r)])

                            def V_(h=h, qb=qb, qi=qi, kjs=kjs, nk=nk, stt_=stt_):
                                ab, r = stt_["ab"], stt_[("r", qi)]
                                for jx, kj in enumerate(kjs):
                                    last = (qi == qb * 4 + 3) and (jx == nk - 1)
                                    self.pv(ab, slice((qi % 4) * 128, (qi % 4 + 1) * 128), kj, h,
                                            PT[:, r, jx * 128:(jx + 1) * 128], False, last,
                                            [("PT", r)] + vkeys(kj, h))
                                if qi == qb * 4 + 3:
                                    self.attn_norm(h, ab, qb)
                            steps.append((S_, P_, V_))
            steps[0][0]()
            for i_, (S_, P_, V_) in enumerate(steps):
                if i_ + 1 < len(steps):
                    steps[i_ + 1][0]()
                P_()
                V_()
            for bi in range(2):
                wt, wk = self.load_w(self.d_wo[j, bi], KD, 512)
                for jj in range(4):
                    oc = bi * 4 + jj
                    for tbl in range(2):
                        pb = self.psum_alloc("proj")
                        for kd in range(KD):
                            self.mm(self.ps[:, pb, :], wt[:, kd, jj * 128:(jj + 1) * 128],
                                    self.hT[:, kd, tbl * 512:(tbl + 1) * 512], kd == 0, kd == KD - 1,
                                    [wk, ("h", kd, tbl)], [("ps", pb)])
                        tb = grp * 2 + tbl
                        xs = self.xT[:, oc, tb * 512:(tb + 1) * 512]
                        self.stt(xs, self.ps[:, pb, :], self.mod[:, l, 16 + oc, ci:ci + 1], xs, ALU.mult, ALU.add,
                                 [("ps", pb), ("mod", l), ("xall",)], [("x", oc, tb)])
        S.barrier()


    def yT_view(self, yc):
        if yc < 8:
            return self.hT[:, yc, 1024:2048], ("h", yc)
        return self.arena[:, 0:8192].rearrange("p (k t) -> p k t", k=8)[:, yc - 8, :], ("yhi", yc - 8)

    def ykeys(self, yc, tbl):
        if yc < 8:
            return [("h", yc, 2 + tbl)]
        return [("yhi", yc - 8, tbl)]

    def ssd_layer(self, l):
        j = l // 2
        S = self.S
        S.barrier()
        A = self.arena

        def f32v(off, n):
            return A[:, off:off + 2 * n].bitcast(F32)

        Rt = f32v(8192, 1024)
        Rhm = A[:, 8192:10240].rearrange("p (r t) -> p r t", r=2)
        tok3 = f32v(10240, 1536).rearrange("p (t c) -> p t c", t=8)
        decbc = f32v(13312, 512).rearrange("p (k c) -> p k c", k=64)
        sm = f32v(14336, 128)
        dvec, ngain, dtp, negA = sm[:, 0:16], sm[:, 16:32], sm[:, 32:34], sm[:, 34:35]
        convp = sm[:, 40:72].rearrange("p (r c) -> p r c", r=2)
        G0 = 14592
        xcT = A[:, G0:G0 + 2048].rearrange("p (k t) -> p k t", k=2)
        BT = A[:, G0 + 2048:G0 + 3072]
        CT = A[:, G0 + 3072:G0 + 4096]
        xbtok = A[:, G0 + 4096:G0 + 7168].rearrange("p (t c) -> p t c", t=8)
        sin = A[:, G0 + 7168:G0 + 11264].rearrange("p (t d c) -> p t d c", t=8, d=2)
        Sst = f32v(G0 + 11264, 512).rearrange("p (d c) -> p d c", d=2)
        xs = A[:, G0 + 12288:G0 + 12800].rearrange("p (r c) -> p r c", r=2)
        cbLU = A[:, G0 + 12800:G0 + 13824].rearrange("p (r c) -> p r c", r=2)
        Q0 = G0 + 13824
        Ebuf = A[:, Q0:Q0 + 1024].rearrange("p (r c) -> p r c", r=2)
        Wbuf = A[:, Q0 + 1024:Q0 + 3072].rearrange("p (r c) -> p r c", r=4)
        eabc = A[:, Q0 + 3072:Q0 + 4096].rearrange("p (r c) -> p r c", r=2)
        coff = A[:, Q0 + 4096:Q0 + 6144].rearrange("p (r c) -> p r c", r=4)
        Xm = f32v(Q0 + 6144, 1024).rearrange("p (r c) -> p r c", r=2)
        assert Q0 + 6144 + 2048 <= ARENA
        TR = [f32v(G0 + i * 2048, 1024) for i in range(6)]
        self.pools.update({"norm": [5], "u2": [0, 2], "tr": [4], "rbc": [0, 1, 2, 3, 4], "cb": [5], "y": [6, 7], "sch": [6, 7]})
        ctmp = [(self.tmpn[:, :, :].rearrange("p r t -> p (r t)"), [("tmpn", 0), ("tmpn", 1)]),
                (self.rstd[:, :, :].rearrange("p r t -> p (r t)"), [("rstd", 0), ("rstd", 1)])]
        ysq = self.sq[:, :, :].rearrange("p a t -> p (a t)").bitcast(F32)

        self.dma("sp", dvec, self.d_dvec[j], [], [("ssdv", 0)], slot=("cst", 6))
        self.dma("sp", ngain, self.d_ngain[j], [], [("ssdv", 1)], slot=("cst", 7))
        self.dma("sp", dtp[0:64, :], self.d_dtp[j], [], [("ssdv", 2)], slot=("cst", 8))
        self.act(negA[0:64, :], dtp[0:64, 1:2], AF.Exp, [("ssdv", 2)], [("negA",)])
        self.ts(negA[0:64, :], negA[0:64, :], -1.0, ALU.mult, [("negA",)], [("negA",)])
        cvi = [0]
        cti = [0]
        cnt = {"xs": 0, "E": 0, "W": 0, "ea": 0, "co": 0, "ysq": 0, "X": 0}

        for grp in (0, 1):
            ci = grp
            S.barrier()
            nseq, L = (1, 1024) if grp == 0 else (4, 256)
            for tbl in range(2):
                self.norm_mod(grp * 2 + tbl,
                              lambda kd: self.Amix[:, l, kd, ci:ci + 1],
                              lambda kd: self.mod[:, l, kd, ci:ci + 1],
                              [("A", l, 1), ("mod", l)],
                              lambda kd, tbl=tbl: self.hT[:, kd, tbl * 512:(tbl + 1) * 512],
                              lambda kd, tbl=tbl: [("h", kd, tbl)])
            S.barrier()
            E1, DT, LNDT, DA, ACS, QF = TR
            CM = Rt
            wt, wk = self.load_w(self.d_wdt[j], KD, 64)
            for tbl in range(2):
                pb = self.psum_alloc("rbc")
                for kd in range(KD):
                    self.mm(self.ps[0:64, pb, :], wt[:, kd, 0:64], self.hT[:, kd, tbl * 512:(tbl + 1) * 512],
                            kd == 0, kd == KD - 1, [wk, ("h", kd, tbl)], [("ps", pb)])
                sl = slice(tbl * 512, (tbl + 1) * 512)
                self.act(E1[0:64, sl], self.ps[0:64, pb, :], AF.Exp, [("ps", pb), ("ssdv", 2)], [("E1",)],
                         bias=dtp[0:64, 0:1])
            self.act(DT[0:64, :], E1[0:64, :], AF.Ln, [("E1",)], [("DT",)], bias=self.onec[0:64, 0:1])
            self.act(LNDT[0:64, :], DT[0:64, :], AF.Ln, [("DT",)], [("LNDT",)])
            self.ts(DA[0:64, :], DT[0:64, :], negA[0:64, 0:1], ALU.mult, [("DT",), ("negA",)], [("DA",)])
            self.op("dve", lambda e: e.memset(CM[0:64, :], 1.0), [], [("R",)])
            self.op("dve", lambda e: e.memset(CM[0:64, :].rearrange("p (c q) -> p c q", q=128)[:, :, 0:1], 0.0),
                    [("R",)], [("R",)])
            self.op("dve", lambda e: e.tensor_tensor_scan(out=ACS[0:64, :], data0=CM[0:64, :], data1=DA[0:64, :],
                                                          initial=0.0, op0=ALU.mult, op1=ALU.add),
                    [("R",), ("DA",)], [("ACS",)])
            a3 = ACS[:, :].rearrange("p (c q) -> p c q", q=128)
            r3 = Rt[:, :].rearrange("p (c q) -> p c q", q=128)
            self.copy(Rt[0:32, :], ACS[0:32, :], [("ACS",)], [("R",)])
            self.tt(E1[32:64, :], DA[32:64, :], ACS[32:64, :], ALU.subtract, [("DA",), ("ACS",), ("DT",)], [("E1",)])
            self.tt(r3[32:64, :, :], E1[32:64, :].rearrange("p (c q) -> p c q", q=128),
                    a3[32:64, :, 127:128].to_broadcast([32, 8, 128]), ALU.add, [("E1",), ("ACS",)], [("R",)])
            self.tt(QF[0:64, :], LNDT[0:64, :], Rt[0:64, :], ALU.subtract, [("LNDT",), ("R",)], [("QF",)])
            q3 = QF[:, :].rearrange("p (c q) -> p c q", q=128)
            e3 = E1[:, :].rearrange("p (c q) -> p c q", q=128)
            self.tt(e3[0:32, :, :], q3[0:32, :, :], r3[0:32, :, 127:128].to_broadcast([32, 8, 128]), ALU.add,
                    [("QF",), ("R",), ("E1",)], [("E1",)])
            self.tt(e3[32:64, :, :], q3[32:64, :, :], r3[32:64, :, 0:1].to_broadcast([32, 8, 128]), ALU.add,
                    [("QF",), ("R",), ("E1",)], [("E1",)])
            self.act(QF[64:128, :], E1[0:64, :], AF.Exp, [("E1",)], [("QF",)])
            Tm = ACS[0:64, 0:8]
            self.copy(ACS[0:32, 0:8].unsqueeze(2), r3[0:32, :, 127:128], [("R",), ("ACS",), ("E1",)], [("ACS",)])
            self.copy(ACS[32:64, 0:8].unsqueeze(2), r3[32:64, :, 0:1], [("R",), ("ACS",)], [("ACS",)])
            Texp = DA[0:64, 0:512].rearrange("p (k c) -> p k c", k=64)
            self.tt(Texp, self.identF[0:64, 0:64].unsqueeze(2).to_broadcast([64, 64, 8]),
                    Tm.unsqueeze(1).to_broadcast([64, 64, 8]), ALU.mult, [("ACS",), ("identF",), ("DA",)], [("DA",)])
            pb = self.psum_alloc("rbc")
            self.mm(self.ps[:, pb, :], self.onesF[0:64, :], DA[0:64, 0:512], True, True, [("DA",), ("onesF",)],
                    [("ps", pb)])
            self.act(decbc.rearrange("p k c -> p (k c)"), self.ps[:, pb, :], AF.Exp, [("ps", pb)], [("decbc",)])
            for tile in range(8):
                tb_ = self.psum_alloc("tr")
                self.op("pe", lambda e, tb_=tb_, tile=tile: e.transpose(self.ps[:, tb_, 0:128],
                                                                      QF[:, tile * 128:(tile + 1) * 128],
                                                                      self.identF[:, :]),
                        [("QF",), ("identF",)], [("ps", tb_)])
                self.op("pe", lambda e, tb_=tb_, tile=tile: e.transpose(self.ps[:, tb_, 128:192],
                                                                      Rt[0:64, tile * 128:(tile + 1) * 128],
                                                                      self.identF[0:64, 0:64]),
                        [("R",), ("identF",)], [("ps", tb_)])
                self.copy(tok3[:, tile, :], self.ps[:, tb_, 0:192], [("ps", tb_)], [("tok3",)])
            hiT = E1[0:64, 0:512].bitcast(BF16)
            self.copy(hiT, Rt[0:64, :], [("R",)], [("E1",)])
            self.tt(DT[0:64, :], Rt[0:64, :], hiT, ALU.subtract, [("R",), ("E1",)], [("DT",)])
            self.copy(Rhm[0:64, 0, :], hiT, [("E1",)], [("R",)])
            self.copy(Rhm[0:64, 1, :], DT[0:64, :], [("DT",)], [("R",)])
            S.barrier()

            for g in range(8):
                cr = cvi[0] % 2
                cvi[0] += 1
                self.dma("sp", convp[:, cr, :], self.d_convp[j, g], [], [("convp", cr)], slot=("convp", cr))
                wtA, wkA = self.load_w(self.d_winA[j, g], KD, 512)
                wtB, wkB = self.load_w(self.d_winB[j, g], KD, 256)
                plan = [(wtA, wkA, 0, "z", 0), (wtA, wkA, 1, "z", 1), (wtA, wkA, 2, "x", 0), (wtA, wkA, 3, "x", 1),
                        (wtB, wkB, 0, "B", 0), (wtB, wkB, 1, "C", 0)]
                for (wt, wk, jc, kind, idx) in plan:
                    ub = self.psum_alloc("u2")
                    for tbl in range(2):
                        for kd in range(KD):
                            self.mm(self.ps[:, ub + tbl, :], wt[:, kd, jc * 128:(jc + 1) * 128],
                                    self.hT[:, kd, tbl * 512:(tbl + 1) * 512], kd == 0, kd == KD - 1,
                                    [wk, ("h", kd, tbl)], [("ps", ub), ("ps", ub + 1)])
                    u = self.ps[:, ub:ub + 2, :].rearrange("p b c -> p (b c)")
                    uk = [("ps", ub), ("ps", ub + 1)]
                    if kind == "z":
                        yv, _ = self.yT_view(2 * g + idx)
                        self.act(yv, u, AF.Silu, uk, self.ykeys(2 * g + idx, 0) + self.ykeys(2 * g + idx, 1))
                        continue
                    cidx = {"x": idx, "B": 2, "C": 3}[kind]
                    cp = convp[:, cr, cidx * 4:cidx * 4 + 4]
                    tbuf, tkeys = ctmp[cti[0] % 2]
                    cti[0] += 1
                    self.act(tbuf, u, AF.Identity, uk + [("convp", cr)], tkeys, bias=cp[:, 3:4], scale=cp[:, 1:2])
                    u3 = u.rearrange("p (s q) -> p s q", s=nseq)
                    t3 = tbuf.rearrange("p (s q) -> p s q", s=nseq)
                    self.stt(t3[:, :, 1:L], u3[:, :, 0:L - 1], cp[:, 0:1], t3[:, :, 1:L], ALU.mult, ALU.add,
                             uk + tkeys + [("convp", cr)], tkeys)
                    self.stt(t3[:, :, 0:L - 1], u3[:, :, 1:L], cp[:, 2:3], t3[:, :, 0:L - 1], ALU.mult, ALU.add,
                             uk + tkeys + [("convp", cr)], tkeys)
                    dst, dk = {"x": (xcT[:, idx, :], ("xcT", idx)), "B": (BT, ("BT",)), "C": (CT, ("CT",))}[kind]
                    self.act(dst, tbuf, AF.Silu, tkeys, [dk])
                for tile in range(8):
                    tb_ = self.psum_alloc("tr")
                    psb = self.ps[:, tb_, :].bitcast(BF16)
                    for q_, (src, sk) in enumerate(((xcT[:, 0, :], ("xcT", 0)), (xcT[:, 1, :], ("xcT", 1)),
                                                    (BT, ("BT",)))):
                        self.op("pe", lambda e, psb=psb, q_=q_, src=src, tile=tile: e.transpose(
                            psb[:, q_ * 128:(q_ + 1) * 128], src[:, tile * 128:(tile + 1) * 128], self.identB[:, :]),
                            [sk, ("identB",)], [("ps", tb_)])
                    self.copy(xbtok[:, tile, :], psb[:, 0:384], [("ps", tb_)], [("xbtok", tile)], eng="act")
                for sq_ in range(nseq):
                    nct = L // 128
                    ft = sq_ * nct
                    for d in range(2):
                        Sd = Sst[:, d, :]
                        if grp == 0:
                            self.dma("sp", Sd, self.d_state0[j, d, :, g * 256:(g + 1) * 256], [], [("S", d)],
                                     slot=("st0", d))
                        else:
                            self.op("dve", lambda e, Sd=Sd: e.memset(Sd, 0.0), [], [("S", d)])
                    for ci_ in range(nct):
                        for d in range(2):
                            Sd = Sst[:, d, :]
                            c = ci_ if d == 0 else nct - 1 - ci_
                            tile = ft + c
                            self.copy(sin[:, tile, d, :], Sd, [("S", d)], [("sin", tile, d)], eng="act")
                            r = cnt["xs"] % 2
                            cnt["xs"] += 1
                            k0 = 64 + d * 32 + 4 * g
                            self.tt(xs[:, r, :].rearrange("p (a b) -> p a b", a=4),
                                    xbtok[:, tile, 0:256].rearrange("p (a b) -> p a b", a=4),
                                    tok3[:, tile, k0:k0 + 4].unsqueeze(2).to_broadcast([128, 4, 64]), ALU.mult,
                                    [("xbtok", tile), ("tok3",)], [("xs", r)])
                            sb_ = self.psum_alloc("sch")
                            self.mm(self.ps[:, sb_, 0:256], xbtok[:, tile, 256:384], xs[:, r, :], True, True,
                                    [("xbtok", tile), ("xs", r)], [("ps", sb_)])
                            kd0 = d * 32 + 4 * g
                            self.tt(Sd.rearrange("p (a b) -> p a b", a=4), Sd.rearrange("p (a b) -> p a b", a=4),
                                    decbc[:, kd0:kd0 + 4, tile:tile + 1].to_broadcast([128, 4, 64]), ALU.mult,
                                    [("S", d), ("decbc",)], [("S", d)])
                            self.tt(Sd, Sd, self.ps[:, sb_, 0:256], ALU.add, [("S", d), ("ps", sb_)], [("S", d)])
                    if grp == 1:
                        for d in range(2):
                            self.dma("sp", self.d_stout[j, sq_, d, :, g * 256:(g + 1) * 256], Sst[:, d, :],
                                     [("S", d)], [], slot=("stout", d), final=True)
                units = [(tbl, hp, hh2) for tbl in range(2) for hp in range(2) for hh2 in range(2)]
                st1 = {}

                def stage_cb(tbl):
                    tsl = slice(tbl * 512, (tbl + 1) * 512)
                    cb = self.psum_alloc("cb")
                    for q_ in range(4):
                        tile = tbl * 4 + q_
                        self.mm(self.ps[:, cb, q_ * 128:(q_ + 1) * 128], BT[:, tile * 128:(tile + 1) * 128],
                                CT[:, tile * 128:(tile + 1) * 128], True, True, [("BT",), ("CT",)], [("ps", cb)])
                    for mi in range(2):
                        self.tt(cbLU[:, mi, :].rearrange("p (a b) -> p a b", a=4),
                                self.ps[:, cb, :].rearrange("p (a b) -> p a b", a=4),
                                self.maskLU[:, mi, :].unsqueeze(1).to_broadcast([128, 4, 128]), ALU.mult,
                                [("ps", cb), ("maskLU",)], [("cbLU", mi)])

                dirs = [(u, d) for u in units for d in range(2)]
                info = {}

                rbank = {}

                def stageP(i):
                    (tbl, hp, hh2), d = dirs[i]
                    tsl = slice(tbl * 512, (tbl + 1) * 512)
                    k = d * 32 + 4 * g + hp * 2 + hh2
                    rb = self.psum_alloc("rbc")
                    rbank[i] = rb
                    for r_ in range(2):
                        self.mm(self.ps[:, rb, :], self.identB[0:64, k:k + 1].to_broadcast([64, 128]),
                                Rhm[0:64, r_, tsl], r_ == 0, r_ == 1, [("R",), ("identB",)], [("ps", rb)])

                def stageA(i):
                    (tbl, hp, hh2), d = dirs[i]
                    tsl = slice(tbl * 512, (tbl + 1) * 512)
                    h = 4 * g + hp * 2 + hh2
                    k = d * 32 + h
                    rb = rbank[i]
                    er = cnt["ea"] % 2
                    cnt["ea"] += 1
                    self.act(eabc[:, er, :], self.ps[:, rb, :], AF.Exp, [("ps", rb)], [("eabc", er)])
                    xi = cnt["X"] % 2
                    cnt["X"] += 1
                    self.tt(Xm[:, xi, :].rearrange("p (a b) -> p a b", a=4),
                            self.ps[:, rb, :].rearrange("p (a b) -> p a b", a=4),
                            tok3[:, tbl * 4:tbl * 4 + 4, 128 + k:128 + k + 1].to_broadcast([128, 4, 128]),
                            ALU.min, [("ps", rb), ("tok3",)], [("X", xi)])
                    info[i] = (er, xi)

                def stageB(i):
                    (tbl, hp, hh2), d = dirs[i]
                    u = dirs[i][0]
                    tsl = slice(tbl * 512, (tbl + 1) * 512)
                    h = 4 * g + hp * 2 + hh2
                    k = d * 32 + h
                    er, xi = info[i]
                    ei = cnt["E"] % 2
                    cnt["E"] += 1
                    for q_ in range(4):
                        tile = tbl * 4 + q_
                        self.act(Ebuf[:, ei, q_ * 128:(q_ + 1) * 128], Xm[:, xi, q_ * 128:(q_ + 1) * 128],
                                 AF.Exp, [("X", xi), ("tok3",)], [("E", ei, q_)], bias=tok3[:, tile, k:k + 1])
                    co = cnt["co"] % 4
                    cnt["co"] += 1
                    self.tt(coff[:, co, :], CT[:, tsl], eabc[:, er, :], ALU.mult,
                            [("CT",), ("eabc", er)], [("coff", co)])
                    wi = cnt["W"] % 4
                    cnt["W"] += 1
                    self.tt(Wbuf[:, wi, :], Ebuf[:, ei, :], cbLU[:, d, :], ALU.mult,
                            [("E", ei, q_) for q_ in range(4)] + [("cbLU", d)], [("W", wi)])
                    wr, cr_ = st1.setdefault(u, ({}, {}))
                    wr[d] = wi
                    cr_[d] = co

                ybank = {}

                def stage2(u):
                    tbl, hp, hh2 = u
                    tsl = slice(tbl * 512, (tbl + 1) * 512)
                    hh = hp * 2 + hh2
                    po = hh2 * 64
                    wr, cr_ = st1[u]
                    if hh2 == 0:
                        ybank[(tbl, hp)] = self.psum_alloc("y")
                    yb = ybank[(tbl, hp)]
                    for q_ in range(4):
                        tile = tbl * 4 + q_
                        qs = slice(q_ * 128, (q_ + 1) * 128)
                        out = self.ps[po:po + 64, yb, qs]
                        xl = xbtok[:, tile, hh * 64:(hh + 1) * 64]
                        self.mm(out, xl, Wbuf[:, wr[0], qs], True, False,
                                [("xbtok", tile), ("W", wr[0])], [("ps", yb)])
                        self.mm(out, xl, Wbuf[:, wr[1], qs], False, False,
                                [("xbtok", tile), ("W", wr[1])], [("ps", yb)])
                        for d in range(2):
                            self.mm(out, sin[:, tile, d, hh * 64:(hh + 1) * 64], coff[:, cr_[d], qs],
                                    False, d == 1, [("sin", tile, d), ("coff", cr_[d])], [("ps", yb)])
                    if hh2 == 1:
                        yc = 2 * g + hp
                        self.stt(ysq, xcT[:, hp, tsl], dvec[:, yc:yc + 1], self.ps[:, yb, :],
                                 ALU.mult, ALU.add, [("xcT", hp), ("ssdv", 0), ("ps", yb)], [("sq", 0), ("sq", 1)])
                        yv, _ = self.yT_view(yc)
                        self.tt(yv[:, tsl], ysq, yv[:, tsl], ALU.mult,
                                [("sq", 0), ("sq", 1)] + self.ykeys(yc, tbl), self.ykeys(yc, tbl))

                stage_cb(0)
                AHEAD = 4
                for i in range(min(AHEAD, len(dirs))):
                    stageP(i)
                stageA(0)
                for i in range(len(dirs)):
                    if i + AHEAD < len(dirs):
                        stageP(i + AHEAD)
                    if i + 1 < len(dirs):
                        stageA(i + 1)
                    if i + 1 < len(dirs) and dirs[i + 1][0][0] != dirs[i][0][0] and dirs[i + 1][1] == 0:
                        pass
                    if dirs[i][1] == 0 and dirs[i][0][1] == 0 and dirs[i][0][2] == 0 and dirs[i][0][0] == 1:
                        stage_cb(1)
                    stageB(i)
                    if dirs[i][1] == 1:
                        stage2(dirs[i][0])
            S.barrier()
            for tbl in range(2):
                tsl = slice(tbl * 512, (tbl + 1) * 512)
                nb = self.psum_alloc("norm")
                for yc in range(16):
                    yv, _ = self.yT_view(yc)
                    sr = self.sqi % 2
                    self.sqi += 1
                    self.act(self.sq[:, sr, :], yv[:, tsl], AF.Square, self.ykeys(yc, tbl), [("sq", sr)])
                    self.mm(self.ps[:, nb, :], self.ones_bf[:, :], self.sq[:, sr, :], yc == 0, yc == 15,
                            [("sq", sr), ("ones",)], [("ps", nb)])
                r = self.uid % 2
                self.uid += 1
                rs = self.rstd[:, r, :]
                rk = ("rstd", r)
                self.act(rs, self.ps[:, nb, :], AF.Sqrt, [("ps", nb), ("eps",)], [rk], bias=self.epsc[:, 0:1],
                         scale=1.0 / 2048)
                self.op("dve", lambda e, rs=rs: e.reciprocal(rs, rs), [rk], [rk])
                for yc in range(16):
                    yv, _ = self.yT_view(yc)
                    self.stt(yv[:, tsl], yv[:, tsl], ngain[:, yc:yc + 1], rs, ALU.mult, ALU.mult,
                             self.ykeys(yc, tbl) + [rk, ("ssdv", 1)], self.ykeys(yc, tbl))
            for cbk in range(4):
                wt, wk = self.load_w(self.d_wout[j, cbk], 16, 256)
                for m in range(2):
                    oc = cbk * 2 + m
                    for tbl in range(2):
                        tsl = slice(tbl * 512, (tbl + 1) * 512)
                        pb = self.psum_alloc("rbc")
                        for yc in range(16):
                            yv, _ = self.yT_view(yc)
                            self.mm(self.ps[:, pb, :], wt[:, yc, m * 128:(m + 1) * 128], yv[:, tsl], yc == 0, yc == 15,
                                    [wk] + self.ykeys(yc, tbl), [("ps", pb)])
                        tb = grp * 2 + tbl
                        xs_ = self.xT[:, oc, tb * 512:(tb + 1) * 512]
                        self.stt(xs_, self.ps[:, pb, :], self.mod[:, l, 16 + oc, ci:ci + 1], xs_, ALU.mult, ALU.add,
                                 [("ps", pb), ("mod", l), ("xall",)], [("x", oc, tb)])
        S.barrier()

    def emit(self):
        nc = self.nc
        names = self.S.finalize()
        sems = {}
        for i, n in enumerate(names):
            sems[n] = self.es.enter_context(nc.semaphore("s%d" % i))
        S = self.S
        with nc.Block() as block:
            @block.tensor
            def _(e):
                S.replay("pe", e, sems)

            @block.scalar
            def _(e):
                S.replay("act", e, sems)

            @block.vector
            def _(e):
                S.replay("dve", e, sems)

            @block.gpsimd
            def _(e):
                S.replay("pool", e, sems)

            @block.sync
            def _(e):
                S.replay("sp", e, sems)
        self.es.close()
        return nc


def _chunk_vec(v):
    sh = v.shape
    c = sh[-1] // 128
    return np.ascontiguousarray(np.swapaxes(v.reshape(sh[:-1] + (c, 128)), -1, -2))


def _wblocks(w, cb):
    K, N = w.shape
    return np.ascontiguousarray(w.reshape(K // 128, 128, N // cb, cb).transpose(2, 1, 0, 3))


def prep_shared(inp, nl):
    sh = {}
    sh["ada_w"] = np.stack([_wblocks(inp["ada_w"][l], 512) for l in range(DEPTH)])
    sh["ada_b"] = np.stack([_chunk_vec(inp["ada_b"][l]) for l in range(DEPTH)])
    sh["gmix"] = np.stack([_chunk_vec(inp["norm_mix_g"][l]) for l in range(DEPTH)])
    sh["gffn"] = np.stack([_chunk_vec(inp["norm_ffn_g"][l]) for l in range(DEPTH)])
    sh["gfin"] = _chunk_vec(inp["final_norm_g"])
    idx = []
    for b in range(11):
        for part in (0, 1):
            for jj in (0, 1):
                j = 2 * b + jj
                idx.append(np.arange(part * DFF + j * 128, part * DFF + (j + 1) * 128))
    idx = np.concatenate(idx)
    sh["w_gu"] = np.stack([_wblocks(inp["ffn_w_gate_up"][l][:, idx], 512) for l in range(DEPTH)])
    sh["w_dn"] = np.stack([_wblocks(inp["ffn_w_down"][l], 256) for l in range(DEPTH)])
    sh["w_qkv"] = np.stack([_wblocks(inp["na_w_qkv"][j], 512) for j in range(2)])
    sh["w_o"] = np.stack([_wblocks(inp["na_w_o"][j], 512) for j in range(2)])
    a = np.arange(2)[:, None, None, None]
    kc = np.arange(64)[None, :, None, None]
    mm = np.arange(14)[None, None, :, None]
    c = np.arange(64)[None, None, None, :]
    dr = a - mm + 6 + 0 * kc + 0 * c
    dc = np.clip(kc - c + 15, 0, 30) + 0 * a + 0 * mm
    c0 = np.clip(c - 8, 0, 48)
    colok = (kc >= c0) & (kc < c0 + 16) & (a >= 0) & (mm >= 0)
    okf = colok
    oki = colok & (dr >= -4) & (dr <= 3)
    rpb = inp["na_rpb"]
    g = rpb[:, :, dr + 7, dc]
    neg = np.float32(-30000.0)
    bf = np.where(okf[None, None], g, neg).reshape(2, NH, 128, 896)
    bi_ = np.where(oki[None, None], g, neg).reshape(2, NH, 128, 896)
    sh["braw"] = np.ascontiguousarray(np.concatenate([bf, bi_], axis=-1).astype(np.float32))
    wA, wB, wdt, cvp, dtp, dvec, ngain, wout = [], [], [], [], [], [], [], []
    for j in range(2):
        w_in = inp["ssd_w_in"][j]
        a_, b_, c_ = [], [], []
        for g in range(8):
            colsA = np.concatenate([g * 256 + np.arange(256), 2048 + g * 256 + np.arange(256)])
            colsB = np.concatenate([4096 + g * 128 + np.arange(128), 5120 + g * 128 + np.arange(128)])
            a_.append(_wblocks(w_in[:, colsA], 512)[0])
            b_.append(_wblocks(w_in[:, colsB], 256)[0])
            chs = [g * 256 + np.arange(128), g * 256 + 128 + np.arange(128), 2048 + g * 128 + np.arange(128),
                   3072 + g * 128 + np.arange(128)]
            cp = np.zeros((128, 16), np.float32)
            for ci_, ch in enumerate(chs):
                cp[:, ci_ * 4:ci_ * 4 + 3] = inp["ssd_conv_w"][j][:, ch].T
                cp[:, ci_ * 4 + 3] = inp["ssd_conv_b"][j][ch]
            c_.append(cp)
        wA.append(np.stack(a_)); wB.append(np.stack(b_)); cvp.append(np.stack(c_))
        wdt.append(_wblocks(w_in[:, 6144:6208], 64)[0])
        dtp.append(np.stack([inp["ssd_dt_bias"][j].reshape(64), inp["ssd_a_log"][j].reshape(64)], axis=-1))
        dvec.append(_chunk_vec(np.repeat(inp["ssd_d"][j], 64)))
        ngain.append(_chunk_vec(inp["ssd_norm_g"][j]))
        wout.append(_wblocks(inp["ssd_w_out"][j], 256))
    sh["w_inA"] = np.stack(wA); sh["w_inB"] = np.stack(wB); sh["w_dt"] = np.stack(wdt)
    sh["convp"] = np.stack(cvp); sh["dtp"] = np.ascontiguousarray(np.stack(dtp).astype(np.float32))
    sh["dvec"] = np.stack(dvec); sh["ngain"] = np.stack(ngain); sh["w_out"] = np.stack(wout)
    ident = np.eye(128, dtype=np.float32)
    sh["consts"] = np.stack([ident, np.triu(np.ones((128, 128), np.float32)), np.tril(np.ones((128, 128), np.float32))])
    return sh


def prep_core(inp, core):
    b = core // 2
    xs = inp["x_sample"][b]
    xp = inp["x_prompt"][4 * core:4 * core + 4].reshape(4 * 256, D)
    xT = np.ascontiguousarray(np.concatenate([xs, xp], axis=0).T)
    cond = np.stack([inp["c"][b], inp["c_ctx"]], axis=-1)
    cond = np.ascontiguousarray(cond.reshape(KD, 128, 2).transpose(1, 0, 2))
    kctxT = np.ascontiguousarray(inp["cache_k"][b].transpose(0, 1, 3, 2).reshape(2, D, 512))
    vctx = np.ascontiguousarray(inp["cache_v"][b].transpose(0, 2, 1, 3).reshape(2, 512, D))
    state0 = np.ascontiguousarray(inp["state_ssm"][b].transpose(0, 1, 4, 2, 3).reshape(2, 2, 128, 2048))
    return {"xT": xT, "cond": cond, "kctxT": kctxT, "vctx": vctx, "state0": state0}


_CACHE = {}


def get_program(cfg):
    if cfg not in _CACHE:
        bld = Builder(*cfg)
        bld.build()
        _CACHE[cfg] = bld.emit()
    return _CACHE[cfg]


def kernel(**inputs):
    inp = {k: np.asarray(v) for k, v in inputs.items()}
    nl = _env_int("K_NL", DEPTH)
    cfg = (nl, bool(_env_int("K_NA", 1)), bool(_env_int("K_SSD", 1)), bool(_env_int("K_FFN", 1)))
    nc = get_program(cfg)
    shared = prep_shared(inp, nl)
    in_maps = []
    for c in range(NCORES):
        m = dict(shared)
        m.update(prep_core(inp, c))
        in_maps.append(m)
    res = run_bass_kernel_spmd(nc, in_maps, core_ids=list(range(NCORES)))
    R = res.results
    y_prompt = np.zeros((32, 256, D), np.float32)
    y_sample = np.zeros((4, 1024, D), np.float32)
    for c in range(NCORES):
        yT = R[c]["yT"]
        if c % 2 == 0:
            y_sample[c // 2] = yT[:, 0:1024].T
        y_prompt[4 * c:4 * c + 4] = yT[:, 1024:2048].T.reshape(4, 256, D)
    new_k = np.zeros((32, 2, NH, 256, 64), np.float32)
    new_v = np.zeros((32, 2, NH, 256, 64), np.float32)
    for c in range(NCORES):
        kT = R[c]["kT_out"]
        vv = R[c]["v_out"]
        new_k[4 * c:4 * c + 4] = kT.reshape(2, NH, 64, 4, 256).transpose(3, 0, 1, 4, 2)
        new_v[4 * c:4 * c + 4] = vv.reshape(2, 4, 256, NH, 64).transpose(1, 0, 3, 2, 4)
    new_s = np.zeros((32, 2, 2, 32, 64, 128), np.float32)
    for c in range(NCORES):
        st = R[c]["st_out"]
        new_s[4 * c:4 * c + 4] = st.reshape(2, 4, 2, 128, 32, 64).transpose(1, 0, 2, 4, 5, 3)
    return (y_prompt, y_sample, new_k, new_v, new_s)
```

```python
import os
import numpy as np
from contextlib import ExitStack
import concourse.bass as bass
import concourse.mybir as mybir
from concourse.bass_types import AP
from concourse.bass_utils import run_bass_kernel_spmd

F32 = mybir.dt.float32
BF16 = mybir.dt.bfloat16
AF = mybir.ActivationFunctionType
ALU = mybir.AluOpType

D = 1024
KD = 8
T = 2048
TG = 1024
DEPTH = 4
DFF = 2816
NH = 16
EPS = 1e-6
NCORES = 8
ARENA = 36608

ENGS = ["pe", "act", "dve", "pool", "sp"]
SAME_SYNC = {"pe": False, "act": True, "dve": True, "pool": False, "sp": False}


class Op:
    __slots__ = ("eng", "fn", "deps", "slot", "signal", "tok")

    def __init__(self, eng, fn, deps, slot):
        self.eng = eng
        self.fn = fn
        self.deps = deps
        self.slot = slot
        self.signal = False
        self.tok = None


class Sched:
    def __init__(self):
        self.ops = []
        self.lw = {}
        self.rd = {}
        self.final_slots = set()
        self.last_se = {}
        self.pending = {}

    def barrier(self):
        last = set(self.last_se.values())
        for e in ENGS:
            self.pending[e] = set(self.pending.get(e, ())) | last

    def add(self, eng, fn, reads=(), writes=(), slot=None, final=False, strict=True):
        i = len(self.ops)
        psr = [k for k in reads if isinstance(k, tuple) and k[0] == "ps"]
        if psr:
            reads = [k for k in reads if not (isinstance(k, tuple) and k[0] == "ps")]
            writes = list(writes) + psr
        deps = set(self.pending.pop(eng, ())) if strict else set()
        for k in reads:
            if k in self.lw:
                deps.add(self.lw[k])
        weak = set()
        for k in writes:
            if k in self.lw:
                weak.add(self.lw[k])
            r = self.rd.get(k)
            if r:
                weak.update(r.values())
        deps.update(weak)
        se = slot if slot is not None else eng
        self.last_se[se] = i
        for k in reads:
            self.rd.setdefault(k, {})[se] = i
        for k in writes:
            self.lw[k] = i
            self.rd[k] = {}
        deps.discard(i)
        self.ops.append(Op(eng, fn, deps, slot))
        if final:
            self.final_slots.add(slot)
        return i

    def barrier_keys(self, keys):
        pass

    def finalize(self):
        ops = self.ops
        for op in ops:
            for d in op.deps:
                dop = ops[d]
                if dop.slot is None and (dop.eng != op.eng or SAME_SYNC[op.eng]):
                    dop.signal = True
        cnt = {}
        for op in ops:
            if op.slot is not None:
                cnt[op.slot] = cnt.get(op.slot, 0) + 16
                op.tok = (op.slot, cnt[op.slot])
            elif op.signal:
                cnt[op.eng] = cnt.get(op.eng, 0) + 1
                op.tok = (op.eng, cnt[op.eng])
        self.cnt = cnt
        self.per_eng = {e: [] for e in ENGS}
        for op in ops:
            self.per_eng[op.eng].append(op)
        return sorted(cnt.keys(), key=str)

    def replay(self, engname, e, sems):
        ops = self.ops
        waited = {}
        for op in self.per_eng[engname]:
            need = {}
            for d in op.deps:
                dop = ops[d]
                if dop.tok is None:
                    continue
                if dop.slot is None and dop.eng == engname and not SAME_SYNC[engname]:
                    continue
                se, v = dop.tok
                if v > need.get(se, 0):
                    need[se] = v
            for se, v in need.items():
                if waited.get(se, 0) < v:
                    e.wait_ge(sems[se], v)
                    waited[se] = v
            ins = op.fn(e)
            if op.tok is not None:
                ins.then_inc(sems[op.tok[0]], 16 if op.slot is not None else 1)
        if engname == "sp":
            for se in sorted(self.final_slots, key=str):
                e.wait_ge(sems[se], self.cnt[se])


def _env_int(name, default):
    v = os.environ.get(name)
    return default if v is None else int(v)


class Builder:
    def __init__(self, nlayers=DEPTH, do_na=True, do_ssd=True, do_ffn=True):
        self.nl = nlayers
        self.do_na = do_na
        self.do_ssd = do_ssd
        self.do_ffn = do_ffn
        self.nc = bass.Bass("TRN2", target_bir_lowering=False)
        self.S = Sched()
        self.es = ExitStack()
        self.wslot = 0
        self.stg = {}
        self.uid = 0
        self.sqi = 0

    def dram_in(self, name, shape):
        return self.nc.dram_tensor(name, list(shape), F32, kind="ExternalInput").ap()

    def dram_out(self, name, shape):
        return self.nc.dram_tensor(name, list(shape), F32, kind="ExternalOutput").ap()

    def sb(self, name, shape, dt):
        return self.es.enter_context(self.nc.sbuf_tensor(name, list(shape), dt))

    def op(self, eng, fn, reads=(), writes=(), slot=None, final=False, strict=True):
        return self.S.add(eng, fn, reads, writes, slot, final, strict)

    def mm(self, out, lhsT, rhs, start, stop, reads, writes):
        self.op("pe", lambda e: e.matmul(out, lhsT, rhs, start=start, stop=stop), reads, writes)

    def act(self, out, in_, func, reads, writes, bias=None, scale=None):
        kw = {}
        if bias is not None:
            kw["bias"] = bias
        if scale is not None:
            kw["scale"] = scale
        self.op("act", lambda e: e.activation(out=out, in_=in_, func=func, **kw), reads, writes)

    def tt(self, out, in0, in1, op, reads, writes, eng="dve"):
        self.op(eng, lambda e: e.tensor_tensor(out=out, in0=in0, in1=in1, op=op), reads, writes)

    def ts(self, out, in0, s1, op0, reads, writes, s2=None, op1=None, eng="dve"):
        if op1 is None:
            self.op(eng, lambda e: e.tensor_scalar(out, in0, s1, scalar2=None, op0=op0), reads, writes)
        else:
            self.op(eng, lambda e: e.tensor_scalar(out, in0, s1, scalar2=s2, op0=op0, op1=op1), reads, writes)

    def stt(self, out, in0, scalar, in1, op0, op1, reads, writes):
        self.op("dve", lambda e: e.scalar_tensor_tensor(out=out, in0=in0, scalar=scalar, in1=in1, op0=op0, op1=op1),
                reads, writes)

    def copy(self, out, in_, reads, writes, eng="dve"):
        if eng == "act":
            self.op("act", lambda e: e.copy(out, in_), reads, writes)
        else:
            self.op(eng, lambda e: e.tensor_copy(out, in_), reads, writes)

    def dma(self, q, out, in_, reads, writes, slot, final=False, strict=True, **kw):
        self.op(q, lambda e: e.dma_start(out=out, in_=in_, **kw), reads, writes, slot=slot, final=final,
                strict=strict)

    def pbank(self, i):
        return self.ps[:, i, :]

    def psum_alloc(self, pool):
        lst = self.pools[pool]
        k = self.pool_idx.get(pool, 0)
        self.pool_idx[pool] = k + 1
        return lst[k % len(lst)]

    def load_w(self, src, kd, cb):
        s = self.wslot % self.NW
        self.wslot += 1
        view = self.wring[:, s, 0:kd * cb].rearrange("p (k c) -> p k c", k=kd)
        self.dma("pool", view, src, [], [("w", s)], slot=("wsem", s), strict=False)
        return view, ("w", s)

    def build(self):
        nc = self.nc
        nl = self.nl
        self.d_xT = self.dram_in("xT", [D, T])
        self.d_cond = self.dram_in("cond", [128, KD, 2])
        self.d_adaw = self.dram_in("ada_w", [DEPTH, 12, 128, KD, 512])
        self.d_adab = self.dram_in("ada_b", [DEPTH, 128, 48])
        self.d_gmix = self.dram_in("gmix", [DEPTH, 128, KD])
        self.d_gffn = self.dram_in("gffn", [DEPTH, 128, KD])
        self.d_gfin = self.dram_in("gfin", [128, KD])
        self.d_wgu = self.dram_in("w_gu", [DEPTH, 11, 128, KD, 512])
        self.d_wdn = self.dram_in("w_dn", [DEPTH, 4, 128, 22, 256])
        self.d_yT = self.dram_out("yT", [D, T])
        self.d_wqkv = self.dram_in("w_qkv", [2, 6, 128, KD, 512])
        self.d_wo = self.dram_in("w_o", [2, 2, 128, KD, 512])
        self.d_braw = self.dram_in("braw", [2, NH, 128, 1792])
        self.d_kctx = self.dram_in("kctxT", [2, D, 512])
        self.d_vctx = self.dram_in("vctx", [2, 512, D])
        self.d_kout = self.dram_out("kT_out", [2, D, 1024])
        self.d_winA = self.dram_in("w_inA", [2, 8, 128, KD, 512])
        self.d_winB = self.dram_in("w_inB", [2, 8, 128, KD, 256])
        self.d_wdt = self.dram_in("w_dt", [2, 128, KD, 64])
        self.d_convp = self.dram_in("convp", [2, 8, 128, 16])
        self.d_dtp = self.dram_in("dtp", [2, 64, 2])
        self.d_dvec = self.dram_in("dvec", [2, 128, 16])
        self.d_ngain = self.dram_in("ngain", [2, 128, 16])
        self.d_wout = self.dram_in("w_out", [2, 4, 128, 16, 256])
        self.d_state0 = self.dram_in("state0", [2, 2, 128, 2048])
        self.d_consts = self.dram_in("consts", [3, 128, 128])
        self.d_stout = self.dram_out("st_out", [2, 4, 2, 128, 2048])
        self.d_vout = self.dram_out("v_out", [2, 1024, D])

        self.xT = self.sb("xT_sb", [128, KD, T], F32)
        self.hT = self.sb("hT_sb", [128, KD, T], BF16)
        self.NW = 3
        self.wring = self.sb("wring", [128, self.NW, 4096], BF16)
        self.arena = self.sb("arena", [128, ARENA], BF16)
        self.mod = self.sb("mod", [128, DEPTH, 48, 2], F32)
        self.Amix = self.sb("Amix", [128, DEPTH, KD, 2], F32)
        self.Affn = self.sb("Affn", [128, DEPTH, KD, 2], F32)
        self.gmix = self.sb("gmix_sb", [128, DEPTH, KD], F32)
        self.gffn = self.sb("gffn_sb", [128, DEPTH, KD], F32)
        self.gfin = self.sb("gfin_sb", [128, KD], F32)
        self.adab = self.sb("adab_sb", [128, DEPTH, 48], F32)
        self.cond = self.sb("cond_sb", [128, KD, 2], F32)
        self.scond = self.sb("scond_sb", [128, KD, 2], BF16)
        self.ones_bf = self.sb("ones_bf", [128, 128], BF16)
        self.epsc = self.sb("epsc", [128, 1], F32)
        self.zero8 = self.sb("zero8", [128, KD], F32)
        self.sq = self.sb("sq", [128, 2, 512], BF16)
        self.rstd = self.sb("rstd", [128, 2, 512], F32)
        self.tmpn = self.sb("tmpn", [128, 2, 512], F32)
        self.identF = self.sb("identF", [128, 128], F32)
        self.identB = self.sb("identB", [128, 128], BF16)
        self.maskLU = self.sb("maskLU", [128, 2, 128], BF16)
        self.onesF = self.sb("onesF", [128, 128], F32)
        self.onec = self.sb("onec", [128, 1], F32)
        self.ps = self.es.enter_context(nc.psum_tensor("ps", [128, 8, 512], F32))
        self.pools = {}
        self.pool_idx = {}

        self.dma("sp", self.xT[:, :, :], self.d_xT.rearrange("(k p) t -> p k t", p=128), [], [("xall",)], slot="xin")
        self.dma("sp", self.cond[:, :, :], self.d_cond, [], [("cond",)], slot=("cst", 0))
        self.dma("sp", self.adab[:, :, :], self.d_adab.rearrange("l p c -> p l c"), [], [("adab",)], slot=("cst", 1))
        self.dma("sp", self.gmix[:, :, :], self.d_gmix.rearrange("l p c -> p l c"), [], [("gmix",)], slot=("cst", 2))
        self.dma("sp", self.gffn[:, :, :], self.d_gffn.rearrange("l p c -> p l c"), [], [("gffn",)], slot=("cst", 3))
        self.dma("sp", self.gfin[:, :], self.d_gfin, [], [("gfin",)], slot=("cst", 4))
        self.op("dve", lambda e: e.memset(self.ones_bf[:, :], 1.0), [], [("ones",)])
        self.op("dve", lambda e: e.memset(self.epsc[:, :], EPS), [], [("eps",)])
        self.op("dve", lambda e: e.memset(self.zero8[:, :], 0.0), [], [("zero8",)])
        self.op("dve", lambda e: e.memset(self.onesF[:, :], 1.0), [], [("onesF",)])
        self.op("dve", lambda e: e.memset(self.onec[:, :], 1.0), [], [("onec",)])
        self.dma("sp", self.identF[:, :], self.d_consts[0], [], [("identF",)], slot=("cst", 5))
        self.copy(self.identB[:, :], self.identF[:, :], [("identF",)], [("identB",)])
        self.dma("pool", self.maskLU[:, :, :], self.d_consts[1:3].rearrange("m p c -> p m c"), [], [("maskLU",)],
                 slot="mlu")
        self.act(self.scond[:, :, :], self.cond[:, :, :], AF.Silu, [("cond",)], [("scond",)])
        self.pools = {"ada": [0, 1]}
        for l in range(nl):
            b = self.psum_alloc("ada")
            pk = ("ps", b)
            pt = self.ps[:, b, 0:96]
            for cb in range(12):
                wt, wk = self.load_w(self.d_adaw[l, cb], KD, 512)
                for j in range(4):
                    oc = cb * 4 + j
                    for kd in range(KD):
                        self.mm(pt[:, oc * 2:oc * 2 + 2], wt[:, kd, j * 128:(j + 1) * 128], self.scond[:, kd, :],
                                kd == 0, kd == KD - 1, [wk, ("scond",)], [pk])
            self.tt(self.mod[:, l, :, :], pt.rearrange("p (c i) -> p c i", i=2),
                    self.adab[:, l, :].unsqueeze(2).to_broadcast([128, 48, 2]), ALU.add,
                    [pk, ("adab",)], [("mod", l)])
            for (A, g, gk, part) in ((self.Amix, self.gmix, ("gmix",), 1), (self.Affn, self.gffn, ("gffn",), 4)):
                self.stt(A[:, l, :, :], self.mod[:, l, part * 8:(part + 1) * 8, :], 1.0,
                         g[:, l, :].unsqueeze(2).to_broadcast([128, KD, 2]), ALU.add, ALU.mult,
                         [("mod", l), gk], [("A", l, part)])

        for l in range(nl):
            if l % 2 == 0:
                if self.do_na:
                    self.na_layer(l)
            else:
                if self.do_ssd:
                    self.ssd_layer(l)
            if self.do_ffn:
                self.ffn(l)
        self.final()

    def xkeys(self, tb):
        return [("x", kd, tb) for kd in range(KD)]

    def norm_mod(self, tb, A_of_kd, B_of_kd, pkeys, out_fn, out_keys_fn, pool="norm"):
        t0 = tb * 512
        xk = self.xkeys(tb) + [("xall",)]
        b = self.psum_alloc(pool)
        pk = ("ps", b)
        for kd in range(KD):
            sr = self.sqi % 2
            self.sqi += 1
            self.act(self.sq[:, sr, :], self.xT[:, kd, t0:t0 + 512], AF.Square, [("x", kd, tb), ("xall",)],
                     [("sq", sr)])
            self.mm(self.ps[:, b, :], self.ones_bf[:, :], self.sq[:, sr, :], kd == 0, kd == KD - 1,
                    [("sq", sr), ("ones",)], [pk])
        r = self.uid % 2
        self.uid += 1
        rs = self.rstd[:, r, :]
        rk = ("rstd", r)
        self.act(rs, self.ps[:, b, :], AF.Sqrt, [pk, ("eps",)], [rk], bias=self.epsc[:, 0:1], scale=1.0 / D)
        self.op("dve", lambda e: e.reciprocal(rs, rs), [rk], [rk])
        for kd in range(KD):
            q = kd % 2
            tk = ("tmpn", q)
            self.stt(self.tmpn[:, q, :], self.xT[:, kd, t0:t0 + 512], A_of_kd(kd), rs, ALU.mult, ALU.mult,
                     [("x", kd, tb), ("xall",), rk] + pkeys, [tk])
            self.act(out_fn(kd), self.tmpn[:, q, :], AF.Identity, [tk] + pkeys, out_keys_fn(kd), bias=B_of_kd(kd))

    def ffn(self, l):
        self.S.barrier()
        self.pools.update({"norm": [6], "g": [0, 1], "u": [2, 3], "dn": [4, 5]})
        act_v = self.arena[:, 0:12 * T].rearrange("p (j t) -> p j t", j=12)
        sg = self.arena[:, 12 * T:12 * T + 4 * 512].bitcast(F32).rearrange("p (r t) -> p r t", r=2)
        for tb in range(4):
            ci = 0 if tb < 2 else 1
            self.norm_mod(tb,
                          lambda kd: self.Affn[:, l, kd, ci:ci + 1],
                          lambda kd: self.mod[:, l, 24 + kd, ci:ci + 1],
                          [("A", l, 4), ("mod", l)],
                          lambda kd: self.hT[:, kd, tb * 512:(tb + 1) * 512],
                          lambda kd: [("h", kd, tb)])
        halves = [(0, 6), (6, 11)]
        sgi = 0
        for (b0, b1) in halves:
            nj = (b1 - b0) * 2
            for bi in range(b0, b1):
                wt, wk = self.load_w(self.d_wgu[l, bi], KD, 512)
                for jj in range(2):
                    jl = (bi - b0) * 2 + jj
                    for tb in range(4):
                        bg = self.psum_alloc("g")
                        bu = self.psum_alloc("u")
                        for kd in range(KD):
                            self.mm(self.ps[:, bg, :], wt[:, kd, jj * 128:(jj + 1) * 128],
                                    self.hT[:, kd, tb * 512:(tb + 1) * 512], kd == 0, kd == KD - 1,
                                    [wk, ("h", kd, tb)], [("ps", bg)])
                        for kd in range(KD):
                            self.mm(self.ps[:, bu, :], wt[:, kd, 256 + jj * 128:256 + (jj + 1) * 128],
                                    self.hT[:, kd, tb * 512:(tb + 1) * 512], kd == 0, kd == KD - 1,
                                    [wk, ("h", kd, tb)], [("ps", bu)])
                        r = sgi % 2
                        sgi += 1
                        self.act(sg[:, r, :], self.ps[:, bg, :], AF.Silu, [("ps", bg)], [("sg", r)])
                        self.tt(act_v[:, jl, tb * 512:(tb + 1) * 512], sg[:, r, :], self.ps[:, bu, :], ALU.mult,
                                [("sg", r), ("ps", bu)], [("act", jl, tb)])
            k0 = b0 * 2
            for cbk in range(4):
                wt, wk = self.load_w(self.d_wdn[l, cbk, :, k0:k0 + nj, :], nj, 256)
                for m in range(2):
                    oc = cbk * 2 + m
                    for tb in range(4):
                        ci = 0 if tb < 2 else 1
                        bd = self.psum_alloc("dn")
                        for j in range(nj):
                            self.mm(self.ps[:, bd, :], wt[:, j, m * 128:(m + 1) * 128],
                                    act_v[:, j, tb * 512:(tb + 1) * 512], j == 0, j == nj - 1,
                                    [wk, ("act", j, tb)], [("ps", bd)])
                        xs = self.xT[:, oc, tb * 512:(tb + 1) * 512]
                        self.stt(xs, self.ps[:, bd, :], self.mod[:, l, 40 + oc, ci:ci + 1], xs, ALU.mult, ALU.add,
                                 [("ps", bd), ("mod", l), ("xall",)], [("x", oc, tb)])

    def final(self):
        self.S.barrier()
        self.pools.update({"norm": [6]})
        stg = self.arena[:, 0:2 * 2 * 512].bitcast(F32).rearrange("p (r t) -> p r t", r=2) if False else None
        ybuf = self.arena[:, 0:4 * 2 * 512].bitcast(F32).rearrange("p (r t) -> p r t", r=4)
        cnt = [0]
        for tb in range(4):
            def out_fn(kd, tb=tb):
                return ybuf[:, (tb * KD + kd) % 4, :]

            def keys_fn(kd, tb=tb):
                return [("ybuf", (tb * KD + kd) % 4)]
            t0 = tb * 512
            xk = self.xkeys(tb) + [("xall",)]
            b = self.psum_alloc("norm")
            pk = ("ps", b)
            for kd in range(KD):
                sr = self.sqi % 2
                self.sqi += 1
                self.act(self.sq[:, sr, :], self.xT[:, kd, t0:t0 + 512], AF.Square, [("x", kd, tb), ("xall",)],
                         [("sq", sr)])
                self.mm(self.ps[:, b, :], self.ones_bf[:, :], self.sq[:, sr, :], kd == 0, kd == KD - 1,
                        [("sq", sr), ("ones",)], [pk])
            r = self.uid % 2
            self.uid += 1
            rs = self.rstd[:, r, :]
            rk = ("rstd", r)
            self.act(rs, self.ps[:, b, :], AF.Sqrt, [pk, ("eps",)], [rk], bias=self.epsc[:, 0:1], scale=1.0 / D)
            self.op("dve", lambda e, rs=rs: e.reciprocal(rs, rs), [rk], [rk])
            for kd in range(KD):
                yi = (tb * KD + kd) % 4
                self.stt(ybuf[:, yi, :], self.xT[:, kd, t0:t0 + 512], self.gfin[:, kd:kd + 1], rs, ALU.mult, ALU.mult,
                         [("x", kd, tb), ("xall",), rk, ("gfin",)], [("ybuf", yi)])
                self.dma("sp", self.d_yT[kd * 128:(kd + 1) * 128, t0:t0 + 512], ybuf[:, yi, :],
                         [("ybuf", yi)], [], slot=("yout", yi), final=True)


    def pv(self, ab, cols, tile, h, rhs, start, stop, reads):
        lo, hi = slice(0, 64), slice(64, 128)
        nr, dr = (lo, hi) if h % 2 == 0 else (hi, lo)
        self.mm(self.ps[nr, ab, cols], self.vt[:, tile, h * 64:(h + 1) * 64], rhs, start, stop, reads, [("ps", ab)])
        self.mm(self.ps[dr, ab, cols], self.ones_bf[:, 0:64], rhs, start, stop, reads + [("ones",)], [("ps", ab)])

    def attn_norm(self, h, ab, tbl):
        c = h // 2
        if h % 2 == 0:
            nr, dr = slice(0, 64), slice(64, 128)
        else:
            nr, dr = slice(64, 128), slice(0, 64)
        r = self.rdi % 2
        self.rdi += 1
        rd = self.rden[:, r, :]
        self.act(rd[dr, :], self.ps[dr, ab, :], AF.Ln, [("ps", ab)], [("rstd", r)])
        self.act(rd[dr, :], rd[dr, :], AF.Exp, [("rstd", r)], [("rstd", r)], scale=-1.0)
        self.tt(self.hT[nr, c, tbl * 512:(tbl + 1) * 512], self.ps[nr, ab, :], rd[dr, :], ALU.mult,
                [("ps", ab), ("rstd", r)], [("h", c, tbl)])

    def na_layer(self, l):
        j = l // 2
        S = self.S
        S.barrier()
        A = self.arena
        kT = A[:, 0:8192].rearrange("p (k t) -> p k t", k=8)
        vt = A[:, 8192:8192 + 12 * 1024].rearrange("p (t c) -> p t c", t=12)
        self.vt = vt
        kctx = A[:, 20480:20480 + 4096].rearrange("p (k t) -> p k t", k=8)
        ostg = A[:, 20480:20480 + 4096].bitcast(F32).rearrange("p (r t) -> p r t", r=4)
        bands = A[:, 24576:24576 + 3584].rearrange("p (r c) -> p r c", r=2)
        brb = A[:, 28160:28160 + 3584].rearrange("p (r c) -> p r c", r=2)
        PT = A[:, 31744:31744 + 1280].rearrange("p (r c) -> p r c", r=2)
        self.rden = self.rstd
        self.rdi = 0
        self.pools.update({"norm": [0], "proj": [0, 1], "st": [2, 3], "st2": [2, 4], "acc": [6, 7]})
        oi = [0]
        pti = [0]

        def out_store(psb, dst):
            r = oi[0] % 4
            oi[0] += 1
            self.copy(ostg[:, r, :], self.ps[:, psb, :], [("ps", psb)], [("ostg", r)], eng="act")
            self.dma("sp", dst, ostg[:, r, :], [("ostg", r)], [], slot=("kvout", r), final=True)

        for grp in (0, 1):
            ci = grp
            if grp == 1:
                S.barrier()
            for tbl in range(2):
                self.norm_mod(grp * 2 + tbl,
                              lambda kd: self.Amix[:, l, kd, ci:ci + 1],
                              lambda kd: self.mod[:, l, kd, ci:ci + 1],
                              [("A", l, 1), ("mod", l)],
                              lambda kd, tbl=tbl: self.hT[:, kd, tbl * 512:(tbl + 1) * 512],
                              lambda kd, tbl=tbl: [("h", kd, tbl)])
            if grp == 0:
                self.dma("pool", kctx[:, :, :], self.d_kctx[j].rearrange("(k p) t -> p k t", p=128), [], [("kctx",)],
                         slot="kctx")
                for kb in range(4):
                    self.dma("pool", vt[:, 8 + kb, :], self.d_vctx[j, kb * 128:(kb + 1) * 128, :], [],
                             [("v", 8 + kb)], slot=("vctx", kb))
            for bi in range(4):
                wt, wk = self.load_w(self.d_wqkv[j, bi], KD, 512)
                for jj in range(4):
                    oc = (bi % 2) * 4 + jj
                    for tbl in range(2):
                        pb = self.psum_alloc("proj")
                        for kd in range(KD):
                            self.mm(self.ps[:, pb, :], wt[:, kd, jj * 128:(jj + 1) * 128],
                                    self.hT[:, kd, tbl * 512:(tbl + 1) * 512], kd == 0, kd == KD - 1,
                                    [wk, ("h", kd, tbl)], [("ps", pb)])
                        if bi < 2:
                            self.copy(self.hT[:, oc, 1024 + tbl * 512:1024 + (tbl + 1) * 512], self.ps[:, pb, :],
                                      [("ps", pb)], [("h", oc, 2 + tbl)], eng="act")
                        else:
                            self.copy(kT[:, oc, tbl * 512:(tbl + 1) * 512], self.ps[:, pb, :],
                                      [("ps", pb)], [("kT", oc, tbl)], eng="dve")
                            if grp == 1:
                                out_store(pb, self.d_kout[j, oc * 128:(oc + 1) * 128, tbl * 512:(tbl + 1) * 512])
            for bi in range(4, 6):
                wt, wk = self.load_w(self.d_wqkv[j, bi], KD, 512)
                for tile in range(8):
                    pb = self.psum_alloc("proj")
                    for kd in range(KD):
                        self.mm(self.ps[:, pb, :], self.hT[:, kd, tile * 128:(tile + 1) * 128], wt[:, kd, :],
                                kd == 0, kd == KD - 1, [wk, ("h", kd, tile // 4)], [("ps", pb)])
                    c0 = (bi - 4) * 512
                    self.copy(vt[:, tile, c0:c0 + 512], self.ps[:, pb, :], [("ps", pb)], [("v", tile, bi)], eng="dve")
                    if grp == 1:
                        out_store(pb, self.d_vout[j, tile * 128:(tile + 1) * 128, (bi - 4) * 512:(bi - 3) * 512])

            def vkeys(tile, h):
                if tile >= 8:
                    return [("v", tile)]
                return [("v", tile, 4 + h // 8)]

            units3 = [2, 4, 0]
            ui = [0]
            steps = []

            def next_unit():
                u_ = units3[ui[0] % 3]
                ui[0] += 1
                return u_

            if grp == 1:
                for sp in range(2):
                    for h in range(NH):
                        c, base = h // 2, (h % 2) * 64
                        stt_ = {}
                        for sq_ in range(2):
                            sidx = sp * 2 + sq_

                            def S_(h=h, c=c, base=base, sidx=sidx, sq_=sq_, stt_=stt_):
                                if sq_ == 0:
                                    stt_["ab"] = self.psum_alloc("acc")
                                st = next_unit()
                                stt_[("st", sq_)] = st
                                for kb in range(2):
                                    k0 = sidx * 256 + kb * 128
                                    self.mm(self.ps[:, st, kb * 256:(kb + 1) * 256], kT[base:base + 64, c, k0:k0 + 128],
                                            self.hT[base:base + 64, c, 1024 + sidx * 256:1024 + (sidx + 1) * 256],
                                            True, True, [("kT", c, sidx // 2), ("h", c, 2 + sidx // 2)], [("ps", st)])

                            def P_(sq_=sq_, stt_=stt_):
                                st = stt_[("st", sq_)]
                                r = pti[0] % 2
                                pti[0] += 1
                                stt_[("r", sq_)] = r
                                self.act(PT[:, r, 0:512], self.ps[:, st, :], AF.Exp, [("ps", st)], [("PT", r)],
                                         scale=0.125)

                            def V_(h=h, sidx=sidx, sq_=sq_, sp=sp, stt_=stt_):
                                ab, r = stt_["ab"], stt_[("r", sq_)]
                                for kb in range(2):
                                    tile = sidx * 2 + kb
                                    self.pv(ab, slice(sq_ * 256, (sq_ + 1) * 256), tile, h,
                                            PT[:, r, kb * 256:(kb + 1) * 256], kb == 0, kb == 1,
                                            [("PT", r)] + vkeys(tile, h))
                                if sq_ == 1:
                                    self.attn_norm(h, ab, sp)
                            steps.append((S_, P_, V_))
            else:
                qtiles = {0: [0, 1, 2, 3], 1: [0, 1, 2, 3], 2: [0, 1, 2, 3, 4], 3: [1, 2, 3, 4, 5],
                          4: [2, 3, 4, 5, 6], 5: [3, 4, 5, 6, 7], 6: [4, 5, 6, 7], 7: [4, 5, 6, 7]}
                for h in range(NH):
                    c, base = h // 2, (h % 2) * 64
                    br = h % 2
                    for qb in range(2):
                        stt_ = {}
                        for kb in range(4):
                            def S_(h=h, c=c, base=base, br=br, qb=qb, kb=kb, stt_=stt_):
                                if qb == 0 and kb == 0:
                                    self.dma("pool", brb[:, br, :], self.d_braw[j, h], [], [("brb", br)],
                                             slot=("brb", br))
                                    self.act(bands[:, br, :], brb[:, br, :], AF.Exp, [("brb", br)], [("bands", br)])
                                if kb == 0:
                                    stt_["ab"] = self.psum_alloc("acc")
                                st = next_unit()
                                stt_[("st", kb)] = st
                                self.mm(self.ps[:, st, :], kctx[base:base + 64, c, kb * 128:(kb + 1) * 128],
                                        self.hT[base:base + 64, c, 1024 + qb * 512:1024 + (qb + 1) * 512], True, True,
                                        [("kctx",), ("h", c, 2 + qb)], [("ps", st)])

                            def P_(kb=kb, stt_=stt_):
                                st = stt_[("st", kb)]
                                r = pti[0] % 2
                                pti[0] += 1
                                stt_[("r", kb)] = r
                                self.act(PT[:, r, 0:512], self.ps[:, st, :], AF.Exp, [("ps", st)], [("PT", r)],
                                         scale=0.125)

                            def V_(h=h, kb=kb, stt_=stt_):
                                ab, r = stt_["ab"], stt_[("r", kb)]
                                self.pv(ab, slice(0, 512), 8 + kb, h, PT[:, r, 0:512], kb == 0, False,
                                        [("PT", r)] + vkeys(8 + kb, h))
                            steps.append((S_, P_, V_))
                        for qi in range(qb * 4, qb * 4 + 4):
                            kjs = qtiles[qi][::-1]
                            nk = len(kjs)

                            def S_(c=c, base=base, qb=qb, qi=qi, kjs=kjs, stt_=stt_):
                                ub = next_unit()
                                stt_[("ub", qi)] = ub
                                for jx, kj in enumerate(kjs):
                                    bb, cc = (ub, jx * 128) if jx < 4 else (ub + 1, 0)
                                    self.mm(self.ps[:, bb, cc:cc + 128], kT[base:base + 64, c, kj * 128:(kj + 1) * 128],
                                            self.hT[base:base + 64, c, 1024 + qi * 128:1024 + (qi + 1) * 128],
                                            True, True, [("kT", c, kj // 4), ("h", c, 2 + qb)],
                                            [("ps", ub), ("ps", ub + 1)])

                            def P_(br=br, qi=qi, kjs=kjs, nk=nk, stt_=stt_):
                                ub = stt_[("ub", qi)]
                                r = pti[0] % 2
                                pti[0] += 1
                                stt_[("r", qi)] = r
                                src = self.ps[:, ub:ub + 2, :].rearrange("p b c -> p (b c)")[:, 0:nk * 128]
                                self.act(PT[:, r, 0:nk * 128], src, AF.Exp, [("ps", ub), ("ps", ub + 1)], [("PT", r)],
                                         scale=0.125)
                                var = 1 if qi in (2, 3, 4, 5) else 0
                                m0 = 6 - 2 * (kjs[0] - qi)
                                bsl = bands[:, br, var * 896 + m0 * 64:var * 896 + m0 * 64 + nk * 128]
                                self.tt(PT[:, r, 0:nk * 128], PT[:, r, 0:nk * 128], bsl, ALU.mult,
                                        [("PT", r), ("bands", br)], [("PT", r)])

                            def V_(h=h, qb=qb, qi=qi, kjs=kjs, nk=nk, stt_=stt_):
                                ab, r = stt_["ab"], stt_[("r", qi)]
                                for jx, kj in enumerate(kjs):
                                    last = (qi == qb * 4 + 3) and (jx == nk - 1)
                                    self.pv(ab, slice((qi % 4) * 128, (qi % 4 + 1) * 128), kj, h,
                                            PT[:, r, jx * 128:(jx + 1) * 128], False, last,
                                            [("PT", r)] + vkeys(kj, h))
                                if qi == qb * 4 + 3:
                                    self.attn_norm(h, ab, qb)
                            steps.append((S_, P_, V_))
            RA = 2
            for k_ in range(min(RA, len(steps))):
                steps[k_][0]()
            for i_, (S_, P_, V_) in enumerate(steps):
                if i_ + RA < len(steps):
                    steps[i_ + RA][0]()
                P_()
                V_()
            for bi in range(2):
                wt, wk = self.load_w(self.d_wo[j, bi], KD, 512)
                for jj in range(4):
                    oc = bi * 4 + jj
                    for tbl in range(2):
                        pb = self.psum_alloc("proj")
                        for kd in range(KD):
                            self.mm(self.ps[:, pb, :], wt[:, kd, jj * 128:(jj + 1) * 128],
                                    self.hT[:, kd, tbl * 512:(tbl + 1) * 512], kd == 0, kd == KD - 1,
                                    [wk, ("h", kd, tbl)], [("ps", pb)])
                        tb = grp * 2 + tbl
                        xs = self.xT[:, oc, tb * 512:(tb + 1) * 512]
                        self.stt(xs, self.ps[:, pb, :], self.mod[:, l, 16 + oc, ci:ci + 1], xs, ALU.mult, ALU.add,
                                 [("ps", pb), ("mod", l), ("xall",)], [("x", oc, tb)])
        S.barrier()


    def yT_view(self, yc):
        if yc < 8:
            return self.hT[:, yc, 1024:2048], ("h", yc)
        return self.arena[:, 0:8192].rearrange("p (k t) -> p k t", k=8)[:, yc - 8, :], ("yhi", yc - 8)

    def ykeys(self, yc, tbl):
        if yc < 8:
            return [("h", yc, 2 + tbl)]
        return [("yhi", yc - 8, tbl)]

    def ssd_layer(self, l):
        j = l // 2
        S = self.S
        S.barrier()
        A = self.arena

        def f32v(off, n):
            return A[:, off:off + 2 * n].bitcast(F32)

        Rt = f32v(8192, 1024)
        Rhm = A[:, 8192:10240].rearrange("p (r t) -> p r t", r=2)
        tok3 = f32v(10240, 1536).rearrange("p (t c) -> p t c", t=8)
        decbc = f32v(13312, 512).rearrange("p (k c) -> p k c", k=64)
        sm = f32v(14336, 128)
        dvec, ngain, dtp, negA = sm[:, 0:16], sm[:, 16:32], sm[:, 32:34], sm[:, 34:35]
        convp = sm[:, 40:72].rearrange("p (r c) -> p r c", r=2)
        G0 = 14592
        xcT = A[:, G0:G0 + 2048].rearrange("p (k t) -> p k t", k=2)
        BT = A[:, G0 + 2048:G0 + 3072]
        CT = A[:, G0 + 3072:G0 + 4096]
        xbtok = A[:, G0 + 4096:G0 + 7168].rearrange("p (t c) -> p t c", t=8)
        sin = A[:, G0 + 7168:G0 + 11264].rearrange("p (t d c) -> p t d c", t=8, d=2)
        Sst = f32v(G0 + 11264, 512).rearrange("p (d c) -> p d c", d=2)
        xs = A[:, G0 + 12288:G0 + 12800].rearrange("p (r c) -> p r c", r=2)
        cbLU = A[:, G0 + 12800:G0 + 13824].rearrange("p (r c) -> p r c", r=2)
        Q0 = G0 + 13824
        Ebuf = A[:, Q0:Q0 + 1024].rearrange("p (r c) -> p r c", r=2)
        Wbuf = A[:, Q0 + 1024:Q0 + 3072].rearrange("p (r c) -> p r c", r=4)
        eabc = A[:, Q0 + 3072:Q0 + 4096].rearrange("p (r c) -> p r c", r=2)
        coff = A[:, Q0 + 4096:Q0 + 6144].rearrange("p (r c) -> p r c", r=4)
        Xm = f32v(Q0 + 6144, 1024).rearrange("p (r c) -> p r c", r=2)
        assert Q0 + 6144 + 2048 <= ARENA
        TR = [f32v(G0 + i * 2048, 1024) for i in range(6)]
        self.pools.update({"norm": [5], "u2": [0, 2], "tr": [4], "rbc": [0, 1, 2, 3, 4], "cb": [5], "y": [6, 7], "sch": [6, 7]})
        ctmp = [(self.tmpn[:, :, :].rearrange("p r t -> p (r t)"), [("tmpn", 0), ("tmpn", 1)]),
                (self.rstd[:, :, :].rearrange("p r t -> p (r t)"), [("rstd", 0), ("rstd", 1)])]
        ysq = self.sq[:, :, :].rearrange("p a t -> p (a t)").bitcast(F32)

        self.dma("sp", dvec, self.d_dvec[j], [], [("ssdv", 0)], slot=("cst", 6))
        self.dma("sp", ngain, self.d_ngain[j], [], [("ssdv", 1)], slot=("cst", 7))
        self.dma("sp", dtp[0:64, :], self.d_dtp[j], [], [("ssdv", 2)], slot=("cst", 8))
        self.act(negA[0:64, :], dtp[0:64, 1:2], AF.Exp, [("ssdv", 2)], [("negA",)])
        self.ts(negA[0:64, :], negA[0:64, :], -1.0, ALU.mult, [("negA",)], [("negA",)])
        cvi = [0]
        cti = [0]
        cnt = {"xs": 0, "E": 0, "W": 0, "ea": 0, "co": 0, "ysq": 0, "X": 0}

        for grp in (0, 1):
            ci = grp
            S.barrier()
            nseq, L = (1, 1024) if grp == 0 else (4, 256)
            for tbl in range(2):
                self.norm_mod(grp * 2 + tbl,
                              lambda kd: self.Amix[:, l, kd, ci:ci + 1],
                              lambda kd: self.mod[:, l, kd, ci:ci + 1],
                              [("A", l, 1), ("mod", l)],
                              lambda kd, tbl=tbl: self.hT[:, kd, tbl * 512:(tbl + 1) * 512],
                              lambda kd, tbl=tbl: [("h", kd, tbl)])
            S.barrier()
            E1, DT, LNDT, DA, ACS, QF = TR
            CM = Rt
            wt, wk = self.load_w(self.d_wdt[j], KD, 64)
            for tbl in range(2):
                pb = self.psum_alloc("rbc")
                for kd in range(KD):
                    self.mm(self.ps[0:64, pb, :], wt[:, kd, 0:64], self.hT[:, kd, tbl * 512:(tbl + 1) * 512],
                            kd == 0, kd == KD - 1, [wk, ("h", kd, tbl)], [("ps", pb)])
                sl = slice(tbl * 512, (tbl + 1) * 512)
                self.act(E1[0:64, sl], self.ps[0:64, pb, :], AF.Exp, [("ps", pb), ("ssdv", 2)], [("E1",)],
                         bias=dtp[0:64, 0:1])
            self.act(DT[0:64, :], E1[0:64, :], AF.Ln, [("E1",)], [("DT",)], bias=self.onec[0:64, 0:1])
            self.act(LNDT[0:64, :], DT[0:64, :], AF.Ln, [("DT",)], [("LNDT",)])
            self.ts(DA[0:64, :], DT[0:64, :], negA[0:64, 0:1], ALU.mult, [("DT",), ("negA",)], [("DA",)])
            self.op("dve", lambda e: e.memset(CM[0:64, :], 1.0), [], [("R",)])
            self.op("dve", lambda e: e.memset(CM[0:64, :].rearrange("p (c q) -> p c q", q=128)[:, :, 0:1], 0.0),
                    [("R",)], [("R",)])
            self.op("dve", lambda e: e.tensor_tensor_scan(out=ACS[0:64, :], data0=CM[0:64, :], data1=DA[0:64, :],
                                                          initial=0.0, op0=ALU.mult, op1=ALU.add),
                    [("R",), ("DA",)], [("ACS",)])
            a3 = ACS[:, :].rearrange("p (c q) -> p c q", q=128)
            r3 = Rt[:, :].rearrange("p (c q) -> p c q", q=128)
            self.copy(Rt[0:32, :], ACS[0:32, :], [("ACS",)], [("R",)])
            self.tt(E1[32:64, :], DA[32:64, :], ACS[32:64, :], ALU.subtract, [("DA",), ("ACS",), ("DT",)], [("E1",)])
            self.tt(r3[32:64, :, :], E1[32:64, :].rearrange("p (c q) -> p c q", q=128),
                    a3[32:64, :, 127:128].to_broadcast([32, 8, 128]), ALU.add, [("E1",), ("ACS",)], [("R",)])
            self.tt(QF[0:64, :], LNDT[0:64, :], Rt[0:64, :], ALU.subtract, [("LNDT",), ("R",)], [("QF",)])
            q3 = QF[:, :].rearrange("p (c q) -> p c q", q=128)
            e3 = E1[:, :].rearrange("p (c q) -> p c q", q=128)
            self.tt(e3[0:32, :, :], q3[0:32, :, :], r3[0:32, :, 127:128].to_broadcast([32, 8, 128]), ALU.add,
                    [("QF",), ("R",), ("E1",)], [("E1",)])
            self.tt(e3[32:64, :, :], q3[32:64, :, :], r3[32:64, :, 0:1].to_broadcast([32, 8, 128]), ALU.add,
                    [("QF",), ("R",), ("E1",)], [("E1",)])
            self.act(QF[64:128, :], E1[0:64, :], AF.Exp, [("E1",)], [("QF",)])
            Tm = ACS[0:64, 0:8]
            self.copy(ACS[0:32, 0:8].unsqueeze(2), r3[0:32, :, 127:128], [("R",), ("ACS",), ("E1",)], [("ACS",)])
            self.copy(ACS[32:64, 0:8].unsqueeze(2), r3[32:64, :, 0:1], [("R",), ("ACS",)], [("ACS",)])
            Texp = DA[0:64, 0:512].rearrange("p (k c) -> p k c", k=64)
            self.tt(Texp, self.identF[0:64, 0:64].unsqueeze(2).to_broadcast([64, 64, 8]),
                    Tm.unsqueeze(1).to_broadcast([64, 64, 8]), ALU.mult, [("ACS",), ("identF",), ("DA",)], [("DA",)])
            pb = self.psum_alloc("rbc")
            self.mm(self.ps[:, pb, :], self.onesF[0:64, :], DA[0:64, 0:512], True, True, [("DA",), ("onesF",)],
                    [("ps", pb)])
            self.act(decbc.rearrange("p k c -> p (k c)"), self.ps[:, pb, :], AF.Exp, [("ps", pb)], [("decbc",)])
            for tile in range(8):
                tb_ = self.psum_alloc("tr")
                self.op("pe", lambda e, tb_=tb_, tile=tile: e.transpose(self.ps[:, tb_, 0:128],
                                                                      QF[:, tile * 128:(tile + 1) * 128],
                                                                      self.identF[:, :]),
                        [("QF",), ("identF",)], [("ps", tb_)])
                self.op("pe", lambda e, tb_=tb_, tile=tile: e.transpose(self.ps[:, tb_, 128:192],
                                                                      Rt[0:64, tile * 128:(tile + 1) * 128],
                                                                      self.identF[0:64, 0:64]),
                        [("R",), ("identF",)], [("ps", tb_)])
                self.copy(tok3[:, tile, :], self.ps[:, tb_, 0:192], [("ps", tb_)], [("tok3",)])
            hiT = E1[0:64, 0:512].bitcast(BF16)
            self.copy(hiT, Rt[0:64, :], [("R",)], [("E1",)])
            self.tt(DT[0:64, :], Rt[0:64, :], hiT, ALU.subtract, [("R",), ("E1",)], [("DT",)])
            self.copy(Rhm[0:64, 0, :], hiT, [("E1",)], [("R",)])
            self.copy(Rhm[0:64, 1, :], DT[0:64, :], [("DT",)], [("R",)])
            S.barrier()

            for g in range(8):
                cr = cvi[0] % 2
                cvi[0] += 1
                self.dma("sp", convp[:, cr, :], self.d_convp[j, g], [], [("convp", cr)], slot=("convp", cr))
                wtA, wkA = self.load_w(self.d_winA[j, g], KD, 512)
                wtB, wkB = self.load_w(self.d_winB[j, g], KD, 256)
                plan = [(wtA, wkA, 0, "z", 0), (wtA, wkA, 1, "z", 1), (wtA, wkA, 2, "x", 0), (wtA, wkA, 3, "x", 1),
                        (wtB, wkB, 0, "B", 0), (wtB, wkB, 1, "C", 0)]
                for (wt, wk, jc, kind, idx) in plan:
                    ub = self.psum_alloc("u2")
                    for tbl in range(2):
                        for kd in range(KD):
                            self.mm(self.ps[:, ub + tbl, :], wt[:, kd, jc * 128:(jc + 1) * 128],
                                    self.hT[:, kd, tbl * 512:(tbl + 1) * 512], kd == 0, kd == KD - 1,
                                    [wk, ("h", kd, tbl)], [("ps", ub), ("ps", ub + 1)])
                    u = self.ps[:, ub:ub + 2, :].rearrange("p b c -> p (b c)")
                    uk = [("ps", ub), ("ps", ub + 1)]
                    if kind == "z":
                        yv, _ = self.yT_view(2 * g + idx)
                        self.act(yv, u, AF.Silu, uk, self.ykeys(2 * g + idx, 0) + self.ykeys(2 * g + idx, 1))
                        continue
                    cidx = {"x": idx, "B": 2, "C": 3}[kind]
                    cp = convp[:, cr, cidx * 4:cidx * 4 + 4]
                    tbuf, tkeys = ctmp[cti[0] % 2]
                    cti[0] += 1
                    self.act(tbuf, u, AF.Identity, uk + [("convp", cr)], tkeys, bias=cp[:, 3:4], scale=cp[:, 1:2])
                    u3 = u.rearrange("p (s q) -> p s q", s=nseq)
                    t3 = tbuf.rearrange("p (s q) -> p s q", s=nseq)
                    self.stt(t3[:, :, 1:L], u3[:, :, 0:L - 1], cp[:, 0:1], t3[:, :, 1:L], ALU.mult, ALU.add,
                             uk + tkeys + [("convp", cr)], tkeys)
                    self.stt(t3[:, :, 0:L - 1], u3[:, :, 1:L], cp[:, 2:3], t3[:, :, 0:L - 1], ALU.mult, ALU.add,
                             uk + tkeys + [("convp", cr)], tkeys)
                    dst, dk = {"x": (xcT[:, idx, :], ("xcT", idx)), "B": (BT, ("BT",)), "C": (CT, ("CT",))}[kind]
                    self.act(dst, tbuf, AF.Silu, tkeys, [dk])
                for tile in range(8):
                    tb_ = self.psum_alloc("tr")
                    psb = self.ps[:, tb_, :].bitcast(BF16)
                    for q_, (src, sk) in enumerate(((xcT[:, 0, :], ("xcT", 0)), (xcT[:, 1, :], ("xcT", 1)),
                                                    (BT, ("BT",)))):
                        self.op("pe", lambda e, psb=psb, q_=q_, src=src, tile=tile: e.transpose(
                            psb[:, q_ * 128:(q_ + 1) * 128], src[:, tile * 128:(tile + 1) * 128], self.identB[:, :]),
                            [sk, ("identB",)], [("ps", tb_)])
                    self.copy(xbtok[:, tile, :], psb[:, 0:384], [("ps", tb_)], [("xbtok", tile)], eng="act")
                for sq_ in range(nseq):
                    nct = L // 128
                    ft = sq_ * nct
                    for d in range(2):
                        Sd = Sst[:, d, :]
                        if grp == 0:
                            self.dma("sp", Sd, self.d_state0[j, d, :, g * 256:(g + 1) * 256], [], [("S", d)],
                                     slot=("st0", d))
                        else:
                            self.op("dve", lambda e, Sd=Sd: e.memset(Sd, 0.0), [], [("S", d)])
                    for ci_ in range(nct):
                        for d in range(2):
                            Sd = Sst[:, d, :]
                            c = ci_ if d == 0 else nct - 1 - ci_
                            tile = ft + c
                            self.copy(sin[:, tile, d, :], Sd, [("S", d)], [("sin", tile, d)], eng="act")
                            r = cnt["xs"] % 2
                            cnt["xs"] += 1
                            k0 = 64 + d * 32 + 4 * g
                            self.tt(xs[:, r, :].rearrange("p (a b) -> p a b", a=4),
                                    xbtok[:, tile, 0:256].rearrange("p (a b) -> p a b", a=4),
                                    tok3[:, tile, k0:k0 + 4].unsqueeze(2).to_broadcast([128, 4, 64]), ALU.mult,
                                    [("xbtok", tile), ("tok3",)], [("xs", r)])
                            sb_ = self.psum_alloc("sch")
                            self.mm(self.ps[:, sb_, 0:256], xbtok[:, tile, 256:384], xs[:, r, :], True, True,
                                    [("xbtok", tile), ("xs", r)], [("ps", sb_)])
                            kd0 = d * 32 + 4 * g
                            self.tt(Sd.rearrange("p (a b) -> p a b", a=4), Sd.rearrange("p (a b) -> p a b", a=4),
                                    decbc[:, kd0:kd0 + 4, tile:tile + 1].to_broadcast([128, 4, 64]), ALU.mult,
                                    [("S", d), ("decbc",)], [("S", d)])
                            self.tt(Sd, Sd, self.ps[:, sb_, 0:256], ALU.add, [("S", d), ("ps", sb_)], [("S", d)])
                    if grp == 1:
                        for d in range(2):
                            self.dma("sp", self.d_stout[j, sq_, d, :, g * 256:(g + 1) * 256], Sst[:, d, :],
                                     [("S", d)], [], slot=("stout", d), final=True)
                units = [(tbl, hp, hh2) for tbl in range(2) for hp in range(2) for hh2 in range(2)]
                st1 = {}

                def stage_cb(tbl):
                    tsl = slice(tbl * 512, (tbl + 1) * 512)
                    cb = self.psum_alloc("cb")
                    for q_ in range(4):
                        tile = tbl * 4 + q_
                        self.mm(self.ps[:, cb, q_ * 128:(q_ + 1) * 128], BT[:, tile * 128:(tile + 1) * 128],
                                CT[:, tile * 128:(tile + 1) * 128], True, True, [("BT",), ("CT",)], [("ps", cb)])
                    for mi in range(2):
                        self.tt(cbLU[:, mi, :].rearrange("p (a b) -> p a b", a=4),
                                self.ps[:, cb, :].rearrange("p (a b) -> p a b", a=4),
                                self.maskLU[:, mi, :].unsqueeze(1).to_broadcast([128, 4, 128]), ALU.mult,
                                [("ps", cb), ("maskLU",)], [("cbLU", mi)])

                dirs = [(u, d) for u in units for d in range(2)]
                info = {}

                rbank = {}

                def stageP(i):
                    (tbl, hp, hh2), d = dirs[i]
                    tsl = slice(tbl * 512, (tbl + 1) * 512)
                    k = d * 32 + 4 * g + hp * 2 + hh2
                    rb = self.psum_alloc("rbc")
                    rbank[i] = rb
                    for r_ in range(2):
                        self.mm(self.ps[:, rb, :], self.identB[0:64, k:k + 1].to_broadcast([64, 128]),
                                Rhm[0:64, r_, tsl], r_ == 0, r_ == 1, [("R",), ("identB",)], [("ps", rb)])

                def stageA(i):
                    (tbl, hp, hh2), d = dirs[i]
                    tsl = slice(tbl * 512, (tbl + 1) * 512)
                    h = 4 * g + hp * 2 + hh2
                    k = d * 32 + h
                    rb = rbank[i]
                    er = cnt["ea"] % 2
                    cnt["ea"] += 1
                    self.act(eabc[:, er, :], self.ps[:, rb, :], AF.Exp, [("ps", rb)], [("eabc", er)])
                    xi = cnt["X"] % 2
                    cnt["X"] += 1
                    self.tt(Xm[:, xi, :].rearrange("p (a b) -> p a b", a=4),
                            self.ps[:, rb, :].rearrange("p (a b) -> p a b", a=4),
                            tok3[:, tbl * 4:tbl * 4 + 4, 128 + k:128 + k + 1].to_broadcast([128, 4, 128]),
                            ALU.min, [("ps", rb), ("tok3",)], [("X", xi)])
                    info[i] = (er, xi)

                def stageB(i):
                    (tbl, hp, hh2), d = dirs[i]
                    u = dirs[i][0]
                    tsl = slice(tbl * 512, (tbl + 1) * 512)
                    h = 4 * g + hp * 2 + hh2
                    k = d * 32 + h
                    er, xi = info[i]
                    ei = cnt["E"] % 2
                    cnt["E"] += 1
                    for q_ in range(4):
                        tile = tbl * 4 + q_
                        self.act(Ebuf[:, ei, q_ * 128:(q_ + 1) * 128], Xm[:, xi, q_ * 128:(q_ + 1) * 128],
                                 AF.Exp, [("X", xi), ("tok3",)], [("E", ei, q_)], bias=tok3[:, tile, k:k + 1])
                    co = cnt["co"] % 4
                    cnt["co"] += 1
                    self.tt(coff[:, co, :], CT[:, tsl], eabc[:, er, :], ALU.mult,
                            [("CT",), ("eabc", er)], [("coff", co)])
                    wi = cnt["W"] % 4
                    cnt["W"] += 1
                    self.tt(Wbuf[:, wi, :], Ebuf[:, ei, :], cbLU[:, d, :], ALU.mult,
                            [("E", ei, q_) for q_ in range(4)] + [("cbLU", d)], [("W", wi)])
                    wr, cr_ = st1.setdefault(u, ({}, {}))
                    wr[d] = wi
                    cr_[d] = co

                ybank = {}

                def stage2(u):
                    tbl, hp, hh2 = u
                    tsl = slice(tbl * 512, (tbl + 1) * 512)
                    hh = hp * 2 + hh2
                    po = hh2 * 64
                    wr, cr_ = st1[u]
                    if hh2 == 0:
                        ybank[(tbl, hp)] = self.psum_alloc("y")
                    yb = ybank[(tbl, hp)]
                    for q_ in range(4):
                        tile = tbl * 4 + q_
                        qs = slice(q_ * 128, (q_ + 1) * 128)
                        out = self.ps[po:po + 64, yb, qs]
                        xl = xbtok[:, tile, hh * 64:(hh + 1) * 64]
                        self.mm(out, xl, Wbuf[:, wr[0], qs], True, False,
                                [("xbtok", tile), ("W", wr[0])], [("ps", yb)])
                        self.mm(out, xl, Wbuf[:, wr[1], qs], False, False,
                                [("xbtok", tile), ("W", wr[1])], [("ps", yb)])
                        for d in range(2):
                            self.mm(out, sin[:, tile, d, hh * 64:(hh + 1) * 64], coff[:, cr_[d], qs],
                                    False, d == 1, [("sin", tile, d), ("coff", cr_[d])], [("ps", yb)])
                    if hh2 == 1:
                        yc = 2 * g + hp
                        self.stt(ysq, xcT[:, hp, tsl], dvec[:, yc:yc + 1], self.ps[:, yb, :],
                                 ALU.mult, ALU.add, [("xcT", hp), ("ssdv", 0), ("ps", yb)], [("sq", 0), ("sq", 1)])
                        yv, _ = self.yT_view(yc)
                        self.tt(yv[:, tsl], ysq, yv[:, tsl], ALU.mult,
                                [("sq", 0), ("sq", 1)] + self.ykeys(yc, tbl), self.ykeys(yc, tbl))

                stage_cb(0)
                AHEAD = 4
                for i in range(min(AHEAD, len(dirs))):
                    stageP(i)
                stageA(0)
                for i in range(len(dirs)):
                    if i + AHEAD < len(dirs):
                        stageP(i + AHEAD)
                    if i + 1 < len(dirs):
                        stageA(i + 1)
                    if i + 1 < len(dirs) and dirs[i + 1][0][0] != dirs[i][0][0] and dirs[i + 1][1] == 0:
                        pass
                    if dirs[i][1] == 0 and dirs[i][0][1] == 0 and dirs[i][0][2] == 0 and dirs[i][0][0] == 1:
                        stage_cb(1)
                    stageB(i)
                    if dirs[i][1] == 1:
                        stage2(dirs[i][0])
            S.barrier()
            for tbl in range(2):
                tsl = slice(tbl * 512, (tbl + 1) * 512)
                nb = self.psum_alloc("norm")
                for yc in range(16):
                    yv, _ = self.yT_view(yc)
                    sr = self.sqi % 2
                    self.sqi += 1
                    self.act(self.sq[:, sr, :], yv[:, tsl], AF.Square, self.ykeys(yc, tbl), [("sq", sr)])
                    self.mm(self.ps[:, nb, :], self.ones_bf[:, :], self.sq[:, sr, :], yc == 0, yc == 15,
                            [("sq", sr), ("ones",)], [("ps", nb)])
                r = self.uid % 2
                self.uid += 1
                rs = self.rstd[:, r, :]
                rk = ("rstd", r)
                self.act(rs, self.ps[:, nb, :], AF.Sqrt, [("ps", nb), ("eps",)], [rk], bias=self.epsc[:, 0:1],
                         scale=1.0 / 2048)
                self.op("dve", lambda e, rs=rs: e.reciprocal(rs, rs), [rk], [rk])
                for yc in range(16):
                    yv, _ = self.yT_view(yc)
                    self.stt(yv[:, tsl], yv[:, tsl], ngain[:, yc:yc + 1], rs, ALU.mult, ALU.mult,
                             self.ykeys(yc, tbl) + [rk, ("ssdv", 1)], self.ykeys(yc, tbl))
            for cbk in range(4):
                wt, wk = self.load_w(self.d_wout[j, cbk], 16, 256)
                for m in range(2):
                    oc = cbk * 2 + m
                    for tbl in range(2):
                        tsl = slice(tbl * 512, (tbl + 1) * 512)
                        pb = self.psum_alloc("rbc")
                        for yc in range(16):
                            yv, _ = self.yT_view(yc)
                            self.mm(self.ps[:, pb, :], wt[:, yc, m * 128:(m + 1) * 128], yv[:, tsl], yc == 0, yc == 15,
                                    [wk] + self.ykeys(yc, tbl), [("ps", pb)])
                        tb = grp * 2 + tbl
                        xs_ = self.xT[:, oc, tb * 512:(tb + 1) * 512]
                        self.stt(xs_, self.ps[:, pb, :], self.mod[:, l, 16 + oc, ci:ci + 1], xs_, ALU.mult, ALU.add,
                                 [("ps", pb), ("mod", l), ("xall",)], [("x", oc, tb)])
        S.barrier()

    def emit(self):
        nc = self.nc
        names = self.S.finalize()
        sems = {}
        for i, n in enumerate(names):
            sems[n] = self.es.enter_context(nc.semaphore("s%d" % i))
        S = self.S
        with nc.Block() as block:
            @block.tensor
            def _(e):
                S.replay("pe", e, sems)

            @block.scalar
            def _(e):
                S.replay("act", e, sems)

            @block.vector
            def _(e):
                S.replay("dve", e, sems)

            @block.gpsimd
            def _(e):
                S.replay("pool", e, sems)

            @block.sync
            def _(e):
                S.replay("sp", e, sems)
        self.es.close()
        return nc


def _chunk_vec(v):
    sh = v.shape
    c = sh[-1] // 128
    return np.ascontiguousarray(np.swapaxes(v.reshape(sh[:-1] + (c, 128)), -1, -2))


def _wblocks(w, cb):
    K, N = w.shape
    return np.ascontiguousarray(w.reshape(K // 128, 128, N // cb, cb).transpose(2, 1, 0, 3))


def prep_shared(inp, nl):
    sh = {}
    sh["ada_w"] = np.stack([_wblocks(inp["ada_w"][l], 512) for l in range(DEPTH)])
    sh["ada_b"] = np.stack([_chunk_vec(inp["ada_b"][l]) for l in range(DEPTH)])
    sh["gmix"] = np.stack([_chunk_vec(inp["norm_mix_g"][l]) for l in range(DEPTH)])
    sh["gffn"] = np.stack([_chunk_vec(inp["norm_ffn_g"][l]) for l in range(DEPTH)])
    sh["gfin"] = _chunk_vec(inp["final_norm_g"])
    idx = []
    for b in range(11):
        for part in (0, 1):
            for jj in (0, 1):
                j = 2 * b + jj
                idx.append(np.arange(part * DFF + j * 128, part * DFF + (j + 1) * 128))
    idx = np.concatenate(idx)
    sh["w_gu"] = np.stack([_wblocks(inp["ffn_w_gate_up"][l][:, idx], 512) for l in range(DEPTH)])
    sh["w_dn"] = np.stack([_wblocks(inp["ffn_w_down"][l], 256) for l in range(DEPTH)])
    sh["w_qkv"] = np.stack([_wblocks(inp["na_w_qkv"][j], 512) for j in range(2)])
    sh["w_o"] = np.stack([_wblocks(inp["na_w_o"][j], 512) for j in range(2)])
    a = np.arange(2)[:, None, None, None]
    kc = np.arange(64)[None, :, None, None]
    mm = np.arange(14)[None, None, :, None]
    c = np.arange(64)[None, None, None, :]
    dr = a - mm + 6 + 0 * kc + 0 * c
    dc = np.clip(kc - c + 15, 0, 30) + 0 * a + 0 * mm
    c0 = np.clip(c - 8, 0, 48)
    colok = (kc >= c0) & (kc < c0 + 16) & (a >= 0) & (mm >= 0)
    okf = colok
    oki = colok & (dr >= -4) & (dr <= 3)
    rpb = inp["na_rpb"]
    g = rpb[:, :, dr + 7, dc]
    neg = np.float32(-30000.0)
    bf = np.where(okf[None, None], g, neg).reshape(2, NH, 128, 896)
    bi_ = np.where(oki[None, None], g, neg).reshape(2, NH, 128, 896)
    sh["braw"] = np.ascontiguousarray(np.concatenate([bf, bi_], axis=-1).astype(np.float32))
    wA, wB, wdt, cvp, dtp, dvec, ngain, wout = [], [], [], [], [], [], [], []
    for j in range(2):
        w_in = inp["ssd_w_in"][j]
        a_, b_, c_ = [], [], []
        for g in range(8):
            colsA = np.concatenate([g * 256 + np.arange(256), 2048 + g * 256 + np.arange(256)])
            colsB = np.concatenate([4096 + g * 128 + np.arange(128), 5120 + g * 128 + np.arange(128)])
            a_.append(_wblocks(w_in[:, colsA], 512)[0])
            b_.append(_wblocks(w_in[:, colsB], 256)[0])
            chs = [g * 256 + np.arange(128), g * 256 + 128 + np.arange(128), 2048 + g * 128 + np.arange(128),
                   3072 + g * 128 + np.arange(128)]
            cp = np.zeros((128, 16), np.float32)
            for ci_, ch in enumerate(chs):
                cp[:, ci_ * 4:ci_ * 4 + 3] = inp["ssd_conv_w"][j][:, ch].T
                cp[:, ci_ * 4 + 3] = inp["ssd_conv_b"][j][ch]
            c_.append(cp)
        wA.append(np.stack(a_)); wB.append(np.stack(b_)); cvp.append(np.stack(c_))
        wdt.append(_wblocks(w_in[:, 6144:6208], 64)[0])
        dtp.append(np.stack([inp["ssd_dt_bias"][j].reshape(64), inp["ssd_a_log"][j].reshape(64)], axis=-1))
        dvec.append(_chunk_vec(np.repeat(inp["ssd_d"][j], 64)))
        ngain.append(_chunk_vec(inp["ssd_norm_g"][j]))
        wout.append(_wblocks(inp["ssd_w_out"][j], 256))
    sh["w_inA"] = np.stack(wA); sh["w_inB"] = np.stack(wB); sh["w_dt"] = np.stack(wdt)
    sh["convp"] = np.stack(cvp); sh["dtp"] = np.ascontiguousarray(np.stack(dtp).astype(np.float32))
    sh["dvec"] = np.stack(dvec); sh["ngain"] = np.stack(ngain); sh["w_out"] = np.stack(wout)
    ident = np.eye(128, dtype=np.float32)
    sh["consts"] = np.stack([ident, np.triu(np.ones((128, 128), np.float32)), np.tril(np.ones((128, 128), np.float32))])
    return sh


def prep_core(inp, core):
    b = core // 2
    xs = inp["x_sample"][b]
    xp = inp["x_prompt"][4 * core:4 * core + 4].reshape(4 * 256, D)
    xT = np.ascontiguousarray(np.concatenate([xs, xp], axis=0).T)
    cond = np.stack([inp["c"][b], inp["c_ctx"]], axis=-1)
    cond = np.ascontiguousarray(cond.reshape(KD, 128, 2).transpose(1, 0, 2))
    kctxT = np.ascontiguousarray(inp["cache_k"][b].transpose(0, 1, 3, 2).reshape(2, D, 512))
    vctx = np.ascontiguousarray(inp["cache_v"][b].transpose(0, 2, 1, 3).reshape(2, 512, D))
    state0 = np.ascontiguousarray(inp["state_ssm"][b].transpose(0, 1, 4, 2, 3).reshape(2, 2, 128, 2048))
    return {"xT": xT, "cond": cond, "kctxT": kctxT, "vctx": vctx, "state0": state0}


_CACHE = {}


def get_program(cfg):
    if cfg not in _CACHE:
        bld = Builder(*cfg)
        bld.build()
        _CACHE[cfg] = bld.emit()
    return _CACHE[cfg]


def kernel(**inputs):
    inp = {k: np.asarray(v) for k, v in inputs.items()}
    nl = _env_int("K_NL", DEPTH)
    cfg = (nl, bool(_env_int("K_NA", 1)), bool(_env_int("K_SSD", 1)), bool(_env_int("K_FFN", 1)))
    nc = get_program(cfg)
    shared = prep_shared(inp, nl)
    in_maps = []
    for c in range(NCORES):
        m = dict(shared)
        m.update(prep_core(inp, c))
        in_maps.append(m)
    res = run_bass_kernel_spmd(nc, in_maps, core_ids=list(range(NCORES)))
    R = res.results
    y_prompt = np.zeros((32, 256, D), np.float32)
    y_sample = np.zeros((4, 1024, D), np.float32)
    for c in range(NCORES):
        yT = R[c]["yT"]
        if c % 2 == 0:
            y_sample[c // 2] = yT[:, 0:1024].T
        y_prompt[4 * c:4 * c + 4] = yT[:, 1024:2048].T.reshape(4, 256, D)
    new_k = np.zeros((32, 2, NH, 256, 64), np.float32)
    new_v = np.zeros((32, 2, NH, 256, 64), np.float32)
    for c in range(NCORES):
        kT = R[c]["kT_out"]
        vv = R[c]["v_out"]
        new_k[4 * c:4 * c + 4] = kT.reshape(2, NH, 64, 4, 256).transpose(3, 0, 1, 4, 2)
        new_v[4 * c:4 * c + 4] = vv.reshape(2, 4, 256, NH, 64).transpose(1, 0, 3, 2, 4)
    new_s = np.zeros((32, 2, 2, 32, 64, 128), np.float32)
    for c in range(NCORES):
        st = R[c]["st_out"]
        new_s[4 * c:4 * c + 4] = st.reshape(2, 4, 2, 128, 32, 64).transpose(1, 0, 2, 4, 5, 3)
    return (y_prompt, y_sample, new_k, new_v, new_s)
```

```python
import os
import numpy as np
from contextlib import ExitStack
import concourse.bass as bass
import concourse.mybir as mybir
from concourse.bass_types import AP
from concourse.bass_utils import run_bass_kernel_spmd

F32 = mybir.dt.float32
BF16 = mybir.dt.bfloat16
AF = mybir.ActivationFunctionType
ALU = mybir.AluOpType

D = 1024
KD = 8
T = 2048
TG = 1024
DEPTH = 4
DFF = 2816
NH = 16
EPS = 1e-6
NCORES = 8
ARENA = 36608

ENGS = ["pe", "act", "dve", "pool", "sp"]
SAME_SYNC = {"pe": False, "act": True, "dve": True, "pool": False, "sp": False}


class Op:
    __slots__ = ("eng", "fn", "deps", "slot", "signal", "tok")

    def __init__(self, eng, fn, deps, slot):
        self.eng = eng
        self.fn = fn
        self.deps = deps
        self.slot = slot
        self.signal = False
        self.tok = None


class Sched:
    def __init__(self):
        self.ops = []
        self.lw = {}
        self.rd = {}
        self.final_slots = set()
        self.last_se = {}
        self.pending = {}

    def barrier(self):
        last = set(self.last_se.values())
        for e in ENGS:
            self.pending[e] = set(self.pending.get(e, ())) | last

    def add(self, eng, fn, reads=(), writes=(), slot=None, final=False, strict=True):
        i = len(self.ops)
        psr = [k for k in reads if isinstance(k, tuple) and k[0] == "ps"]
        if psr:
            reads = [k for k in reads if not (isinstance(k, tuple) and k[0] == "ps")]
            writes = list(writes) + psr
        deps = set(self.pending.pop(eng, ())) if strict else set()
        for k in reads:
            if k in self.lw:
                deps.add(self.lw[k])
        weak = set()
        for k in writes:
            if k in self.lw:
                weak.add(self.lw[k])
            r = self.rd.get(k)
            if r:
                weak.update(r.values())
        deps.update(weak)
        se = slot if slot is not None else eng
        self.last_se[se] = i
        for k in reads:
            self.rd.setdefault(k, {})[se] = i
        for k in writes:
            self.lw[k] = i
            self.rd[k] = {}
        deps.discard(i)
        self.ops.append(Op(eng, fn, deps, slot))
        if final:
            self.final_slots.add(slot)
        return i

    def barrier_keys(self, keys):
        pass

    def finalize(self):
        ops = self.ops
        for op in ops:
            for d in op.deps:
                dop = ops[d]
                if dop.slot is None and (dop.eng != op.eng or SAME_SYNC[op.eng]):
                    dop.signal = True
        cnt = {}
        for op in ops:
            if op.slot is not None:
                cnt[op.slot] = cnt.get(op.slot, 0) + 16
                op.tok = (op.slot, cnt[op.slot])
            elif op.signal:
                cnt[op.eng] = cnt.get(op.eng, 0) + 1
                op.tok = (op.eng, cnt[op.eng])
        self.cnt = cnt
        self.per_eng = {e: [] for e in ENGS}
        for op in ops:
            self.per_eng[op.eng].append(op)
        return sorted(cnt.keys(), key=str)

    def replay(self, engname, e, sems):
        ops = self.ops
        waited = {}
        for op in self.per_eng[engname]:
            need = {}
            for d in op.deps:
                dop = ops[d]
                if dop.tok is None:
                    continue
                if dop.slot is None and dop.eng == engname and not SAME_SYNC[engname]:
                    continue
                se, v = dop.tok
                if v > need.get(se, 0):
                    need[se] = v
            for se, v in need.items():
                if waited.get(se, 0) < v:
                    e.wait_ge(sems[se], v)
                    waited[se] = v
            ins = op.fn(e)
            if op.tok is not None:
                ins.then_inc(sems[op.tok[0]], 16 if op.slot is not None else 1)
        if engname == "sp":
            for se in sorted(self.final_slots, key=str):
                e.wait_ge(sems[se], self.cnt[se])


def _env_int(name, default):
    v = os.environ.get(name)
    return default if v is None else int(v)


class Builder:
    def __init__(self, nlayers=DEPTH, do_na=True, do_ssd=True, do_ffn=True):
        self.nl = nlayers
        self.do_na = do_na
        self.do_ssd = do_ssd
        self.do_ffn = do_ffn
        self.nc = bass.Bass("TRN2", target_bir_lowering=False)
        self.S = Sched()
        self.es = ExitStack()
        self.wslot = 0
        self.stg = {}
        self.uid = 0
        self.sqi = 0

    def dram_in(self, name, shape):
        return self.nc.dram_tensor(name, list(shape), F32, kind="ExternalInput").ap()

    def dram_out(self, name, shape):
        return self.nc.dram_tensor(name, list(shape), F32, kind="ExternalOutput").ap()

    def sb(self, name, shape, dt):
        return self.es.enter_context(self.nc.sbuf_tensor(name, list(shape), dt))

    def op(self, eng, fn, reads=(), writes=(), slot=None, final=False, strict=True):
        return self.S.add(eng, fn, reads, writes, slot, final, strict)

    def mm(self, out, lhsT, rhs, start, stop, reads, writes):
        self.op("pe", lambda e: e.matmul(out, lhsT, rhs, start=start, stop=stop), reads, writes)

    def act(self, out, in_, func, reads, writes, bias=None, scale=None):
        kw = {}
        if bias is not None:
            kw["bias"] = bias
        if scale is not None:
            kw["scale"] = scale
        self.op("act", lambda e: e.activation(out=out, in_=in_, func=func, **kw), reads, writes)

    def tt(self, out, in0, in1, op, reads, writes, eng="dve"):
        self.op(eng, lambda e: e.tensor_tensor(out=out, in0=in0, in1=in1, op=op), reads, writes)

    def ts(self, out, in0, s1, op0, reads, writes, s2=None, op1=None, eng="dve"):
        if op1 is None:
            self.op(eng, lambda e: e.tensor_scalar(out, in0, s1, scalar2=None, op0=op0), reads, writes)
        else:
            self.op(eng, lambda e: e.tensor_scalar(out, in0, s1, scalar2=s2, op0=op0, op1=op1), reads, writes)

    def stt(self, out, in0, scalar, in1, op0, op1, reads, writes):
        self.op("dve", lambda e: e.scalar_tensor_tensor(out=out, in0=in0, scalar=scalar, in1=in1, op0=op0, op1=op1),
                reads, writes)

    def copy(self, out, in_, reads, writes, eng="dve"):
        if eng == "act":
            self.op("act", lambda e: e.copy(out, in_), reads, writes)
        else:
            self.op(eng, lambda e: e.tensor_copy(out, in_), reads, writes)

    def dma(self, q, out, in_, reads, writes, slot, final=False, strict=True, **kw):
        self.op(q, lambda e: e.dma_start(out=out, in_=in_, **kw), reads, writes, slot=slot, final=final,
                strict=strict)

    def pbank(self, i):
        return self.ps[:, i, :]

    def psum_alloc(self, pool):
        lst = self.pools[pool]
        k = self.pool_idx.get(pool, 0)
        self.pool_idx[pool] = k + 1
        return lst[k % len(lst)]

    def load_w(self, src, kd, cb):
        s = self.wslot % self.NW
        self.wslot += 1
        view = self.wring[:, s, 0:kd * cb].rearrange("p (k c) -> p k c", k=kd)
        self.dma("pool", view, src, [], [("w", s)], slot=("wsem", s), strict=False)
        return view, ("w", s)

    def build(self):
        nc = self.nc
        nl = self.nl
        self.d_xT = self.dram_in("xT", [D, T])
        self.d_cond = self.dram_in("cond", [128, KD, 2])
        self.d_adaw = self.dram_in("ada_w", [DEPTH, 12, 128, KD, 512])
        self.d_adab = self.dram_in("ada_b", [DEPTH, 128, 48])
        self.d_gmix = self.dram_in("gmix", [DEPTH, 128, KD])
        self.d_gffn = self.dram_in("gffn", [DEPTH, 128, KD])
        self.d_gfin = self.dram_in("gfin", [128, KD])
        self.d_wgu = self.dram_in("w_gu", [DEPTH, 11, 128, KD, 512])
        self.d_wdn = self.dram_in("w_dn", [DEPTH, 4, 128, 22, 256])
        self.d_yT = self.dram_out("yT", [D, T])
        self.d_wqkv = self.dram_in("w_qkv", [2, 6, 128, KD, 512])
        self.d_wo = self.dram_in("w_o", [2, 2, 128, KD, 512])
        self.d_braw = self.dram_in("braw", [2, NH, 128, 1792])
        self.d_kctx = self.dram_in("kctxT", [2, D, 512])
        self.d_vctx = self.dram_in("vctx", [2, 512, D])
        self.d_kout = self.dram_out("kT_out", [2, D, 1024])
        self.d_winA = self.dram_in("w_inA", [2, 8, 128, KD, 512])
        self.d_winB = self.dram_in("w_inB", [2, 8, 128, KD, 256])
        self.d_wdt = self.dram_in("w_dt", [2, 128, KD, 64])
        self.d_convp = self.dram_in("convp", [2, 8, 128, 16])
        self.d_dtp = self.dram_in("dtp", [2, 64, 2])
        self.d_dvec = self.dram_in("dvec", [2, 128, 16])
        self.d_ngain = self.dram_in("ngain", [2, 128, 16])
        self.d_wout = self.dram_in("w_out", [2, 4, 128, 16, 256])
        self.d_state0 = self.dram_in("state0", [2, 2, 128, 2048])
        self.d_consts = self.dram_in("consts", [3, 128, 128])
        self.d_stout = self.dram_out("st_out", [2, 4, 2, 128, 2048])
        self.d_vout = self.dram_out("v_out", [2, 1024, D])

        self.xT = self.sb("xT_sb", [128, KD, T], F32)
        self.hT = self.sb("hT_sb", [128, KD, T], BF16)
        self.NW = 3
        self.wring = self.sb("wring", [128, self.NW, 4096], BF16)
        self.arena = self.sb("arena", [128, ARENA], BF16)
        self.mod = self.sb("mod", [128, DEPTH, 48, 2], F32)
        self.Amix = self.sb("Amix", [128, DEPTH, KD, 2], F32)
        self.Affn = self.sb("Affn", [128, DEPTH, KD, 2], F32)
        self.gmix = self.sb("gmix_sb", [128, DEPTH, KD], F32)
        self.gffn = self.sb("gffn_sb", [128, DEPTH, KD], F32)
        self.gfin = self.sb("gfin_sb", [128, KD], F32)
        self.adab = self.sb("adab_sb", [128, DEPTH, 48], F32)
        self.cond = self.sb("cond_sb", [128, KD, 2], F32)
        self.scond = self.sb("scond_sb", [128, KD, 2], BF16)
        self.ones_bf = self.sb("ones_bf", [128, 128], BF16)
        self.epsc = self.sb("epsc", [128, 1], F32)
        self.zero8 = self.sb("zero8", [128, KD], F32)
        self.sq = self.sb("sq", [128, 2, 512], BF16)
        self.rstd = self.sb("rstd", [128, 2, 512], F32)
        self.tmpn = self.sb("tmpn", [128, 2, 512], F32)
        self.identF = self.sb("identF", [128, 128], F32)
        self.identB = self.sb("identB", [128, 128], BF16)
        self.maskLU = self.sb("maskLU", [128, 2, 128], BF16)
        self.onesF = self.sb("onesF", [128, 128], F32)
        self.onec = self.sb("onec", [128, 1], F32)
        self.ps = self.es.enter_context(nc.psum_tensor("ps", [128, 8, 512], F32))
        self.pools = {}
        self.pool_idx = {}

        self.dma("sp", self.xT[:, :, :], self.d_xT.rearrange("(k p) t -> p k t", p=128), [], [("xall",)], slot="xin")
        self.dma("sp", self.cond[:, :, :], self.d_cond, [], [("cond",)], slot=("cst", 0))
        self.dma("sp", self.adab[:, :, :], self.d_adab.rearrange("l p c -> p l c"), [], [("adab",)], slot=("cst", 1))
        self.dma("sp", self.gmix[:, :, :], self.d_gmix.rearrange("l p c -> p l c"), [], [("gmix",)], slot=("cst", 2))
        self.dma("sp", self.gffn[:, :, :], self.d_gffn.rearrange("l p c -> p l c"), [], [("gffn",)], slot=("cst", 3))
        self.dma("sp", self.gfin[:, :], self.d_gfin, [], [("gfin",)], slot=("cst", 4))
        self.op("dve", lambda e: e.memset(self.ones_bf[:, :], 1.0), [], [("ones",)])
        self.op("dve", lambda e: e.memset(self.epsc[:, :], EPS), [], [("eps",)])
        self.op("dve", lambda e: e.memset(self.zero8[:, :], 0.0), [], [("zero8",)])
        self.op("dve", lambda e: e.memset(self.onesF[:, :], 1.0), [], [("onesF",)])
        self.op("dve", lambda e: e.memset(self.onec[:, :], 1.0), [], [("onec",)])
        self.dma("sp", self.identF[:, :], self.d_consts[0], [], [("identF",)], slot=("cst", 5))
        self.copy(self.identB[:, :], self.identF[:, :], [("identF",)], [("identB",)])
        self.dma("pool", self.maskLU[:, :, :], self.d_consts[1:3].rearrange("m p c -> p m c"), [], [("maskLU",)],
                 slot="mlu")
        self.act(self.scond[:, :, :], self.cond[:, :, :], AF.Silu, [("cond",)], [("scond",)])
        self.pools = {"ada": [0, 1]}
        for l in range(nl):
            b = self.psum_alloc("ada")
            pk = ("ps", b)
            pt = self.ps[:, b, 0:96]
            for cb in range(12):
                wt, wk = self.load_w(self.d_adaw[l, cb], KD, 512)
                for j in range(4):
                    oc = cb * 4 + j
                    for kd in range(KD):
                        self.mm(pt[:, oc * 2:oc * 2 + 2], wt[:, kd, j * 128:(j + 1) * 128], self.scond[:, kd, :],
                                kd == 0, kd == KD - 1, [wk, ("scond",)], [pk])
            self.tt(self.mod[:, l, :, :], pt.rearrange("p (c i) -> p c i", i=2),
                    self.adab[:, l, :].unsqueeze(2).to_broadcast([128, 48, 2]), ALU.add,
                    [pk, ("adab",)], [("mod", l)])
            for (A, g, gk, part) in ((self.Amix, self.gmix, ("gmix",), 1), (self.Affn, self.gffn, ("gffn",), 4)):
                self.stt(A[:, l, :, :], self.mod[:, l, part * 8:(part + 1) * 8, :], 1.0,
                         g[:, l, :].unsqueeze(2).to_broadcast([128, KD, 2]), ALU.add, ALU.mult,
                         [("mod", l), gk], [("A", l, part)])

        for l in range(nl):
            if l % 2 == 0:
                if self.do_na:
                    self.na_layer(l)
            else:
                if self.do_ssd:
                    self.ssd_layer(l)
            if self.do_ffn:
                self.ffn(l)
        self.final()

    def xkeys(self, tb):
        return [("x", kd, tb) for kd in range(KD)]

    def norm_mod(self, tb, A_of_kd, B_of_kd, pkeys, out_fn, out_keys_fn, pool="norm"):
        t0 = tb * 512
        xk = self.xkeys(tb) + [("xall",)]
        b = self.psum_alloc(pool)
        pk = ("ps", b)
        for kd in range(KD):
            sr = self.sqi % 2
            self.sqi += 1
            self.act(self.sq[:, sr, :], self.xT[:, kd, t0:t0 + 512], AF.Square, [("x", kd, tb), ("xall",)],
                     [("sq", sr)])
            self.mm(self.ps[:, b, :], self.ones_bf[:, :], self.sq[:, sr, :], kd == 0, kd == KD - 1,
                    [("sq", sr), ("ones",)], [pk])
        r = self.uid % 2
        self.uid += 1
        rs = self.rstd[:, r, :]
        rk = ("rstd", r)
        self.act(rs, self.ps[:, b, :], AF.Sqrt, [pk, ("eps",)], [rk], bias=self.epsc[:, 0:1], scale=1.0 / D)
        self.op("dve", lambda e: e.reciprocal(rs, rs), [rk], [rk])
        for kd in range(KD):
            q = kd % 2
            tk = ("tmpn", q)
            self.stt(self.tmpn[:, q, :], self.xT[:, kd, t0:t0 + 512], A_of_kd(kd), rs, ALU.mult, ALU.mult,
                     [("x", kd, tb), ("xall",), rk] + pkeys, [tk])
            self.act(out_fn(kd), self.tmpn[:, q, :], AF.Identity, [tk] + pkeys, out_keys_fn(kd), bias=B_of_kd(kd))

    def ffn(self, l):
        self.S.barrier()
        self.pools.update({"norm": [6], "g": [0, 1], "u": [2, 3], "dn": [4, 5]})
        act_v = self.arena[:, 0:12 * T].rearrange("p (j t) -> p j t", j=12)
        sg = self.arena[:, 12 * T:12 * T + 4 * 512].bitcast(F32).rearrange("p (r t) -> p r t", r=2)
        for tb in range(4):
            ci = 0 if tb < 2 else 1
            self.norm_mod(tb,
                          lambda kd: self.Affn[:, l, kd, ci:ci + 1],
                          lambda kd: self.mod[:, l, 24 + kd, ci:ci + 1],
                          [("A", l, 4), ("mod", l)],
                          lambda kd: self.hT[:, kd, tb * 512:(tb + 1) * 512],
                          lambda kd: [("h", kd, tb)])
        halves = [(0, 6), (6, 11)]
        sgi = 0
        for (b0, b1) in halves:
            nj = (b1 - b0) * 2
            for bi in range(b0, b1):
                wt, wk = self.load_w(self.d_wgu[l, bi], KD, 512)
                for jj in range(2):
                    jl = (bi - b0) * 2 + jj
                    for tb in range(4):
                        bg = self.psum_alloc("g")
                        bu = self.psum_alloc("u")
                        for kd in range(KD):
                            self.mm(self.ps[:, bg, :], wt[:, kd, jj * 128:(jj + 1) * 128],
                                    self.hT[:, kd, tb * 512:(tb + 1) * 512], kd == 0, kd == KD - 1,
                                    [wk, ("h", kd, tb)], [("ps", bg)])
                        for kd in range(KD):
                            self.mm(self.ps[:, bu, :], wt[:, kd, 256 + jj * 128:256 + (jj + 1) * 128],
                                    self.hT[:, kd, tb * 512:(tb + 1) * 512], kd == 0, kd == KD - 1,
                                    [wk, ("h", kd, tb)], [("ps", bu)])
                        r = sgi % 2
                        sgi += 1
                        self.act(sg[:, r, :], self.ps[:, bg, :], AF.Silu, [("ps", bg)], [("sg", r)])
                        self.tt(act_v[:, jl, tb * 512:(tb + 1) * 512], sg[:, r, :], self.ps[:, bu, :], ALU.mult,
                                [("sg", r), ("ps", bu)], [("act", jl, tb)])
            k0 = b0 * 2
            for cbk in range(4):
                wt, wk = self.load_w(self.d_wdn[l, cbk, :, k0:k0 + nj, :], nj, 256)
                for m in range(2):
                    oc = cbk * 2 + m
                    for tb in range(4):
                        ci = 0 if tb < 2 else 1
                        bd = self.psum_alloc("dn")
                        for j in range(nj):
                            self.mm(self.ps[:, bd, :], wt[:, j, m * 128:(m + 1) * 128],
                                    act_v[:, j, tb * 512:(tb + 1) * 512], j == 0, j == nj - 1,
                                    [wk, ("act", j, tb)], [("ps", bd)])
                        xs = self.xT[:, oc, tb * 512:(tb + 1) * 512]
                        self.stt(xs, self.ps[:, bd, :], self.mod[:, l, 40 + oc, ci:ci + 1], xs, ALU.mult, ALU.add,
                                 [("ps", bd), ("mod", l), ("xall",)], [("x", oc, tb)])

    def final(self):
        self.S.barrier()
        self.pools.update({"norm": [6]})
        stg = self.arena[:, 0:2 * 2 * 512].bitcast(F32).rearrange("p (r t) -> p r t", r=2) if False else None
        ybuf = self.arena[:, 0:4 * 2 * 512].bitcast(F32).rearrange("p (r t) -> p r t", r=4)
        cnt = [0]
        for tb in range(4):
            def out_fn(kd, tb=tb):
                return ybuf[:, (tb * KD + kd) % 4, :]

            def keys_fn(kd, tb=tb):
                return [("ybuf", (tb * KD + kd) % 4)]
            t0 = tb * 512
            xk = self.xkeys(tb) + [("xall",)]
            b = self.psum_alloc("norm")
            pk = ("ps", b)
            for kd in range(KD):
                sr = self.sqi % 2
                self.sqi += 1
                self.act(self.sq[:, sr, :], self.xT[:, kd, t0:t0 + 512], AF.Square, [("x", kd, tb), ("xall",)],
                         [("sq", sr)])
                self.mm(self.ps[:, b, :], self.ones_bf[:, :], self.sq[:, sr, :], kd == 0, kd == KD - 1,
                        [("sq", sr), ("ones",)], [pk])
            r = self.uid % 2
            self.uid += 1
            rs = self.rstd[:, r, :]
            rk = ("rstd", r)
            self.act(rs, self.ps[:, b, :], AF.Sqrt, [pk, ("eps",)], [rk], bias=self.epsc[:, 0:1], scale=1.0 / D)
            self.op("dve", lambda e, rs=rs: e.reciprocal(rs, rs), [rk], [rk])
            for kd in range(KD):
                yi = (tb * KD + kd) % 4
                self.stt(ybuf[:, yi, :], self.xT[:, kd, t0:t0 + 512], self.gfin[:, kd:kd + 1], rs, ALU.mult, ALU.mult,
                         [("x", kd, tb), ("xall",), rk, ("gfin",)], [("ybuf", yi)])
                self.dma("sp", self.d_yT[kd * 128:(kd + 1) * 128, t0:t0 + 512], ybuf[:, yi, :],
                         [("ybuf", yi)], [], slot=("yout", yi), final=True)


    def pv(self, ab, cols, tile, h, rhs, start, stop, reads):
        lo, hi = slice(0, 64), slice(64, 128)
        nr, dr = (lo, hi) if h % 2 == 0 else (hi, lo)
        self.mm(self.ps[nr, ab, cols], self.vt[:, tile, h * 64:(h + 1) * 64], rhs, start, stop, reads, [("ps", ab)])
        self.mm(self.ps[dr, ab, cols], self.ones_bf[:, 0:64], rhs, start, stop, reads + [("ones",)], [("ps", ab)])

    def attn_norm(self, h, ab, tbl):
        c = h // 2
        if h % 2 == 0:
            nr, dr = slice(0, 64), slice(64, 128)
        else:
            nr, dr = slice(64, 128), slice(0, 64)
        r = self.rdi % 2
        self.rdi += 1
        rd = self.rden[:, r, :]
        self.act(rd[dr, :], self.ps[dr, ab, :], AF.Ln, [("ps", ab)], [("rstd", r)])
        self.act(rd[dr, :], rd[dr, :], AF.Exp, [("rstd", r)], [("rstd", r)], scale=-1.0)
        self.tt(self.hT[nr, c, tbl * 512:(tbl + 1) * 512], self.ps[nr, ab, :], rd[dr, :], ALU.mult,
                [("ps", ab), ("rstd", r)], [("h", c, tbl)])

    def na_layer(self, l):
        j = l // 2
        S = self.S
        S.barrier()
        A = self.arena
        kT = A[:, 0:8192].rearrange("p (k t) -> p k t", k=8)
        vt = A[:, 8192:8192 + 12 * 1024].rearrange("p (t c) -> p t c", t=12)
        self.vt = vt
        kctx = A[:, 20480:20480 + 4096].rearrange("p (k t) -> p k t", k=8)
        ostg = A[:, 20480:20480 + 4096].bitcast(F32).rearrange("p (r t) -> p r t", r=4)
        bands = A[:, 24576:24576 + 3584].rearrange("p (r c) -> p r c", r=2)
        brb = A[:, 28160:28160 + 3584].rearrange("p (r c) -> p r c", r=2)
        PT = A[:, 31744:31744 + 1280].rearrange("p (r c) -> p r c", r=2)
        self.rden = self.rstd
        self.rdi = 0
        self.pools.update({"norm": [0], "proj": [0, 1], "st": [2, 3], "st2": [2, 4], "acc": [6, 7]})
        oi = [0]
        pti = [0]

        def out_store(psb, dst):
            r = oi[0] % 4
            oi[0] += 1
            self.copy(ostg[:, r, :], self.ps[:, psb, :], [("ps", psb)], [("ostg", r)], eng="act")
            self.dma("sp", dst, ostg[:, r, :], [("ostg", r)], [], slot=("kvout", r), final=True)

        for grp in (0, 1):
            ci = grp
            if grp == 1:
                S.barrier()
            for tbl in range(2):
                self.norm_mod(grp * 2 + tbl,
                              lambda kd: self.Amix[:, l, kd, ci:ci + 1],
                              lambda kd: self.mod[:, l, kd, ci:ci + 1],
                              [("A", l, 1), ("mod", l)],
                              lambda kd, tbl=tbl: self.hT[:, kd, tbl * 512:(tbl + 1) * 512],
                              lambda kd, tbl=tbl: [("h", kd, tbl)])
            if grp == 0:
                self.dma("pool", kctx[:, :, :], self.d_kctx[j].rearrange("(k p) t -> p k t", p=128), [], [("kctx",)],
                         slot="kctx")
                for kb in range(4):
                    self.dma("pool", vt[:, 8 + kb, :], self.d_vctx[j, kb * 128:(kb + 1) * 128, :], [],
                             [("v", 8 + kb)], slot=("vctx", kb))
            for bi in range(4):
                wt, wk = self.load_w(self.d_wqkv[j, bi], KD, 512)
                for jj in range(4):
                    oc = (bi % 2) * 4 + jj
                    for tbl in range(2):
                        pb = self.psum_alloc("proj")
                        for kd in range(KD):
                            self.mm(self.ps[:, pb, :], wt[:, kd, jj * 128:(jj + 1) * 128],
                                    self.hT[:, kd, tbl * 512:(tbl + 1) * 512], kd == 0, kd == KD - 1,
                                    [wk, ("h", kd, tbl)], [("ps", pb)])
                        if bi < 2:
                            self.copy(self.hT[:, oc, 1024 + tbl * 512:1024 + (tbl + 1) * 512], self.ps[:, pb, :],
                                      [("ps", pb)], [("h", oc, 2 + tbl)], eng="act")
                        else:
                            self.copy(kT[:, oc, tbl * 512:(tbl + 1) * 512], self.ps[:, pb, :],
                                      [("ps", pb)], [("kT", oc, tbl)], eng="dve")
                            if grp == 1:
                                out_store(pb, self.d_kout[j, oc * 128:(oc + 1) * 128, tbl * 512:(tbl + 1) * 512])
            for bi in range(4, 6):
                wt, wk = self.load_w(self.d_wqkv[j, bi], KD, 512)
                for tile in range(8):
                    pb = self.psum_alloc("proj")
                    for kd in range(KD):
                        self.mm(self.ps[:, pb, :], self.hT[:, kd, tile * 128:(tile + 1) * 128], wt[:, kd, :],
                                kd == 0, kd == KD - 1, [wk, ("h", kd, tile // 4)], [("ps", pb)])
                    c0 = (bi - 4) * 512
                    self.copy(vt[:, tile, c0:c0 + 512], self.ps[:, pb, :], [("ps", pb)], [("v", tile, bi)], eng="dve")
                    if grp == 1:
                        out_store(pb, self.d_vout[j, tile * 128:(tile + 1) * 128, (bi - 4) * 512:(bi - 3) * 512])

            def vkeys(tile, h):
                if tile >= 8:
                    return [("v", tile)]
                return [("v", tile, 4 + h // 8)]

            units3 = [2, 4, 0]
            ui = [0]
            steps = []

            def next_unit():
                u_ = units3[ui[0] % 3]
                ui[0] += 1
                return u_

            if grp == 1:
                for sp in range(2):
                    for h in range(NH):
                        c, base = h // 2, (h % 2) * 64
                        stt_ = {}
                        for sq_ in range(2):
                            sidx = sp * 2 + sq_

                            def S_(h=h, c=c, base=base, sidx=sidx, sq_=sq_, stt_=stt_):
                                if sq_ == 0:
                                    stt_["ab"] = self.psum_alloc("acc")
                                st = next_unit()
                                stt_[("st", sq_)] = st
                                for kb in range(2):
                                    k0 = sidx * 256 + kb * 128
                                    self.mm(self.ps[:, st, kb * 256:(kb + 1) * 256], kT[base:base + 64, c, k0:k0 + 128],
                                            self.hT[base:base + 64, c, 1024 + sidx * 256:1024 + (sidx + 1) * 256],
                                            True, True, [("kT", c, sidx // 2), ("h", c, 2 + sidx // 2)], [("ps", st)])

                            def P_(sq_=sq_, stt_=stt_):
                                st = stt_[("st", sq_)]
                                r = pti[0] % 2
                                pti[0] += 1
                                stt_[("r", sq_)] = r
                                self.act(PT[:, r, 0:512], self.ps[:, st, :], AF.Exp, [("ps", st)], [("PT", r)],
                                         scale=0.125)

                            def V_(h=h, sidx=sidx, sq_=sq_, sp=sp, stt_=stt_):
                                ab, r = stt_["ab"], stt_[("r", sq_)]
                                for kb in range(2):
                                    tile = sidx * 2 + kb
                                    self.pv(ab, slice(sq_ * 256, (sq_ + 1) * 256), tile, h,
                                            PT[:, r, kb * 256:(kb + 1) * 256], kb == 0, kb == 1,
                                            [("PT", r)] + vkeys(tile, h))
                                if sq_ == 1:
                                    self.attn_norm(h, ab, sp)
                            steps.append((S_, P_, V_))
            else:
                qtiles = {0: [0, 1, 2, 3], 1: [0, 1, 2, 3], 2: [0, 1, 2, 3, 4], 3: [1, 2, 3, 4, 5],
                          4: [2, 3, 4, 5, 6], 5: [3, 4, 5, 6, 7], 6: [4, 5, 6, 7], 7: [4, 5, 6, 7]}
                for h in range(NH):
                    c, base = h // 2, (h % 2) * 64
                    br = h % 2
                    for qb in range(2):
                        stt_ = {}
                        for kb in range(4):
                            def S_(h=h, c=c, base=base, br=br, qb=qb, kb=kb, stt_=stt_):
                                if qb == 0 and kb == 0:
                                    self.dma("pool", brb[:, br, :], self.d_braw[j, h], [], [("brb", br)],
                                             slot=("brb", br))
                                    self.act(bands[:, br, :], brb[:, br, :], AF.Exp, [("brb", br)], [("bands", br)])
                                if kb == 0:
                                    stt_["ab"] = self.psum_alloc("acc")
                                st = next_unit()
                                stt_[("st", kb)] = st
                                self.mm(self.ps[:, st, :], kctx[base:base + 64, c, kb * 128:(kb + 1) * 128],
                                        self.hT[base:base + 64, c, 1024 + qb * 512:1024 + (qb + 1) * 512], True, True,
                                        [("kctx",), ("h", c, 2 + qb)], [("ps", st)])

                            def P_(kb=kb, stt_=stt_):
                                st = stt_[("st", kb)]
                                r = pti[0] % 2
                                pti[0] += 1
                                stt_[("r", kb)] = r
                                self.act(PT[:, r, 0:512], self.ps[:, st, :], AF.Exp, [("ps", st)], [("PT", r)],
                                         scale=0.125)

                            def V_(h=h, kb=kb, stt_=stt_):
                                ab, r = stt_["ab"], stt_[("r", kb)]
                                self.pv(ab, slice(0, 512), 8 + kb, h, PT[:, r, 0:512], kb == 0, False,
                                        [("PT", r)] + vkeys(8 + kb, h))
                            steps.append((S_, P_, V_))
                        for qi in range(qb * 4, qb * 4 + 4):
                            kjs = qtiles[qi][::-1]
                            nk = len(kjs)

                            def S_(c=c, base=base, qb=qb, qi=qi, kjs=kjs, stt_=stt_):
                                ub = next_unit()
                                stt_[("ub", qi)] = ub
                                for jx, kj in enumerate(kjs):
                                    bb, cc = (ub, jx * 128) if jx < 4 else (ub + 1, 0)
                                    self.mm(self.ps[:, bb, cc:cc + 128], kT[base:base + 64, c, kj * 128:(kj + 1) * 128],
                                            self.hT[base:base + 64, c, 1024 + qi * 128:1024 + (qi + 1) * 128],
                                            True, True, [("kT", c, kj // 4), ("h", c, 2 + qb)],
                                            [("ps", ub), ("ps", ub + 1)])

                            def P_(br=br, qi=qi, kjs=kjs, nk=nk, stt_=stt_):
                                ub = stt_[("ub", qi)]
                                r = pti[0] % 2
                                pti[0] += 1
                                stt_[("r", qi)] = r
                                src = self.ps[:, ub:ub + 2, :].rearrange("p b c -> p (b c)")[:, 0:nk * 128]
                                self.act(PT[:, r, 0:nk * 128], src, AF.Exp, [("ps", ub), ("ps", ub + 1)], [("PT", r)],
                                         scale=0.125)
                                var = 1 if qi in (2, 3, 4, 5) else 0
                                m0 = 6 - 2 * (kjs[0] - qi)
                                bsl = bands[:, br, var * 896 + m0 * 64:var * 896 + m0 * 64 + nk * 128]
                                self.tt(PT[:, r, 0:nk * 128], PT[:, r, 0:nk * 128], bsl, ALU.mult,
                                        [("PT", r), ("bands", br)], [("PT", r)])

                            def V_(h=h, qb=qb, qi=qi, kjs=kjs, nk=nk, stt_=stt_):
                                ab, r = stt_["ab"], stt_[("r", qi)]
                                for jx, kj in enumerate(kjs):
                                    last = (qi == qb * 4 + 3) and (jx == nk - 1)
                                    self.pv(ab, slice((qi % 4) * 128, (qi % 4 + 1) * 128), kj, h,
                                            PT[:, r, jx * 128:(jx + 1) * 128], False, last,
                                            [("PT", r)] + vkeys(kj, h))
                                if qi == qb * 4 + 3:
                                    self.attn_norm(h, ab, qb)
                            steps.append((S_, P_, V_))
            RA = 2
            for k_ in range(min(RA, len(steps))):
                steps[k_][0]()
            for i_, (S_, P_, V_) in enumerate(steps):
                if i_ + RA < len(steps):
                    steps[i_ + RA][0]()
                P_()
                V_()
            for bi in range(2):
                wt, wk = self.load_w(self.d_wo[j, bi], KD, 512)
                for jj in range(4):
                    oc = bi * 4 + jj
                    for tbl in range(2):
                        pb = self.psum_alloc("proj")
                        for kd in range(KD):
                            self.mm(self.ps[:, pb, :], wt[:, kd, jj * 128:(jj + 1) * 128],
                                    self.hT[:, kd, tbl * 512:(tbl + 1) * 512], kd == 0, kd == KD - 1,
                                    [wk, ("h", kd, tbl)], [("ps", pb)])
                        tb = grp * 2 + tbl
                        xs = self.xT[:, oc, tb * 512:(tb + 1) * 512]
                        self.stt(xs, self.ps[:, pb, :], self.mod[:, l, 16 + oc, ci:ci + 1], xs, ALU.mult, ALU.add,
                                 [("ps", pb), ("mod", l), ("xall",)], [("x", oc, tb)])
        S.barrier()


    def yT_view(self, yc):
        if yc < 8:
            return self.hT[:, yc, 1024:2048], ("h", yc)
        return self.arena[:, 0:8192].rearrange("p (k t) -> p k t", k=8)[:, yc - 8, :], ("yhi", yc - 8)

    def ykeys(self, yc, tbl):
        if yc < 8:
            return [("h", yc, 2 + tbl)]
        return [("yhi", yc - 8, tbl)]

    def ssd_layer(self, l):
        j = l // 2
        S = self.S
        S.barrier()
        A = self.arena

        def f32v(off, n):
            return A[:, off:off + 2 * n].bitcast(F32)

        Rt = f32v(8192, 1024)
        Rhm = A[:, 8192:10240].rearrange("p (r t) -> p r t", r=2)
        tok3 = f32v(10240, 1536).rearrange("p (t c) -> p t c", t=8)
        decbc = f32v(13312, 512).rearrange("p (k c) -> p k c", k=64)
        sm = f32v(14336, 128)
        dvec, ngain, dtp, negA = sm[:, 0:16], sm[:, 16:32], sm[:, 32:34], sm[:, 34:35]
        convp = sm[:, 40:72].rearrange("p (r c) -> p r c", r=2)
        G0 = 14592
        xcT = A[:, G0:G0 + 2048].rearrange("p (k t) -> p k t", k=2)
        BT = A[:, G0 + 2048:G0 + 3072]
        CT = A[:, G0 + 3072:G0 + 4096]
        xbtok = A[:, G0 + 4096:G0 + 7168].rearrange("p (t c) -> p t c", t=8)
        sin = A[:, G0 + 7168:G0 + 11264].rearrange("p (t d c) -> p t d c", t=8, d=2)
        Sst = f32v(G0 + 11264, 512).rearrange("p (d c) -> p d c", d=2)
        xs = A[:, G0 + 12288:G0 + 12800].rearrange("p (r c) -> p r c", r=2)
        cbLU = A[:, G0 + 12800:G0 + 13824].rearrange("p (r c) -> p r c", r=2)
        Q0 = G0 + 13824
        Ebuf = A[:, Q0:Q0 + 1024].rearrange("p (r c) -> p r c", r=2)
        Wbuf = A[:, Q0 + 1024:Q0 + 3072].rearrange("p (r c) -> p r c", r=4)
        eabc = A[:, Q0 + 3072:Q0 + 4096].rearrange("p (r c) -> p r c", r=2)
        coff = A[:, Q0 + 4096:Q0 + 6144].rearrange("p (r c) -> p r c", r=4)
        Xm = f32v(Q0 + 6144, 1024).rearrange("p (r c) -> p r c", r=2)
        assert Q0 + 6144 + 2048 <= ARENA
        TR = [f32v(G0 + i * 2048, 1024) for i in range(6)]
        self.pools.update({"norm": [5], "u2": [0, 2], "tr": [4], "rbc": [0, 1, 2, 3, 4], "cb": [5], "y": [6, 7], "sch": [6, 7]})
        ctmp = [(self.tmpn[:, :, :].rearrange("p r t -> p (r t)"), [("tmpn", 0), ("tmpn", 1)]),
                (self.rstd[:, :, :].rearrange("p r t -> p (r t)"), [("rstd", 0), ("rstd", 1)])]
        ysq = self.sq[:, :, :].rearrange("p a t -> p (a t)").bitcast(F32)

        self.dma("sp", dvec, self.d_dvec[j], [], [("ssdv", 0)], slot=("cst", 6))
        self.dma("sp", ngain, self.d_ngain[j], [], [("ssdv", 1)], slot=("cst", 7))
        self.dma("sp", dtp[0:64, :], self.d_dtp[j], [], [("ssdv", 2)], slot=("cst", 8))
        self.act(negA[0:64, :], dtp[0:64, 1:2], AF.Exp, [("ssdv", 2)], [("negA",)])
        self.ts(negA[0:64, :], negA[0:64, :], -1.0, ALU.mult, [("negA",)], [("negA",)])
        cvi = [0]
        cti = [0]
        cnt = {"xs": 0, "E": 0, "W": 0, "ea": 0, "co": 0, "ysq": 0, "X": 0}

        for grp in (0, 1):
            ci = grp
            S.barrier()
            nseq, L = (1, 1024) if grp == 0 else (4, 256)
            for tbl in range(2):
                self.norm_mod(grp * 2 + tbl,
                              lambda kd: self.Amix[:, l, kd, ci:ci + 1],
                              lambda kd: self.mod[:, l, kd, ci:ci + 1],
                              [("A", l, 1), ("mod", l)],
                              lambda kd, tbl=tbl: self.hT[:, kd, tbl * 512:(tbl + 1) * 512],
                              lambda kd, tbl=tbl: [("h", kd, tbl)])
            S.barrier()
            E1, DT, LNDT, DA, ACS, QF = TR
            CM = Rt
            wt, wk = self.load_w(self.d_wdt[j], KD, 64)
            for tbl in range(2):
                pb = self.psum_alloc("rbc")
                for kd in range(KD):
                    self.mm(self.ps[0:64, pb, :], wt[:, kd, 0:64], self.hT[:, kd, tbl * 512:(tbl + 1) * 512],
                            kd == 0, kd == KD - 1, [wk, ("h", kd, tbl)], [("ps", pb)])
                sl = slice(tbl * 512, (tbl + 1) * 512)
                self.act(E1[0:64, sl], self.ps[0:64, pb, :], AF.Exp, [("ps", pb), ("ssdv", 2)], [("E1",)],
                         bias=dtp[0:64, 0:1])
            self.act(DT[0:64, :], E1[0:64, :], AF.Ln, [("E1",)], [("DT",)], bias=self.onec[0:64, 0:1])
            self.act(LNDT[0:64, :], DT[0:64, :], AF.Ln, [("DT",)], [("LNDT",)])
            self.ts(DA[0:64, :], DT[0:64, :], negA[0:64, 0:1], ALU.mult, [("DT",), ("negA",)], [("DA",)])
            self.op("dve", lambda e: e.memset(CM[0:64, :], 1.0), [], [("R",)])
            self.op("dve", lambda e: e.memset(CM[0:64, :].rearrange("p (c q) -> p c q", q=128)[:, :, 0:1], 0.0),
                    [("R",)], [("R",)])
            self.op("dve", lambda e: e.tensor_tensor_scan(out=ACS[0:64, :], data0=CM[0:64, :], data1=DA[0:64, :],
                                                          initial=0.0, op0=ALU.mult, op1=ALU.add),
                    [("R",), ("DA",)], [("ACS",)])
            a3 = ACS[:, :].rearrange("p (c q) -> p c q", q=128)
            r3 = Rt[:, :].rearrange("p (c q) -> p c q", q=128)
            self.copy(Rt[0:32, :], ACS[0:32, :], [("ACS",)], [("R",)])
            self.tt(E1[32:64, :], DA[32:64, :], ACS[32:64, :], ALU.subtract, [("DA",), ("ACS",), ("DT",)], [("E1",)])
            self.tt(r3[32:64, :, :], E1[32:64, :].rearrange("p (c q) -> p c q", q=128),
                    a3[32:64, :, 127:128].to_broadcast([32, 8, 128]), ALU.add, [("E1",), ("ACS",)], [("R",)])
            self.tt(QF[0:64, :], LNDT[0:64, :], Rt[0:64, :], ALU.subtract, [("LNDT",), ("R",)], [("QF",)])
            q3 = QF[:, :].rearrange("p (c q) -> p c q", q=128)
            e3 = E1[:, :].rearrange("p (c q) -> p c q", q=128)
            self.tt(e3[0:32, :, :], q3[0:32, :, :], r3[0:32, :, 127:128].to_broadcast([32, 8, 128]), ALU.add,
                    [("QF",), ("R",), ("E1",)], [("E1",)])
            self.tt(e3[32:64, :, :], q3[32:64, :, :], r3[32:64, :, 0:1].to_broadcast([32, 8, 128]), ALU.add,
                    [("QF",), ("R",), ("E1",)], [("E1",)])
            self.act(QF[64:128, :], E1[0:64, :], AF.Exp, [("E1",)], [("QF",)])
            Tm = ACS[0:64, 0:8]
            self.copy(ACS[0:32, 0:8].unsqueeze(2), r3[0:32, :, 127:128], [("R",), ("ACS",), ("E1",)], [("ACS",)])
            self.copy(ACS[32:64, 0:8].unsqueeze(2), r3[32:64, :, 0:1], [("R",), ("ACS",)], [("ACS",)])
            Texp = DA[0:64, 0:512].rearrange("p (k c) -> p k c", k=64)
            self.tt(Texp, self.identF[0:64, 0:64].unsqueeze(2).to_broadcast([64, 64, 8]),
                    Tm.unsqueeze(1).to_broadcast([64, 64, 8]), ALU.mult, [("ACS",), ("identF",), ("DA",)], [("DA",)])
            pb = self.psum_alloc("rbc")
            self.mm(self.ps[:, pb, :], self.onesF[0:64, :], DA[0:64, 0:512], True, True, [("DA",), ("onesF",)],
                    [("ps", pb)])
            self.act(decbc.rearrange("p k c -> p (k c)"), self.ps[:, pb, :], AF.Exp, [("ps", pb)], [("decbc",)])
            for tile in range(8):
                tb_ = self.psum_alloc("tr")
                self.op("pe", lambda e, tb_=tb_, tile=tile: e.transpose(self.ps[:, tb_, 0:128],
                                                                      QF[:, tile * 128:(tile + 1) * 128],
                                                                      self.identF[:, :]),
                        [("QF",), ("identF",)], [("ps", tb_)])
                self.op("pe", lambda e, tb_=tb_, tile=tile: e.transpose(self.ps[:, tb_, 128:192],
                                                                      Rt[0:64, tile * 128:(tile + 1) * 128],
                                                                      self.identF[0:64, 0:64]),
                        [("R",), ("identF",)], [("ps", tb_)])
                self.copy(tok3[:, tile, :], self.ps[:, tb_, 0:192], [("ps", tb_)], [("tok3",)])
            hiT = E1[0:64, 0:512].bitcast(BF16)
            self.copy(hiT, Rt[0:64, :], [("R",)], [("E1",)])
            self.tt(DT[0:64, :], Rt[0:64, :], hiT, ALU.subtract, [("R",), ("E1",)], [("DT",)])
            self.copy(Rhm[0:64, 0, :], hiT, [("E1",)], [("R",)])
            self.copy(Rhm[0:64, 1, :], DT[0:64, :], [("DT",)], [("R",)])
            S.barrier()

            for g in range(8):
                cr = cvi[0] % 2
                cvi[0] += 1
                self.dma("sp", convp[:, cr, :], self.d_convp[j, g], [], [("convp", cr)], slot=("convp", cr))
                wtA, wkA = self.load_w(self.d_winA[j, g], KD, 512)
                wtB, wkB = self.load_w(self.d_winB[j, g], KD, 256)
                plan = [(wtA, wkA, 0, "z", 0), (wtA, wkA, 1, "z", 1), (wtA, wkA, 2, "x", 0), (wtA, wkA, 3, "x", 1),
                        (wtB, wkB, 0, "B", 0), (wtB, wkB, 1, "C", 0)]
                for (wt, wk, jc, kind, idx) in plan:
                    ub = self.psum_alloc("u2")
                    for tbl in range(2):
                        for kd in range(KD):
                            self.mm(self.ps[:, ub + tbl, :], wt[:, kd, jc * 128:(jc + 1) * 128],
                                    self.hT[:, kd, tbl * 512:(tbl + 1) * 512], kd == 0, kd == KD - 1,
                                    [wk, ("h", kd, tbl)], [("ps", ub), ("ps", ub + 1)])
                    u = self.ps[:, ub:ub + 2, :].rearrange("p b c -> p (b c)")
                    uk = [("ps", ub), ("ps", ub + 1)]
                    if kind == "z":
                        yv, _ = self.yT_view(2 * g + idx)
                        self.act(yv, u, AF.Silu, uk, self.ykeys(2 * g + idx, 0) + self.ykeys(2 * g + idx, 1))
                        continue
                    cidx = {"x": idx, "B": 2, "C": 3}[kind]
                    cp = convp[:, cr, cidx * 4:cidx * 4 + 4]
                    tbuf, tkeys = ctmp[cti[0] % 2]
                    cti[0] += 1
                    self.act(tbuf, u, AF.Identity, uk + [("convp", cr)], tkeys, bias=cp[:, 3:4], scale=cp[:, 1:2])
                    u3 = u.rearrange("p (s q) -> p s q", s=nseq)
                    t3 = tbuf.rearrange("p (s q) -> p s q", s=nseq)
                    self.stt(t3[:, :, 1:L], u3[:, :, 0:L - 1], cp[:, 0:1], t3[:, :, 1:L], ALU.mult, ALU.add,
                             uk + tkeys + [("convp", cr)], tkeys)
                    self.stt(t3[:, :, 0:L - 1], u3[:, :, 1:L], cp[:, 2:3], t3[:, :, 0:L - 1], ALU.mult, ALU.add,
                             uk + tkeys + [("convp", cr)], tkeys)
                    dst, dk = {"x": (xcT[:, idx, :], ("xcT", idx)), "B": (BT, ("BT",)), "C": (CT, ("CT",))}[kind]
                    self.act(dst, tbuf, AF.Silu, tkeys, [dk])
                for tile in range(8):
                    tb_ = self.psum_alloc("tr")
                    psb = self.ps[:, tb_, :].bitcast(BF16)
                    for q_, (src, sk) in enumerate(((xcT[:, 0, :], ("xcT", 0)), (xcT[:, 1, :], ("xcT", 1)),
                                                    (BT, ("BT",)))):
                        self.op("pe", lambda e, psb=psb, q_=q_, src=src, tile=tile: e.transpose(
                            psb[:, q_ * 128:(q_ + 1) * 128], src[:, tile * 128:(tile + 1) * 128], self.identB[:, :]),
                            [sk, ("identB",)], [("ps", tb_)])
                    self.copy(xbtok[:, tile, :], psb[:, 0:384], [("ps", tb_)], [("xbtok", tile)], eng="act")
                for sq_ in range(nseq):
                    nct = L // 128
                    ft = sq_ * nct
                    for d in range(2):
                        Sd = Sst[:, d, :]
                        if grp == 0:
                            self.dma("sp", Sd, self.d_state0[j, d, :, g * 256:(g + 1) * 256], [], [("S", d)],
                                     slot=("st0", d))
                        else:
                            self.op("dve", lambda e, Sd=Sd: e.memset(Sd, 0.0), [], [("S", d)])
                    seq_steps = [(ci_, d) for ci_ in range(nct) for d in range(2)]

                    def pre(ci_, d, ft=ft, nct=nct):
                        c = ci_ if d == 0 else nct - 1 - ci_
                        tile = ft + c
                        r = cnt["xs"] % 2
                        cnt["xs"] += 1
                        k0 = 64 + d * 32 + 4 * g
                        self.tt(xs[:, r, :].rearrange("p (a b) -> p a b", a=4),
                                xbtok[:, tile, 0:256].rearrange("p (a b) -> p a b", a=4),
                                tok3[:, tile, k0:k0 + 4].unsqueeze(2).to_broadcast([128, 4, 64]), ALU.mult,
                                [("xbtok", tile), ("tok3",)], [("xs", r)])
                        sb_ = self.psum_alloc("sch")
                        self.mm(self.ps[:, sb_, 0:256], xbtok[:, tile, 256:384], xs[:, r, :], True, True,
                                [("xbtok", tile), ("xs", r)], [("ps", sb_)])
                        return tile, sb_

                    def post(d, tile, sb_):
                        Sd = Sst[:, d, :]
                        self.copy(sin[:, tile, d, :], Sd, [("S", d)], [("sin", tile, d)], eng="act")
                        kd0 = d * 32 + 4 * g
                        self.tt(Sd.rearrange("p (a b) -> p a b", a=4), Sd.rearrange("p (a b) -> p a b", a=4),
                                decbc[:, kd0:kd0 + 4, tile:tile + 1].to_broadcast([128, 4, 64]), ALU.mult,
                                [("S", d), ("decbc",)], [("S", d)])
                        self.tt(Sd, Sd, self.ps[:, sb_, 0:256], ALU.add, [("S", d), ("ps", sb_)], [("S", d)])

                    pend = [pre(*seq_steps[0])]
                    for idx_, (ci_, d) in enumerate(seq_steps):
                        if idx_ + 1 < len(seq_steps):
                            pend.append(pre(*seq_steps[idx_ + 1]))
                        tile, sb_ = pend.pop(0)
                        post(d, tile, sb_)
                    if grp == 1:
                        for d in range(2):
                            self.dma("sp", self.d_stout[j, sq_, d, :, g * 256:(g + 1) * 256], Sst[:, d, :],
                                     [("S", d)], [], slot=("stout", d), final=True)
                units = [(tbl, hp, hh2) for tbl in range(2) for hp in range(2) for hh2 in range(2)]
                st1 = {}

                def stage_cb(tbl):
                    tsl = slice(tbl * 512, (tbl + 1) * 512)
                    cb = self.psum_alloc("cb")
                    for q_ in range(4):
                        tile = tbl * 4 + q_
                        self.mm(self.ps[:, cb, q_ * 128:(q_ + 1) * 128], BT[:, tile * 128:(tile + 1) * 128],
                                CT[:, tile * 128:(tile + 1) * 128], True, True, [("BT",), ("CT",)], [("ps", cb)])
                    for mi in range(2):
                        self.tt(cbLU[:, mi, :].rearrange("p (a b) -> p a b", a=4),
                                self.ps[:, cb, :].rearrange("p (a b) -> p a b", a=4),
                                self.maskLU[:, mi, :].unsqueeze(1).to_broadcast([128, 4, 128]), ALU.mult,
                                [("ps", cb), ("maskLU",)], [("cbLU", mi)])

                dirs = [(u, d) for u in units for d in range(2)]
                info = {}

                rbank = {}

                def stageP(i):
                    (tbl, hp, hh2), d = dirs[i]
                    tsl = slice(tbl * 512, (tbl + 1) * 512)
                    k = d * 32 + 4 * g + hp * 2 + hh2
                    rb = self.psum_alloc("rbc")
                    rbank[i] = rb
                    for r_ in range(2):
                        self.mm(self.ps[:, rb, :], self.identB[0:64, k:k + 1].to_broadcast([64, 128]),
                                Rhm[0:64, r_, tsl], r_ == 0, r_ == 1, [("R",), ("identB",)], [("ps", rb)])

                def stageA(i):
                    (tbl, hp, hh2), d = dirs[i]
                    tsl = slice(tbl * 512, (tbl + 1) * 512)
                    h = 4 * g + hp * 2 + hh2
                    k = d * 32 + h
                    rb = rbank[i]
                    er = cnt["ea"] % 2
                    cnt["ea"] += 1
                    self.act(eabc[:, er, :], self.ps[:, rb, :], AF.Exp, [("ps", rb)], [("eabc", er)])
                    xi = cnt["X"] % 2
                    cnt["X"] += 1
                    self.tt(Xm[:, xi, :].rearrange("p (a b) -> p a b", a=4),
                            self.ps[:, rb, :].rearrange("p (a b) -> p a b", a=4),
                            tok3[:, tbl * 4:tbl * 4 + 4, 128 + k:128 + k + 1].to_broadcast([128, 4, 128]),
                            ALU.min, [("ps", rb), ("tok3",)], [("X", xi)])
                    info[i] = (er, xi)

                def stageB(i):
                    (tbl, hp, hh2), d = dirs[i]
                    u = dirs[i][0]
                    tsl = slice(tbl * 512, (tbl + 1) * 512)
                    h = 4 * g + hp * 2 + hh2
                    k = d * 32 + h
                    er, xi = info[i]
                    ei = cnt["E"] % 2
                    cnt["E"] += 1
                    for q_ in range(4):
                        tile = tbl * 4 + q_
                        self.act(Ebuf[:, ei, q_ * 128:(q_ + 1) * 128], Xm[:, xi, q_ * 128:(q_ + 1) * 128],
                                 AF.Exp, [("X", xi), ("tok3",)], [("E", ei, q_)], bias=tok3[:, tile, k:k + 1])
                    co = cnt["co"] % 4
                    cnt["co"] += 1
                    self.tt(coff[:, co, :], CT[:, tsl], eabc[:, er, :], ALU.mult,
                            [("CT",), ("eabc", er)], [("coff", co)])
                    wi = cnt["W"] % 4
                    cnt["W"] += 1
                    self.tt(Wbuf[:, wi, :], Ebuf[:, ei, :], cbLU[:, d, :], ALU.mult,
                            [("E", ei, q_) for q_ in range(4)] + [("cbLU", d)], [("W", wi)])
                    wr, cr_ = st1.setdefault(u, ({}, {}))
                    wr[d] = wi
                    cr_[d] = co

                ybank = {}

                def stage2(u):
                    tbl, hp, hh2 = u
                    tsl = slice(tbl * 512, (tbl + 1) * 512)
                    hh = hp * 2 + hh2
                    po = hh2 * 64
                    wr, cr_ = st1[u]
                    if hh2 == 0:
                        ybank[(tbl, hp)] = self.psum_alloc("y")
                    yb = ybank[(tbl, hp)]
                    for q_ in range(4):
                        tile = tbl * 4 + q_
                        qs = slice(q_ * 128, (q_ + 1) * 128)
                        out = self.ps[po:po + 64, yb, qs]
                        xl = xbtok[:, tile, hh * 64:(hh + 1) * 64]
                        self.mm(out, xl, Wbuf[:, wr[0], qs], True, False,
                                [("xbtok", tile), ("W", wr[0])], [("ps", yb)])
                        self.mm(out, xl, Wbuf[:, wr[1], qs], False, False,
                                [("xbtok", tile), ("W", wr[1])], [("ps", yb)])
                        for d in range(2):
                            self.mm(out, sin[:, tile, d, hh * 64:(hh + 1) * 64], coff[:, cr_[d], qs],
                                    False, d == 1, [("sin", tile, d), ("coff", cr_[d])], [("ps", yb)])
                    if hh2 == 1:
                        yc = 2 * g + hp
                        self.stt(ysq, xcT[:, hp, tsl], dvec[:, yc:yc + 1], self.ps[:, yb, :],
                                 ALU.mult, ALU.add, [("xcT", hp), ("ssdv", 0), ("ps", yb)], [("sq", 0), ("sq", 1)])
                        yv, _ = self.yT_view(yc)
                        self.tt(yv[:, tsl], ysq, yv[:, tsl], ALU.mult,
                                [("sq", 0), ("sq", 1)] + self.ykeys(yc, tbl), self.ykeys(yc, tbl))

                stage_cb(0)
                AHEAD = 4
                for i in range(min(AHEAD, len(dirs))):
                    stageP(i)
                stageA(0)
                for i in range(len(dirs)):
                    if i + AHEAD < len(dirs):
                        stageP(i + AHEAD)
                    if i + 1 < len(dirs):
                        stageA(i + 1)
                    if i + 1 < len(dirs) and dirs[i + 1][0][0] != dirs[i][0][0] and dirs[i + 1][1] == 0:
                        pass
                    if dirs[i][1] == 0 and dirs[i][0][1] == 0 and dirs[i][0][2] == 0 and dirs[i][0][0] == 1:
                        stage_cb(1)
                    stageB(i)
                    if dirs[i][1] == 1:
                        stage2(dirs[i][0])
            S.barrier()
            for tbl in range(2):
                tsl = slice(tbl * 512, (tbl + 1) * 512)
                nb = self.psum_alloc("norm")
                for yc in range(16):
                    yv, _ = self.yT_view(yc)
                    sr = self.sqi % 2
                    self.sqi += 1
                    self.act(self.sq[:, sr, :], yv[:, tsl], AF.Square, self.ykeys(yc, tbl), [("sq", sr)])
                    self.mm(self.ps[:, nb, :], self.ones_bf[:, :], self.sq[:, sr, :], yc == 0, yc == 15,
                            [("sq", sr), ("ones",)], [("ps", nb)])
                r = self.uid % 2
                self.uid += 1
                rs = self.rstd[:, r, :]
                rk = ("rstd", r)
                self.act(rs, self.ps[:, nb, :], AF.Sqrt, [("ps", nb), ("eps",)], [rk], bias=self.epsc[:, 0:1],
                         scale=1.0 / 2048)
                self.op("dve", lambda e, rs=rs: e.reciprocal(rs, rs), [rk], [rk])
                for yc in range(16):
                    yv, _ = self.yT_view(yc)
                    self.stt(yv[:, tsl], yv[:, tsl], ngain[:, yc:yc + 1], rs, ALU.mult, ALU.mult,
                             self.ykeys(yc, tbl) + [rk, ("ssdv", 1)], self.ykeys(yc, tbl))
            for cbk in range(4):
                wt, wk = self.load_w(self.d_wout[j, cbk], 16, 256)
                for m in range(2):
                    oc = cbk * 2 + m
                    for tbl in range(2):
                        tsl = slice(tbl * 512, (tbl + 1) * 512)
                        pb = self.psum_alloc("rbc")
                        for yc in range(16):
                            yv, _ = self.yT_view(yc)
                            self.mm(self.ps[:, pb, :], wt[:, yc, m * 128:(m + 1) * 128], yv[:, tsl], yc == 0, yc == 15,
                                    [wk] + self.ykeys(yc, tbl), [("ps", pb)])
                        tb = grp * 2 + tbl
                        xs_ = self.xT[:, oc, tb * 512:(tb + 1) * 512]
                        self.stt(xs_, self.ps[:, pb, :], self.mod[:, l, 16 + oc, ci:ci + 1], xs_, ALU.mult, ALU.add,
                                 [("ps", pb), ("mod", l), ("xall",)], [("x", oc, tb)])
        S.barrier()

    def emit(self):
        nc = self.nc
        names = self.S.finalize()
        sems = {}
        for i, n in enumerate(names):
            sems[n] = self.es.enter_context(nc.semaphore("s%d" % i))
        S = self.S
        with nc.Block() as block:
            @block.tensor
            def _(e):
                S.replay("pe", e, sems)

            @block.scalar
            def _(e):
                S.replay("act", e, sems)

            @block.vector
            def _(e):
                S.replay("dve", e, sems)

            @block.gpsimd
            def _(e):
                S.replay("pool", e, sems)

            @block.sync
            def _(e):
                S.replay("sp", e, sems)
        self.es.close()
        return nc


def _chunk_vec(v):
    sh = v.shape
    c = sh[-1] // 128
    return np.ascontiguousarray(np.swapaxes(v.reshape(sh[:-1] + (c, 128)), -1, -2))


def _wblocks(w, cb):
    K, N = w.shape
    return np.ascontiguousarray(w.reshape(K // 128, 128, N // cb, cb).transpose(2, 1, 0, 3))


def prep_shared(inp, nl):
    sh = {}
    sh["ada_w"] = np.stack([_wblocks(inp["ada_w"][l], 512) for l in range(DEPTH)])
    sh["ada_b"] = np.stack([_chunk_vec(inp["ada_b"][l]) for l in range(DEPTH)])
    sh["gmix"] = np.stack([_chunk_vec(inp["norm_mix_g"][l]) for l in range(DEPTH)])
    sh["gffn"] = np.stack([_chunk_vec(inp["norm_ffn_g"][l]) for l in range(DEPTH)])
    sh["gfin"] = _chunk_vec(inp["final_norm_g"])
    idx = []
    for b in range(11):
        for part in (0, 1):
            for jj in (0, 1):
                j = 2 * b + jj
                idx.append(np.arange(part * DFF + j * 128, part * DFF + (j + 1) * 128))
    idx = np.concatenate(idx)
    sh["w_gu"] = np.stack([_wblocks(inp["ffn_w_gate_up"][l][:, idx], 512) for l in range(DEPTH)])
    sh["w_dn"] = np.stack([_wblocks(inp["ffn_w_down"][l], 256) for l in range(DEPTH)])
    sh["w_qkv"] = np.stack([_wblocks(inp["na_w_qkv"][j], 512) for j in range(2)])
    sh["w_o"] = np.stack([_wblocks(inp["na_w_o"][j], 512) for j in range(2)])
    a = np.arange(2)[:, None, None, None]
    kc = np.arange(64)[None, :, None, None]
    mm = np.arange(14)[None, None, :, None]
    c = np.arange(64)[None, None, None, :]
    dr = a - mm + 6 + 0 * kc + 0 * c
    dc = np.clip(kc - c + 15, 0, 30) + 0 * a + 0 * mm
    c0 = np.clip(c - 8, 0, 48)
    colok = (kc >= c0) & (kc < c0 + 16) & (a >= 0) & (mm >= 0)
    okf = colok
    oki = colok & (dr >= -4) & (dr <= 3)
    rpb = inp["na_rpb"]
    g = rpb[:, :, dr + 7, dc]
    neg = np.float32(-30000.0)
    bf = np.where(okf[None, None], g, neg).reshape(2, NH, 128, 896)
    bi_ = np.where(oki[None, None], g, neg).reshape(2, NH, 128, 896)
    sh["braw"] = np.ascontiguousarray(np.concatenate([bf, bi_], axis=-1).astype(np.float32))
    wA, wB, wdt, cvp, dtp, dvec, ngain, wout = [], [], [], [], [], [], [], []
    for j in range(2):
        w_in = inp["ssd_w_in"][j]
        a_, b_, c_ = [], [], []
        for g in range(8):
            colsA = np.concatenate([g * 256 + np.arange(256), 2048 + g * 256 + np.arange(256)])
            colsB = np.concatenate([4096 + g * 128 + np.arange(128), 5120 + g * 128 + np.arange(128)])
            a_.append(_wblocks(w_in[:, colsA], 512)[0])
            b_.append(_wblocks(w_in[:, colsB], 256)[0])
            chs = [g * 256 + np.arange(128), g * 256 + 128 + np.arange(128), 2048 + g * 128 + np.arange(128),
                   3072 + g * 128 + np.arange(128)]
            cp = np.zeros((128, 16), np.float32)
            for ci_, ch in enumerate(chs):
                cp[:, ci_ * 4:ci_ * 4 + 3] = inp["ssd_conv_w"][j][:, ch].T
                cp[:, ci_ * 4 + 3] = inp["ssd_conv_b"][j][ch]
            c_.append(cp)
        wA.append(np.stack(a_)); wB.append(np.stack(b_)); cvp.append(np.stack(c_))
        wdt.append(_wblocks(w_in[:, 6144:6208], 64)[0])
        dtp.append(np.stack([inp["ssd_dt_bias"][j].reshape(64), inp["ssd_a_log"][j].reshape(64)], axis=-1))
        dvec.append(_chunk_vec(np.repeat(inp["ssd_d"][j], 64)))
        ngain.append(_chunk_vec(inp["ssd_norm_g"][j]))
        wout.append(_wblocks(inp["ssd_w_out"][j], 256))
    sh["w_inA"] = np.stack(wA); sh["w_inB"] = np.stack(wB); sh["w_dt"] = np.stack(wdt)
    sh["convp"] = np.stack(cvp); sh["dtp"] = np.ascontiguousarray(np.stack(dtp).astype(np.float32))
    sh["dvec"] = np.stack(dvec); sh["ngain"] = np.stack(ngain); sh["w_out"] = np.stack(wout)
    ident = np.eye(128, dtype=np.float32)
    sh["consts"] = np.stack([ident, np.triu(np.ones((128, 128), np.float32)), np.tril(np.ones((128, 128), np.float32))])
    return sh


def prep_core(inp, core):
    b = core // 2
    xs = inp["x_sample"][b]
    xp = inp["x_prompt"][4 * core:4 * core + 4].reshape(4 * 256, D)
    xT = np.ascontiguousarray(np.concatenate([xs, xp], axis=0).T)
    cond = np.stack([inp["c"][b], inp["c_ctx"]], axis=-1)
    cond = np.ascontiguousarray(cond.reshape(KD, 128, 2).transpose(1, 0, 2))
    kctxT = np.ascontiguousarray(inp["cache_k"][b].transpose(0, 1, 3, 2).reshape(2, D, 512))
    vctx = np.ascontiguousarray(inp["cache_v"][b].transpose(0, 2, 1, 3).reshape(2, 512, D))
    state0 = np.ascontiguousarray(inp["state_ssm"][b].transpose(0, 1, 4, 2, 3).reshape(2, 2, 128, 2048))
    return {"xT": xT, "cond": cond, "kctxT": kctxT, "vctx": vctx, "state0": state0}


_CACHE = {}


def get_program(cfg):
    if cfg not in _CACHE:
        bld = Builder(*cfg)
        bld.build()
        _CACHE[cfg] = bld.emit()
    return _CACHE[cfg]


def kernel(**inputs):
    inp = {k: np.asarray(v) for k, v in inputs.items()}
    nl = _env_int("K_NL", DEPTH)
    cfg = (nl, bool(_env_int("K_NA", 1)), bool(_env_int("K_SSD", 1)), bool(_env_int("K_FFN", 1)))
    nc = get_program(cfg)
    shared = prep_shared(inp, nl)
    in_maps = []
    for c in range(NCORES):
        m = dict(shared)
        m.update(prep_core(inp, c))
        in_maps.append(m)
    res = run_bass_kernel_spmd(nc, in_maps, core_ids=list(range(NCORES)))
    R = res.results
    y_prompt = np.zeros((32, 256, D), np.float32)
    y_sample = np.zeros((4, 1024, D), np.float32)
    for c in range(NCORES):
        yT = R[c]["yT"]
        if c % 2 == 0:
            y_sample[c // 2] = yT[:, 0:1024].T
        y_prompt[4 * c:4 * c + 4] = yT[:, 1024:2048].T.reshape(4, 256, D)
    new_k = np.zeros((32, 2, NH, 256, 64), np.float32)
    new_v = np.zeros((32, 2, NH, 256, 64), np.float32)
    for c in range(NCORES):
        kT = R[c]["kT_out"]
        vv = R[c]["v_out"]
        new_k[4 * c:4 * c + 4] = kT.reshape(2, NH, 64, 4, 256).transpose(3, 0, 1, 4, 2)
        new_v[4 * c:4 * c + 4] = vv.reshape(2, 4, 256, NH, 64).transpose(1, 0, 3, 2, 4)
    new_s = np.zeros((32, 2, 2, 32, 64, 128), np.float32)
    for c in range(NCORES):
        st = R[c]["st_out"]
        new_s[4 * c:4 * c + 4] = st.reshape(2, 4, 2, 128, 32, 64).transpose(1, 0, 2, 4, 5, 3)
    return (y_prompt, y_sample, new_k, new_v, new_s)
```

```python
import os
import numpy as np
from contextlib import ExitStack
import concourse.bass as bass
import concourse.mybir as mybir
from concourse.bass_types import AP
from concourse.bass_utils import run_bass_kernel_spmd

F32 = mybir.dt.float32
BF16 = mybir.dt.bfloat16
AF = mybir.ActivationFunctionType
ALU = mybir.AluOpType

D = 1024
KD = 8
T = 2048
TG = 1024
DEPTH = 4
DFF = 2816
NH = 16
EPS = 1e-6
NCORES = 8
ARENA = 36608

ENGS = ["pe", "act", "dve", "pool", "sp"]
SAME_SYNC = {"pe": False, "act": True, "dve": True, "pool": False, "sp": False}


class Op:
    __slots__ = ("eng", "fn", "deps", "slot", "signal", "tok")

    def __init__(self, eng, fn, deps, slot):
        self.eng = eng
        self.fn = fn
        self.deps = deps
        self.slot = slot
        self.signal = False
        self.tok = None


class Sched:
    def __init__(self):
        self.ops = []
        self.lw = {}
        self.rd = {}
        self.final_slots = set()
        self.last_se = {}
        self.pending = {}

    def barrier(self):
        last = set(self.last_se.values())
        for e in ENGS:
            self.pending[e] = set(self.pending.get(e, ())) | last

    def add(self, eng, fn, reads=(), writes=(), slot=None, final=False, strict=True):
        i = len(self.ops)
        psr = [k for k in reads if isinstance(k, tuple) and k[0] == "ps"]
        if psr:
            reads = [k for k in reads if not (isinstance(k, tuple) and k[0] == "ps")]
            writes = list(writes) + psr
        deps = set(self.pending.pop(eng, ())) if strict else set()
        for k in reads:
            if k in self.lw:
                deps.add(self.lw[k])
        weak = set()
        for k in writes:
            if k in self.lw:
                weak.add(self.lw[k])
            r = self.rd.get(k)
            if r:
                weak.update(r.values())
        deps.update(weak)
        se = slot if slot is not None else eng
        self.last_se[se] = i
        for k in reads:
            self.rd.setdefault(k, {})[se] = i
        for k in writes:
            self.lw[k] = i
            self.rd[k] = {}
        deps.discard(i)
        self.ops.append(Op(eng, fn, deps, slot))
        if final:
            self.final_slots.add(slot)
        return i

    def barrier_keys(self, keys):
        pass

    def finalize(self):
        ops = self.ops
        for op in ops:
            for d in op.deps:
                dop = ops[d]
                if dop.slot is None and (dop.eng != op.eng or SAME_SYNC[op.eng]):
                    dop.signal = True
        cnt = {}
        for op in ops:
            if op.slot is not None:
                cnt[op.slot] = cnt.get(op.slot, 0) + 16
                op.tok = (op.slot, cnt[op.slot])
            elif op.signal:
                cnt[op.eng] = cnt.get(op.eng, 0) + 1
                op.tok = (op.eng, cnt[op.eng])
        self.cnt = cnt
        self.per_eng = {e: [] for e in ENGS}
        for op in ops:
            self.per_eng[op.eng].append(op)
        return sorted(cnt.keys(), key=str)

    def replay(self, engname, e, sems):
        ops = self.ops
        waited = {}
        for op in self.per_eng[engname]:
            need = {}
            for d in op.deps:
                dop = ops[d]
                if dop.tok is None:
                    continue
                if dop.slot is None and dop.eng == engname and not SAME_SYNC[engname]:
                    continue
                se, v = dop.tok
                if v > need.get(se, 0):
                    need[se] = v
            for se, v in need.items():
                if waited.get(se, 0) < v:
                    e.wait_ge(sems[se], v)
                    waited[se] = v
            ins = op.fn(e)
            if op.tok is not None:
                ins.then_inc(sems[op.tok[0]], 16 if op.slot is not None else 1)
        if engname == "sp":
            for se in sorted(self.final_slots, key=str):
                e.wait_ge(sems[se], self.cnt[se])


def _env_int(name, default):
    v = os.environ.get(name)
    return default if v is None else int(v)


class Builder:
    def __init__(self, nlayers=DEPTH, do_na=True, do_ssd=True, do_ffn=True):
        self.nl = nlayers
        self.do_na = do_na
        self.do_ssd = do_ssd
        self.do_ffn = do_ffn
        self.nc = bass.Bass("TRN2", target_bir_lowering=False)
        self.S = Sched()
        self.es = ExitStack()
        self.wslot = 0
        self.stg = {}
        self.uid = 0
        self.sqi = 0

    def dram_in(self, name, shape):
        return self.nc.dram_tensor(name, list(shape), F32, kind="ExternalInput").ap()

    def dram_out(self, name, shape):
        return self.nc.dram_tensor(name, list(shape), F32, kind="ExternalOutput").ap()

    def sb(self, name, shape, dt):
        return self.es.enter_context(self.nc.sbuf_tensor(name, list(shape), dt))

    def op(self, eng, fn, reads=(), writes=(), slot=None, final=False, strict=True):
        return self.S.add(eng, fn, reads, writes, slot, final, strict)

    def mm(self, out, lhsT, rhs, start, stop, reads, writes):
        self.op("pe", lambda e: e.matmul(out, lhsT, rhs, start=start, stop=stop), reads, writes)

    def act(self, out, in_, func, reads, writes, bias=None, scale=None):
        kw = {}
        if bias is not None:
            kw["bias"] = bias
        if scale is not None:
            kw["scale"] = scale
        self.op("act", lambda e: e.activation(out=out, in_=in_, func=func, **kw), reads, writes)

    def tt(self, out, in0, in1, op, reads, writes, eng="dve"):
        self.op(eng, lambda e: e.tensor_tensor(out=out, in0=in0, in1=in1, op=op), reads, writes)

    def ts(self, out, in0, s1, op0, reads, writes, s2=None, op1=None, eng="dve"):
        if op1 is None:
            self.op(eng, lambda e: e.tensor_scalar(out, in0, s1, scalar2=None, op0=op0), reads, writes)
        else:
            self.op(eng, lambda e: e.tensor_scalar(out, in0, s1, scalar2=s2, op0=op0, op1=op1), reads, writes)

    def stt(self, out, in0, scalar, in1, op0, op1, reads, writes):
        self.op("dve", lambda e: e.scalar_tensor_tensor(out=out, in0=in0, scalar=scalar, in1=in1, op0=op0, op1=op1),
                reads, writes)

    def copy(self, out, in_, reads, writes, eng="dve"):
        if eng == "act":
            self.op("act", lambda e: e.copy(out, in_), reads, writes)
        else:
            self.op(eng, lambda e: e.tensor_copy(out, in_), reads, writes)

    def dma(self, q, out, in_, reads, writes, slot, final=False, strict=True, **kw):
        self.op(q, lambda e: e.dma_start(out=out, in_=in_, **kw), reads, writes, slot=slot, final=final,
                strict=strict)

    def pbank(self, i):
        return self.ps[:, i, :]

    def psum_alloc(self, pool):
        lst = self.pools[pool]
        k = self.pool_idx.get(pool, 0)
        self.pool_idx[pool] = k + 1
        return lst[k % len(lst)]

    def load_w(self, src, kd, cb):
        s = self.wslot % self.NW
        self.wslot += 1
        view = self.wring[:, s, 0:kd * cb].rearrange("p (k c) -> p k c", k=kd)
        self.dma("pool", view, src, [], [("w", s)], slot=("wsem", s), strict=False)
        return view, ("w", s)

    def build(self):
        nc = self.nc
        nl = self.nl
        self.d_xT = self.dram_in("xT", [D, T])
        self.d_cond = self.dram_in("cond", [128, KD, 2])
        self.d_adaw = self.dram_in("ada_w", [DEPTH, 12, 128, KD, 512])
        self.d_adab = self.dram_in("ada_b", [DEPTH, 128, 48])
        self.d_gmix = self.dram_in("gmix", [DEPTH, 128, KD])
        self.d_gffn = self.dram_in("gffn", [DEPTH, 128, KD])
        self.d_gfin = self.dram_in("gfin", [128, KD])
        self.d_wgu = self.dram_in("w_gu", [DEPTH, 11, 128, KD, 512])
        self.d_wdn = self.dram_in("w_dn", [DEPTH, 4, 128, 22, 256])
        self.d_yT = self.dram_out("yT", [D, T])
        self.d_wqkv = self.dram_in("w_qkv", [2, 6, 128, KD, 512])
        self.d_wo = self.dram_in("w_o", [2, 2, 128, KD, 512])
        self.d_braw = self.dram_in("braw", [2, NH, 128, 1792])
        self.d_kctx = self.dram_in("kctxT", [2, D, 512])
        self.d_vctx = self.dram_in("vctx", [2, 512, D])
        self.d_kout = self.dram_out("kT_out", [2, D, 1024])
        self.d_winA = self.dram_in("w_inA", [2, 8, 128, KD, 512])
        self.d_winB = self.dram_in("w_inB", [2, 8, 128, KD, 256])
        self.d_wdt = self.dram_in("w_dt", [2, 128, KD, 64])
        self.d_convp = self.dram_in("convp", [2, 8, 128, 16])
        self.d_dtp = self.dram_in("dtp", [2, 64, 2])
        self.d_dvec = self.dram_in("dvec", [2, 128, 16])
        self.d_ngain = self.dram_in("ngain", [2, 128, 16])
        self.d_wout = self.dram_in("w_out", [2, 4, 128, 16, 256])
        self.d_state0 = self.dram_in("state0", [2, 2, 128, 2048])
        self.d_consts = self.dram_in("consts", [3, 128, 128])
        self.d_stout = self.dram_out("st_out", [2, 4, 2, 128, 2048])
        self.d_vout = self.dram_out("v_out", [2, 1024, D])

        self.xT = self.sb("xT_sb", [128, KD, T], F32)
        self.hT = self.sb("hT_sb", [128, KD, T], BF16)
        self.NW = 3
        self.wring = self.sb("wring", [128, self.NW, 4096], BF16)
        self.arena = self.sb("arena", [128, ARENA], BF16)
        self.mod = self.sb("mod", [128, DEPTH, 48, 2], F32)
        self.Amix = self.sb("Amix", [128, DEPTH, KD, 2], F32)
        self.Affn = self.sb("Affn", [128, DEPTH, KD, 2], F32)
        self.gmix = self.sb("gmix_sb", [128, DEPTH, KD], F32)
        self.gffn = self.sb("gffn_sb", [128, DEPTH, KD], F32)
        self.gfin = self.sb("gfin_sb", [128, KD], F32)
        self.adab = self.sb("adab_sb", [128, DEPTH, 48], F32)
        self.cond = self.sb("cond_sb", [128, KD, 2], F32)
        self.scond = self.sb("scond_sb", [128, KD, 2], BF16)
        self.ones_bf = self.sb("ones_bf", [128, 128], BF16)
        self.epsc = self.sb("epsc", [128, 1], F32)
        self.zero8 = self.sb("zero8", [128, KD], F32)
        self.sq = self.sb("sq", [128, 2, 512], BF16)
        self.rstd = self.sb("rstd", [128, 2, 512], F32)
        self.tmpn = self.sb("tmpn", [128, 2, 512], F32)
        self.identF = self.sb("identF", [128, 128], F32)
        self.identB = self.sb("identB", [128, 128], BF16)
        self.maskLU = self.sb("maskLU", [128, 2, 128], BF16)
        self.onesF = self.sb("onesF", [128, 128], F32)
        self.onec = self.sb("onec", [128, 1], F32)
        self.ps = self.es.enter_context(nc.psum_tensor("ps", [128, 8, 512], F32))
        self.pools = {}
        self.pool_idx = {}

        self.dma("sp", self.xT[:, :, :], self.d_xT.rearrange("(k p) t -> p k t", p=128), [], [("xall",)], slot="xin")
        self.dma("sp", self.cond[:, :, :], self.d_cond, [], [("cond",)], slot=("cst", 0))
        self.dma("sp", self.adab[:, :, :], self.d_adab.rearrange("l p c -> p l c"), [], [("adab",)], slot=("cst", 1))
        self.dma("sp", self.gmix[:, :, :], self.d_gmix.rearrange("l p c -> p l c"), [], [("gmix",)], slot=("cst", 2))
        self.dma("sp", self.gffn[:, :, :], self.d_gffn.rearrange("l p c -> p l c"), [], [("gffn",)], slot=("cst", 3))
        self.dma("sp", self.gfin[:, :], self.d_gfin, [], [("gfin",)], slot=("cst", 4))
        self.op("dve", lambda e: e.memset(self.ones_bf[:, :], 1.0), [], [("ones",)])
        self.op("dve", lambda e: e.memset(self.epsc[:, :], EPS), [], [("eps",)])
        self.op("dve", lambda e: e.memset(self.zero8[:, :], 0.0), [], [("zero8",)])
        self.op("dve", lambda e: e.memset(self.onesF[:, :], 1.0), [], [("onesF",)])
        self.op("dve", lambda e: e.memset(self.onec[:, :], 1.0), [], [("onec",)])
        self.dma("sp", self.identF[:, :], self.d_consts[0], [], [("identF",)], slot=("cst", 5))
        self.copy(self.identB[:, :], self.identF[:, :], [("identF",)], [("identB",)])
        self.dma("pool", self.maskLU[:, :, :], self.d_consts[1:3].rearrange("m p c -> p m c"), [], [("maskLU",)],
                 slot="mlu")
        self.act(self.scond[:, :, :], self.cond[:, :, :], AF.Silu, [("cond",)], [("scond",)])
        self.pools = {"ada": [0, 1]}
        for l in range(nl):
            b = self.psum_alloc("ada")
            pk = ("ps", b)
            pt = self.ps[:, b, 0:96]
            for cb in range(12):
                wt, wk = self.load_w(self.d_adaw[l, cb], KD, 512)
                for j in range(4):
                    oc = cb * 4 + j
                    for kd in range(KD):
                        self.mm(pt[:, oc * 2:oc * 2 + 2], wt[:, kd, j * 128:(j + 1) * 128], self.scond[:, kd, :],
                                kd == 0, kd == KD - 1, [wk, ("scond",)], [pk])
            self.tt(self.mod[:, l, :, :], pt.rearrange("p (c i) -> p c i", i=2),
                    self.adab[:, l, :].unsqueeze(2).to_broadcast([128, 48, 2]), ALU.add,
                    [pk, ("adab",)], [("mod", l)])
            for (A, g, gk, part) in ((self.Amix, self.gmix, ("gmix",), 1), (self.Affn, self.gffn, ("gffn",), 4)):
                self.stt(A[:, l, :, :], self.mod[:, l, part * 8:(part + 1) * 8, :], 1.0,
                         g[:, l, :].unsqueeze(2).to_broadcast([128, KD, 2]), ALU.add, ALU.mult,
                         [("mod", l), gk], [("A", l, part)])

        for l in range(nl):
            if l % 2 == 0:
                if self.do_na:
                    self.na_layer(l)
            else:
                if self.do_ssd:
                    self.ssd_layer(l)
            if self.do_ffn:
                self.ffn(l)
        self.final()

    def xkeys(self, tb):
        return [("x", kd, tb) for kd in range(KD)]

    def norm_mod(self, tb, A_of_kd, B_of_kd, pkeys, out_fn, out_keys_fn, pool="norm"):
        t0 = tb * 512
        xk = self.xkeys(tb) + [("xall",)]
        b = self.psum_alloc(pool)
        pk = ("ps", b)
        for kd in range(KD):
            sr = self.sqi % 2
            self.sqi += 1
            self.act(self.sq[:, sr, :], self.xT[:, kd, t0:t0 + 512], AF.Square, [("x", kd, tb), ("xall",)],
                     [("sq", sr)])
            self.mm(self.ps[:, b, :], self.ones_bf[:, :], self.sq[:, sr, :], kd == 0, kd == KD - 1,
                    [("sq", sr), ("ones",)], [pk])
        r = self.uid % 2
        self.uid += 1
        rs = self.rstd[:, r, :]
        rk = ("rstd", r)
        self.act(rs, self.ps[:, b, :], AF.Sqrt, [pk, ("eps",)], [rk], bias=self.epsc[:, 0:1], scale=1.0 / D)
        self.op("dve", lambda e: e.reciprocal(rs, rs), [rk], [rk])
        for kd in range(KD):
            q = kd % 2
            tk = ("tmpn", q)
            self.stt(self.tmpn[:, q, :], self.xT[:, kd, t0:t0 + 512], A_of_kd(kd), rs, ALU.mult, ALU.mult,
                     [("x", kd, tb), ("xall",), rk] + pkeys, [tk])
            self.act(out_fn(kd), self.tmpn[:, q, :], AF.Identity, [tk] + pkeys, out_keys_fn(kd), bias=B_of_kd(kd))

    def ffn(self, l):
        self.S.barrier()
        self.pools.update({"norm": [6], "g": [0, 1], "u": [2, 3], "dn": [4, 5]})
        act_v = self.arena[:, 0:12 * T].rearrange("p (j t) -> p j t", j=12)
        sg = self.arena[:, 12 * T:12 * T + 4 * 512].bitcast(F32).rearrange("p (r t) -> p r t", r=2)
        for tb in range(4):
            ci = 0 if tb < 2 else 1
            self.norm_mod(tb,
                          lambda kd: self.Affn[:, l, kd, ci:ci + 1],
                          lambda kd: self.mod[:, l, 24 + kd, ci:ci + 1],
                          [("A", l, 4), ("mod", l)],
                          lambda kd: self.hT[:, kd, tb * 512:(tb + 1) * 512],
                          lambda kd: [("h", kd, tb)])
        halves = [(0, 6), (6, 11)]
        sgi = 0
        for (b0, b1) in halves:
            nj = (b1 - b0) * 2
            for bi in range(b0, b1):
                wt, wk = self.load_w(self.d_wgu[l, bi], KD, 512)
                for jj in range(2):
                    jl = (bi - b0) * 2 + jj
                    for tb in range(4):
                        bg = self.psum_alloc("g")
                        bu = self.psum_alloc("u")
                        for kd in range(KD):
                            self.mm(self.ps[:, bg, :], wt[:, kd, jj * 128:(jj + 1) * 128],
                                    self.hT[:, kd, tb * 512:(tb + 1) * 512], kd == 0, kd == KD - 1,
                                    [wk, ("h", kd, tb)], [("ps", bg)])
                        for kd in range(KD):
                            self.mm(self.ps[:, bu, :], wt[:, kd, 256 + jj * 128:256 + (jj + 1) * 128],
                                    self.hT[:, kd, tb * 512:(tb + 1) * 512], kd == 0, kd == KD - 1,
                                    [wk, ("h", kd, tb)], [("ps", bu)])
                        r = sgi % 2
                        sgi += 1
                        self.act(sg[:, r, :], self.ps[:, bg, :], AF.Silu, [("ps", bg)], [("sg", r)])
                        self.tt(act_v[:, jl, tb * 512:(tb + 1) * 512], sg[:, r, :], self.ps[:, bu, :], ALU.mult,
                                [("sg", r), ("ps", bu)], [("act", jl, tb)])
            k0 = b0 * 2
            for cbk in range(4):
                wt, wk = self.load_w(self.d_wdn[l, cbk, :, k0:k0 + nj, :], nj, 256)
                for m in range(2):
                    oc = cbk * 2 + m
                    for tb in range(4):
                        ci = 0 if tb < 2 else 1
                        bd = self.psum_alloc("dn")
                        for j in range(nj):
                            self.mm(self.ps[:, bd, :], wt[:, j, m * 128:(m + 1) * 128],
                                    act_v[:, j, tb * 512:(tb + 1) * 512], j == 0, j == nj - 1,
                                    [wk, ("act", j, tb)], [("ps", bd)])
                        xs = self.xT[:, oc, tb * 512:(tb + 1) * 512]
                        self.stt(xs, self.ps[:, bd, :], self.mod[:, l, 40 + oc, ci:ci + 1], xs, ALU.mult, ALU.add,
                                 [("ps", bd), ("mod", l), ("xall",)], [("x", oc, tb)])

    def final(self):
        self.S.barrier()
        self.pools.update({"norm": [6]})
        stg = self.arena[:, 0:2 * 2 * 512].bitcast(F32).rearrange("p (r t) -> p r t", r=2) if False else None
        ybuf = self.arena[:, 0:4 * 2 * 512].bitcast(F32).rearrange("p (r t) -> p r t", r=4)
        cnt = [0]
        for tb in range(4):
            def out_fn(kd, tb=tb):
                return ybuf[:, (tb * KD + kd) % 4, :]

            def keys_fn(kd, tb=tb):
                return [("ybuf", (tb * KD + kd) % 4)]
            t0 = tb * 512
            xk = self.xkeys(tb) + [("xall",)]
            b = self.psum_alloc("norm")
            pk = ("ps", b)
            for kd in range(KD):
                sr = self.sqi % 2
                self.sqi += 1
                self.act(self.sq[:, sr, :], self.xT[:, kd, t0:t0 + 512], AF.Square, [("x", kd, tb), ("xall",)],
                         [("sq", sr)])
                self.mm(self.ps[:, b, :], self.ones_bf[:, :], self.sq[:, sr, :], kd == 0, kd == KD - 1,
                        [("sq", sr), ("ones",)], [pk])
            r = self.uid % 2
            self.uid += 1
            rs = self.rstd[:, r, :]
            rk = ("rstd", r)
            self.act(rs, self.ps[:, b, :], AF.Sqrt, [pk, ("eps",)], [rk], bias=self.epsc[:, 0:1], scale=1.0 / D)
            self.op("dve", lambda e, rs=rs: e.reciprocal(rs, rs), [rk], [rk])
            for kd in range(KD):
                yi = (tb * KD + kd) % 4
                self.stt(ybuf[:, yi, :], self.xT[:, kd, t0:t0 + 512], self.gfin[:, kd:kd + 1], rs, ALU.mult, ALU.mult,
                         [("x", kd, tb), ("xall",), rk, ("gfin",)], [("ybuf", yi)])
                self.dma("sp", self.d_yT[kd * 128:(kd + 1) * 128, t0:t0 + 512], ybuf[:, yi, :],
                         [("ybuf", yi)], [], slot=("yout", yi), final=True)


    def pv(self, ab, cols, tile, h, rhs, start, stop, reads):
        lo, hi = slice(0, 64), slice(64, 128)
        nr, dr = (lo, hi) if h % 2 == 0 else (hi, lo)
        self.mm(self.ps[nr, ab, cols], self.vt[:, tile, h * 64:(h + 1) * 64], rhs, start, stop, reads, [("ps", ab)])
        self.mm(self.ps[dr, ab, cols], self.ones_bf[:, 0:64], rhs, start, stop, reads + [("ones",)], [("ps", ab)])

    def attn_norm(self, h, ab, tbl):
        c = h // 2
        if h % 2 == 0:
            nr, dr = slice(0, 64), slice(64, 128)
        else:
            nr, dr = slice(64, 128), slice(0, 64)
        r = self.rdi % 2
        self.rdi += 1
        rd = self.rden[:, r, :]
        self.act(rd[dr, :], self.ps[dr, ab, :], AF.Ln, [("ps", ab)], [("rstd", r)])
        self.act(rd[dr, :], rd[dr, :], AF.Exp, [("rstd", r)], [("rstd", r)], scale=-1.0)
        self.tt(self.hT[nr, c, tbl * 512:(tbl + 1) * 512], self.ps[nr, ab, :], rd[dr, :], ALU.mult,
                [("ps", ab), ("rstd", r)], [("h", c, tbl)])

    def na_layer(self, l):
        j = l // 2
        S = self.S
        S.barrier()
        A = self.arena
        kT = A[:, 0:8192].rearrange("p (k t) -> p k t", k=8)
        vt = A[:, 8192:8192 + 12 * 1024].rearrange("p (t c) -> p t c", t=12)
        self.vt = vt
        kctx = A[:, 20480:20480 + 4096].rearrange("p (k t) -> p k t", k=8)
        ostg = A[:, 20480:20480 + 4096].bitcast(F32).rearrange("p (r t) -> p r t", r=4)
        bands = A[:, 24576:24576 + 3584].rearrange("p (r c) -> p r c", r=2)
        brb = A[:, 28160:28160 + 3584].rearrange("p (r c) -> p r c", r=2)
        PT = A[:, 31744:31744 + 1280].rearrange("p (r c) -> p r c", r=2)
        self.rden = self.rstd
        self.rdi = 0
        self.pools.update({"norm": [0], "proj": [0, 1], "st": [2, 3], "st2": [2, 4], "acc": [6, 7]})
        oi = [0]
        pti = [0]

        def out_store(psb, dst):
            r = oi[0] % 4
            oi[0] += 1
            self.copy(ostg[:, r, :], self.ps[:, psb, :], [("ps", psb)], [("ostg", r)], eng="act")
            self.dma("sp", dst, ostg[:, r, :], [("ostg", r)], [], slot=("kvout", r), final=True)

        for grp in (0, 1):
            ci = grp
            if grp == 1:
                S.barrier()
            for tbl in range(2):
                self.norm_mod(grp * 2 + tbl,
                              lambda kd: self.Amix[:, l, kd, ci:ci + 1],
                              lambda kd: self.mod[:, l, kd, ci:ci + 1],
                              [("A", l, 1), ("mod", l)],
                              lambda kd, tbl=tbl: self.hT[:, kd, tbl * 512:(tbl + 1) * 512],
                              lambda kd, tbl=tbl: [("h", kd, tbl)])
            if grp == 0:
                self.dma("pool", kctx[:, :, :], self.d_kctx[j].rearrange("(k p) t -> p k t", p=128), [], [("kctx",)],
                         slot="kctx")
                for kb in range(4):
                    self.dma("pool", vt[:, 8 + kb, :], self.d_vctx[j, kb * 128:(kb + 1) * 128, :], [],
                             [("v", 8 + kb)], slot=("vctx", kb))
            for bi in range(4):
                wt, wk = self.load_w(self.d_wqkv[j, bi], KD, 512)
                for jj in range(4):
                    oc = (bi % 2) * 4 + jj
                    for tbl in range(2):
                        pb = self.psum_alloc("proj")
                        for kd in range(KD):
                            self.mm(self.ps[:, pb, :], wt[:, kd, jj * 128:(jj + 1) * 128],
                                    self.hT[:, kd, tbl * 512:(tbl + 1) * 512], kd == 0, kd == KD - 1,
                                    [wk, ("h", kd, tbl)], [("ps", pb)])
                        if bi < 2:
                            self.copy(self.hT[:, oc, 1024 + tbl * 512:1024 + (tbl + 1) * 512], self.ps[:, pb, :],
                                      [("ps", pb)], [("h", oc, 2 + tbl)], eng="act")
                        else:
                            self.copy(kT[:, oc, tbl * 512:(tbl + 1) * 512], self.ps[:, pb, :],
                                      [("ps", pb)], [("kT", oc, tbl)], eng="dve")
                            if grp == 1:
                                out_store(pb, self.d_kout[j, oc * 128:(oc + 1) * 128, tbl * 512:(tbl + 1) * 512])
            for bi in range(4, 6):
                wt, wk = self.load_w(self.d_wqkv[j, bi], KD, 512)
                for tile in range(8):
                    pb = self.psum_alloc("proj")
                    for kd in range(KD):
                        self.mm(self.ps[:, pb, :], self.hT[:, kd, tile * 128:(tile + 1) * 128], wt[:, kd, :],
                                kd == 0, kd == KD - 1, [wk, ("h", kd, tile // 4)], [("ps", pb)])
                    c0 = (bi - 4) * 512
                    self.copy(vt[:, tile, c0:c0 + 512], self.ps[:, pb, :], [("ps", pb)], [("v", tile, bi)], eng="dve")
                    if grp == 1:
                        out_store(pb, self.d_vout[j, tile * 128:(tile + 1) * 128, (bi - 4) * 512:(bi - 3) * 512])

            def vkeys(tile, h):
                if tile >= 8:
                    return [("v", tile)]
                return [("v", tile, 4 + h // 8)]

            units3 = [2, 4, 0]
            ui = [0]
            steps = []

            def next_unit():
                u_ = units3[ui[0] % 3]
                ui[0] += 1
                return u_

            if grp == 1:
                for sp in range(2):
                    for h in range(NH):
                        c, base = h // 2, (h % 2) * 64
                        stt_ = {}
                        for sq_ in range(2):
                            sidx = sp * 2 + sq_

                            def S_(h=h, c=c, base=base, sidx=sidx, sq_=sq_, stt_=stt_):
                                if sq_ == 0:
                                    stt_["ab"] = self.psum_alloc("acc")
                                st = next_unit()
                                stt_[("st", sq_)] = st
                                for kb in range(2):
                                    k0 = sidx * 256 + kb * 128
                                    self.mm(self.ps[:, st, kb * 256:(kb + 1) * 256], kT[base:base + 64, c, k0:k0 + 128],
                                            self.hT[base:base + 64, c, 1024 + sidx * 256:1024 + (sidx + 1) * 256],
                                            True, True, [("kT", c, sidx // 2), ("h", c, 2 + sidx // 2)], [("ps", st)])

                            def P_(sq_=sq_, stt_=stt_):
                                st = stt_[("st", sq_)]
                                r = pti[0] % 2
                                pti[0] += 1
                                stt_[("r", sq_)] = r
                                self.act(PT[:, r, 0:512], self.ps[:, st, :], AF.Exp, [("ps", st)], [("PT", r)],
                                         scale=0.125)

                            def V_(h=h, sidx=sidx, sq_=sq_, sp=sp, stt_=stt_):
                                ab, r = stt_["ab"], stt_[("r", sq_)]
                                for kb in range(2):
                                    tile = sidx * 2 + kb
                                    self.pv(ab, slice(sq_ * 256, (sq_ + 1) * 256), tile, h,
                                            PT[:, r, kb * 256:(kb + 1) * 256], kb == 0, kb == 1,
                                            [("PT", r)] + vkeys(tile, h))
                                if sq_ == 1:
                                    self.attn_norm(h, ab, sp)
                            steps.append((S_, P_, V_))
            else:
                qtiles = {0: [0, 1, 2, 3], 1: [0, 1, 2, 3], 2: [0, 1, 2, 3, 4], 3: [1, 2, 3, 4, 5],
                          4: [2, 3, 4, 5, 6], 5: [3, 4, 5, 6, 7], 6: [4, 5, 6, 7], 7: [4, 5, 6, 7]}
                for h in range(NH):
                    c, base = h // 2, (h % 2) * 64
                    br = h % 2
                    for qb in range(2):
                        stt_ = {}
                        for kb in range(4):
                            def S_(h=h, c=c, base=base, br=br, qb=qb, kb=kb, stt_=stt_):
                                if qb == 0 and kb == 0:
                                    self.dma("pool", brb[:, br, :], self.d_braw[j, h], [], [("brb", br)],
                                             slot=("brb", br))
                                    self.act(bands[:, br, :], brb[:, br, :], AF.Exp, [("brb", br)], [("bands", br)])
                                if kb == 0:
                                    stt_["ab"] = self.psum_alloc("acc")
                                st = next_unit()
                                stt_[("st", kb)] = st
                                self.mm(self.ps[:, st, :], kctx[base:base + 64, c, kb * 128:(kb + 1) * 128],
                                        self.hT[base:base + 64, c, 1024 + qb * 512:1024 + (qb + 1) * 512], True, True,
                                        [("kctx",), ("h", c, 2 + qb)], [("ps", st)])

                            def P_(kb=kb, stt_=stt_):
                                st = stt_[("st", kb)]
                                r = pti[0] % 2
                                pti[0] += 1
                                stt_[("r", kb)] = r
                                self.act(PT[:, r, 0:512], self.ps[:, st, :], AF.Exp, [("ps", st)], [("PT", r)],
                                         scale=0.125)

                            def V_(h=h, kb=kb, stt_=stt_):
                                ab, r = stt_["ab"], stt_[("r", kb)]
                                self.pv(ab, slice(0, 512), 8 + kb, h, PT[:, r, 0:512], kb == 0, False,
                                        [("PT", r)] + vkeys(8 + kb, h))
                            steps.append((S_, P_, V_))
                        for qi in range(qb * 4, qb * 4 + 4):
                            kjs = qtiles[qi][::-1]
                            nk = len(kjs)

                            def S_(c=c, base=base, qb=qb, qi=qi, kjs=kjs, stt_=stt_):
                                ub = next_unit()
                                stt_[("ub", qi)] = ub
                                for jx, kj in enumerate(kjs):
                                    bb, cc = (ub, jx * 128) if jx < 4 else (ub + 1, 0)
                                    self.mm(self.ps[:, bb, cc:cc + 128], kT[base:base + 64, c, kj * 128:(kj + 1) * 128],
                                            self.hT[base:base + 64, c, 1024 + qi * 128:1024 + (qi + 1) * 128],
                                            True, True, [("kT", c, kj // 4), ("h", c, 2 + qb)],
                                            [("ps", ub), ("ps", ub + 1)])

                            def P_(br=br, qi=qi, kjs=kjs, nk=nk, stt_=stt_):
                                ub = stt_[("ub", qi)]
                                r = pti[0] % 2
                                pti[0] += 1
                                stt_[("r", qi)] = r
                                src = self.ps[:, ub:ub + 2, :].rearrange("p b c -> p (b c)")[:, 0:nk * 128]
                                self.act(PT[:, r, 0:nk * 128], src, AF.Exp, [("ps", ub), ("ps", ub + 1)], [("PT", r)],
                                         scale=0.125)
                                var = 1 if qi in (2, 3, 4, 5) else 0
                                m0 = 6 - 2 * (kjs[0] - qi)
                                bsl = bands[:, br, var * 896 + m0 * 64:var * 896 + m0 * 64 + nk * 128]
                                self.tt(PT[:, r, 0:nk * 128], PT[:, r, 0:nk * 128], bsl, ALU.mult,
                                        [("PT", r), ("bands", br)], [("PT", r)])

                            def V_(h=h, qb=qb, qi=qi, kjs=kjs, nk=nk, stt_=stt_):
                                ab, r = stt_["ab"], stt_[("r", qi)]
                                for jx, kj in enumerate(kjs):
                                    last = (qi == qb * 4 + 3) and (jx == nk - 1)
                                    self.pv(ab, slice((qi % 4) * 128, (qi % 4 + 1) * 128), kj, h,
                                            PT[:, r, jx * 128:(jx + 1) * 128], False, last,
                                            [("PT", r)] + vkeys(kj, h))
                                if qi == qb * 4 + 3:
                                    self.attn_norm(h, ab, qb)
                            steps.append((S_, P_, V_))
            RA = 2
            for k_ in range(min(RA, len(steps))):
                steps[k_][0]()
            for i_, (S_, P_, V_) in enumerate(steps):
                if i_ + RA < len(steps):
                    steps[i_ + RA][0]()
                P_()
                V_()
            for bi in range(2):
                wt, wk = self.load_w(self.d_wo[j, bi], KD, 512)
                for jj in range(4):
                    oc = bi * 4 + jj
                    for tbl in range(2):
                        pb = self.psum_alloc("proj")
                        for kd in range(KD):
                            self.mm(self.ps[:, pb, :], wt[:, kd, jj * 128:(jj + 1) * 128],
                                    self.hT[:, kd, tbl * 512:(tbl + 1) * 512], kd == 0, kd == KD - 1,
                                    [wk, ("h", kd, tbl)], [("ps", pb)])
                        tb = grp * 2 + tbl
                        xs = self.xT[:, oc, tb * 512:(tb + 1) * 512]
                        self.stt(xs, self.ps[:, pb, :], self.mod[:, l, 16 + oc, ci:ci + 1], xs, ALU.mult, ALU.add,
                                 [("ps", pb), ("mod", l), ("xall",)], [("x", oc, tb)])
        S.barrier()


    def yT_view(self, yc):
        if yc < 8:
            return self.hT[:, yc, 1024:2048], ("h", yc)
        return self.arena[:, 0:8192].rearrange("p (k t) -> p k t", k=8)[:, yc - 8, :], ("yhi", yc - 8)

    def ykeys(self, yc, tbl):
        if yc < 8:
            return [("h", yc, 2 + tbl)]
        return [("yhi", yc - 8, tbl)]

    def ssd_layer(self, l):
        j = l // 2
        S = self.S
        S.barrier()
        A = self.arena

        def f32v(off, n):
            return A[:, off:off + 2 * n].bitcast(F32)

        Rt = f32v(8192, 1024)
        Rhm = A[:, 8192:10240].rearrange("p (r t) -> p r t", r=2)
        tok3 = f32v(10240, 1536).rearrange("p (t c) -> p t c", t=8)
        decbc = f32v(13312, 512).rearrange("p (k c) -> p k c", k=64)
        sm = f32v(14336, 128)
        dvec, ngain, dtp, negA = sm[:, 0:16], sm[:, 16:32], sm[:, 32:34], sm[:, 34:35]
        convp = sm[:, 40:72].rearrange("p (r c) -> p r c", r=2)
        G0 = 14592
        xcT = A[:, G0:G0 + 2048].rearrange("p (k t) -> p k t", k=2)
        BT = A[:, G0 + 2048:G0 + 3072]
        CT = A[:, G0 + 3072:G0 + 4096]
        xbtok = A[:, G0 + 4096:G0 + 7168].rearrange("p (t c) -> p t c", t=8)
        sin = A[:, G0 + 7168:G0 + 11264].rearrange("p (t d c) -> p t d c", t=8, d=2)
        Sst = f32v(G0 + 11264, 512).rearrange("p (d c) -> p d c", d=2)
        xs = A[:, G0 + 12288:G0 + 12800].rearrange("p (r c) -> p r c", r=2)
        cbLU = A[:, G0 + 12800:G0 + 13824].rearrange("p (r c) -> p r c", r=2)
        Q0 = G0 + 13824
        Ebuf = A[:, Q0:Q0 + 1024].rearrange("p (r c) -> p r c", r=2)
        Wbuf = A[:, Q0 + 1024:Q0 + 3072].rearrange("p (r c) -> p r c", r=4)
        eabc = A[:, Q0 + 3072:Q0 + 4096].rearrange("p (r c) -> p r c", r=2)
        coff = A[:, Q0 + 4096:Q0 + 6144].rearrange("p (r c) -> p r c", r=4)
        Xm = f32v(Q0 + 6144, 1024).rearrange("p (r c) -> p r c", r=2)
        assert Q0 + 6144 + 2048 <= ARENA
        TR = [f32v(G0 + i * 2048, 1024) for i in range(6)]
        self.pools.update({"norm": [5], "u2": [0, 2], "tr": [4], "rbc": [0, 1, 2, 3, 4], "cb": [5], "y": [6, 7], "sch": [6, 7]})
        ctmp = [(self.tmpn[:, :, :].rearrange("p r t -> p (r t)"), [("tmpn", 0), ("tmpn", 1)]),
                (self.rstd[:, :, :].rearrange("p r t -> p (r t)"), [("rstd", 0), ("rstd", 1)])]
        ysq = self.sq[:, :, :].rearrange("p a t -> p (a t)").bitcast(F32)

        self.dma("sp", dvec, self.d_dvec[j], [], [("ssdv", 0)], slot=("cst", 6))
        self.dma("sp", ngain, self.d_ngain[j], [], [("ssdv", 1)], slot=("cst", 7))
        self.dma("sp", dtp[0:64, :], self.d_dtp[j], [], [("ssdv", 2)], slot=("cst", 8))
        self.act(negA[0:64, :], dtp[0:64, 1:2], AF.Exp, [("ssdv", 2)], [("negA",)])
        self.ts(negA[0:64, :], negA[0:64, :], -1.0, ALU.mult, [("negA",)], [("negA",)])
        cvi = [0]
        cti = [0]
        cnt = {"xs": 0, "E": 0, "W": 0, "ea": 0, "co": 0, "ysq": 0, "X": 0}

        for grp in (0, 1):
            ci = grp
            S.barrier()
            nseq, L = (1, 1024) if grp == 0 else (4, 256)
            for tbl in range(2):
                self.norm_mod(grp * 2 + tbl,
                              lambda kd: self.Amix[:, l, kd, ci:ci + 1],
                              lambda kd: self.mod[:, l, kd, ci:ci + 1],
                              [("A", l, 1), ("mod", l)],
                              lambda kd, tbl=tbl: self.hT[:, kd, tbl * 512:(tbl + 1) * 512],
                              lambda kd, tbl=tbl: [("h", kd, tbl)])
            E1, DT, LNDT, DA, ACS, QF = TR
            CM = Rt
            wt, wk = self.load_w(self.d_wdt[j], KD, 64)
            for tbl in range(2):
                pb = self.psum_alloc("rbc")
                for kd in range(KD):
                    self.mm(self.ps[0:64, pb, :], wt[:, kd, 0:64], self.hT[:, kd, tbl * 512:(tbl + 1) * 512],
                            kd == 0, kd == KD - 1, [wk, ("h", kd, tbl)], [("ps", pb)])
                sl = slice(tbl * 512, (tbl + 1) * 512)
                self.act(E1[0:64, sl], self.ps[0:64, pb, :], AF.Exp, [("ps", pb), ("ssdv", 2)], [("E1",)],
                         bias=dtp[0:64, 0:1])
            self.act(DT[0:64, :], E1[0:64, :], AF.Ln, [("E1",)], [("DT",)], bias=self.onec[0:64, 0:1])
            self.act(LNDT[0:64, :], DT[0:64, :], AF.Ln, [("DT",)], [("LNDT",)])
            self.ts(DA[0:64, :], DT[0:64, :], negA[0:64, 0:1], ALU.mult, [("DT",), ("negA",)], [("DA",)])
            self.op("dve", lambda e: e.memset(CM[0:64, :], 1.0), [], [("R",)])
            self.op("dve", lambda e: e.memset(CM[0:64, :].rearrange("p (c q) -> p c q", q=128)[:, :, 0:1], 0.0),
                    [("R",)], [("R",)])
            self.op("dve", lambda e: e.tensor_tensor_scan(out=ACS[0:64, :], data0=CM[0:64, :], data1=DA[0:64, :],
                                                          initial=0.0, op0=ALU.mult, op1=ALU.add),
                    [("R",), ("DA",)], [("ACS",)])
            a3 = ACS[:, :].rearrange("p (c q) -> p c q", q=128)
            r3 = Rt[:, :].rearrange("p (c q) -> p c q", q=128)
            self.copy(Rt[0:32, :], ACS[0:32, :], [("ACS",)], [("R",)])
            self.tt(E1[32:64, :], DA[32:64, :], ACS[32:64, :], ALU.subtract, [("DA",), ("ACS",), ("DT",)], [("E1",)])
            self.tt(r3[32:64, :, :], E1[32:64, :].rearrange("p (c q) -> p c q", q=128),
                    a3[32:64, :, 127:128].to_broadcast([32, 8, 128]), ALU.add, [("E1",), ("ACS",)], [("R",)])
            self.tt(QF[0:64, :], LNDT[0:64, :], Rt[0:64, :], ALU.subtract, [("LNDT",), ("R",)], [("QF",)])
            q3 = QF[:, :].rearrange("p (c q) -> p c q", q=128)
            e3 = E1[:, :].rearrange("p (c q) -> p c q", q=128)
            self.tt(e3[0:32, :, :], q3[0:32, :, :], r3[0:32, :, 127:128].to_broadcast([32, 8, 128]), ALU.add,
                    [("QF",), ("R",), ("E1",)], [("E1",)])
            self.tt(e3[32:64, :, :], q3[32:64, :, :], r3[32:64, :, 0:1].to_broadcast([32, 8, 128]), ALU.add,
                    [("QF",), ("R",), ("E1",)], [("E1",)])
            self.act(QF[64:128, :], E1[0:64, :], AF.Exp, [("E1",)], [("QF",)])
            Tm = ACS[0:64, 0:8]
            self.copy(ACS[0:32, 0:8].unsqueeze(2), r3[0:32, :, 127:128], [("R",), ("ACS",), ("E1",)], [("ACS",)])
            self.copy(ACS[32:64, 0:8].unsqueeze(2), r3[32:64, :, 0:1], [("R",), ("ACS",)], [("ACS",)])
            Texp = DA[0:64, 0:512].rearrange("p (k c) -> p k c", k=64)
            self.tt(Texp, self.identF[0:64, 0:64].unsqueeze(2).to_broadcast([64, 64, 8]),
                    Tm.unsqueeze(1).to_broadcast([64, 64, 8]), ALU.mult, [("ACS",), ("identF",), ("DA",)], [("DA",)])
            pb = self.psum_alloc("rbc")
            self.mm(self.ps[:, pb, :], self.onesF[0:64, :], DA[0:64, 0:512], True, True, [("DA",), ("onesF",)],
                    [("ps", pb)])
            self.act(decbc.rearrange("p k c -> p (k c)"), self.ps[:, pb, :], AF.Exp, [("ps", pb)], [("decbc",)])
            for tile in range(8):
                tb_ = self.psum_alloc("tr")
                self.op("pe", lambda e, tb_=tb_, tile=tile: e.transpose(self.ps[:, tb_, 0:128],
                                                                      QF[:, tile * 128:(tile + 1) * 128],
                                                                      self.identF[:, :]),
                        [("QF",), ("identF",)], [("ps", tb_)])
                self.op("pe", lambda e, tb_=tb_, tile=tile: e.transpose(self.ps[:, tb_, 128:192],
                                                                      Rt[0:64, tile * 128:(tile + 1) * 128],
                                                                      self.identF[0:64, 0:64]),
                        [("R",), ("identF",)], [("ps", tb_)])
                self.copy(tok3[:, tile, :], self.ps[:, tb_, 0:192], [("ps", tb_)], [("tok3",)])
            hiT = E1[0:64, 0:512].bitcast(BF16)
            self.copy(hiT, Rt[0:64, :], [("R",)], [("E1",)])
            self.tt(DT[0:64, :], Rt[0:64, :], hiT, ALU.subtract, [("R",), ("E1",)], [("DT",)])
            self.copy(Rhm[0:64, 0, :], hiT, [("E1",)], [("R",)])
            self.copy(Rhm[0:64, 1, :], DT[0:64, :], [("DT",)], [("R",)])
            S.barrier()

            for g in range(8):
                cr = cvi[0] % 2
                cvi[0] += 1
                self.dma("sp", convp[:, cr, :], self.d_convp[j, g], [], [("convp", cr)], slot=("convp", cr))
                wtA, wkA = self.load_w(self.d_winA[j, g], KD, 512)
                wtB, wkB = self.load_w(self.d_winB[j, g], KD, 256)
                plan = [(wtA, wkA, 0, "z", 0), (wtA, wkA, 1, "z", 1), (wtA, wkA, 2, "x", 0), (wtA, wkA, 3, "x", 1),
                        (wtB, wkB, 0, "B", 0), (wtB, wkB, 1, "C", 0)]
                for (wt, wk, jc, kind, idx) in plan:
                    ub = self.psum_alloc("u2")
                    for tbl in range(2):
                        for kd in range(KD):
                            self.mm(self.ps[:, ub + tbl, :], wt[:, kd, jc * 128:(jc + 1) * 128],
                                    self.hT[:, kd, tbl * 512:(tbl + 1) * 512], kd == 0, kd == KD - 1,
                                    [wk, ("h", kd, tbl)], [("ps", ub), ("ps", ub + 1)])
                    u = self.ps[:, ub:ub + 2, :].rearrange("p b c -> p (b c)")
                    uk = [("ps", ub), ("ps", ub + 1)]
                    if kind == "z":
                        yv, _ = self.yT_view(2 * g + idx)
                        self.act(yv, u, AF.Silu, uk, self.ykeys(2 * g + idx, 0) + self.ykeys(2 * g + idx, 1))
                        continue
                    cidx = {"x": idx, "B": 2, "C": 3}[kind]
                    cp = convp[:, cr, cidx * 4:cidx * 4 + 4]
                    tbuf, tkeys = ctmp[cti[0] % 2]
                    cti[0] += 1
                    self.act(tbuf, u, AF.Identity, uk + [("convp", cr)], tkeys, bias=cp[:, 3:4], scale=cp[:, 1:2])
                    u3 = u.rearrange("p (s q) -> p s q", s=nseq)
                    t3 = tbuf.rearrange("p (s q) -> p s q", s=nseq)
                    self.stt(t3[:, :, 1:L], u3[:, :, 0:L - 1], cp[:, 0:1], t3[:, :, 1:L], ALU.mult, ALU.add,
                             uk + tkeys + [("convp", cr)], tkeys)
                    self.stt(t3[:, :, 0:L - 1], u3[:, :, 1:L], cp[:, 2:3], t3[:, :, 0:L - 1], ALU.mult, ALU.add,
                             uk + tkeys + [("convp", cr)], tkeys)
                    dst, dk = {"x": (xcT[:, idx, :], ("xcT", idx)), "B": (BT, ("BT",)), "C": (CT, ("CT",))}[kind]
                    self.act(dst, tbuf, AF.Silu, tkeys, [dk])
                for tile in range(8):
                    tb_ = self.psum_alloc("tr")
                    psb = self.ps[:, tb_, :].bitcast(BF16)
                    for q_, (src, sk) in enumerate(((xcT[:, 0, :], ("xcT", 0)), (xcT[:, 1, :], ("xcT", 1)),
                                                    (BT, ("BT",)))):
                        self.op("pe", lambda e, psb=psb, q_=q_, src=src, tile=tile: e.transpose(
                            psb[:, q_ * 128:(q_ + 1) * 128], src[:, tile * 128:(tile + 1) * 128], self.identB[:, :]),
                            [sk, ("identB",)], [("ps", tb_)])
                    self.copy(xbtok[:, tile, :], psb[:, 0:384], [("ps", tb_)], [("xbtok", tile)], eng="act")
                for sq_ in range(nseq):
                    nct = L // 128
                    ft = sq_ * nct
                    for d in range(2):
                        Sd = Sst[:, d, :]
                        if grp == 0:
                            self.dma("sp", Sd, self.d_state0[j, d, :, g * 256:(g + 1) * 256], [], [("S", d)],
                                     slot=("st0", d))
                        else:
                            self.op("dve", lambda e, Sd=Sd: e.memset(Sd, 0.0), [], [("S", d)])
                    seq_steps = [(ci_, d) for ci_ in range(nct) for d in range(2)]

                    def pre(ci_, d, ft=ft, nct=nct):
                        c = ci_ if d == 0 else nct - 1 - ci_
                        tile = ft + c
                        r = cnt["xs"] % 2
                        cnt["xs"] += 1
                        k0 = 64 + d * 32 + 4 * g
                        self.tt(xs[:, r, :].rearrange("p (a b) -> p a b", a=4),
                                xbtok[:, tile, 0:256].rearrange("p (a b) -> p a b", a=4),
                                tok3[:, tile, k0:k0 + 4].unsqueeze(2).to_broadcast([128, 4, 64]), ALU.mult,
                                [("xbtok", tile), ("tok3",)], [("xs", r)])
                        sb_ = self.psum_alloc("sch")
                        self.mm(self.ps[:, sb_, 0:256], xbtok[:, tile, 256:384], xs[:, r, :], True, True,
                                [("xbtok", tile), ("xs", r)], [("ps", sb_)])
                        return tile, sb_

                    def post(d, tile, sb_):
                        Sd = Sst[:, d, :]
                        self.copy(sin[:, tile, d, :], Sd, [("S", d)], [("sin", tile, d)], eng="act")
                        kd0 = d * 32 + 4 * g
                        self.tt(Sd.rearrange("p (a b) -> p a b", a=4), Sd.rearrange("p (a b) -> p a b", a=4),
                                decbc[:, kd0:kd0 + 4, tile:tile + 1].to_broadcast([128, 4, 64]), ALU.mult,
                                [("S", d), ("decbc",)], [("S", d)])
                        self.tt(Sd, Sd, self.ps[:, sb_, 0:256], ALU.add, [("S", d), ("ps", sb_)], [("S", d)])

                    pend = [pre(*seq_steps[0])]
                    for idx_, (ci_, d) in enumerate(seq_steps):
                        if idx_ + 1 < len(seq_steps):
                            pend.append(pre(*seq_steps[idx_ + 1]))
                        tile, sb_ = pend.pop(0)
                        post(d, tile, sb_)
                    if grp == 1:
                        for d in range(2):
                            self.dma("sp", self.d_stout[j, sq_, d, :, g * 256:(g + 1) * 256], Sst[:, d, :],
                                     [("S", d)], [], slot=("stout", d), final=True)
                units = [(tbl, hp, hh2) for tbl in range(2) for hp in range(2) for hh2 in range(2)]
                st1 = {}

                def stage_cb(tbl):
                    tsl = slice(tbl * 512, (tbl + 1) * 512)
                    cb = self.psum_alloc("cb")
                    for q_ in range(4):
                        tile = tbl * 4 + q_
                        self.mm(self.ps[:, cb, q_ * 128:(q_ + 1) * 128], BT[:, tile * 128:(tile + 1) * 128],
                                CT[:, tile * 128:(tile + 1) * 128], True, True, [("BT",), ("CT",)], [("ps", cb)])
                    for mi in range(2):
                        self.tt(cbLU[:, mi, :].rearrange("p (a b) -> p a b", a=4),
                                self.ps[:, cb, :].rearrange("p (a b) -> p a b", a=4),
                                self.maskLU[:, mi, :].unsqueeze(1).to_broadcast([128, 4, 128]), ALU.mult,
                                [("ps", cb), ("maskLU",)], [("cbLU", mi)])

                dirs = [(u, d) for u in units for d in range(2)]
                info = {}

                rbank = {}

                def stageP(i):
                    (tbl, hp, hh2), d = dirs[i]
                    tsl = slice(tbl * 512, (tbl + 1) * 512)
                    k = d * 32 + 4 * g + hp * 2 + hh2
                    rb = self.psum_alloc("rbc")
                    rbank[i] = rb
                    for r_ in range(2):
                        self.mm(self.ps[:, rb, :], self.identB[0:64, k:k + 1].to_broadcast([64, 128]),
                                Rhm[0:64, r_, tsl], r_ == 0, r_ == 1, [("R",), ("identB",)], [("ps", rb)])

                def stageA(i):
                    (tbl, hp, hh2), d = dirs[i]
                    tsl = slice(tbl * 512, (tbl + 1) * 512)
                    h = 4 * g + hp * 2 + hh2
                    k = d * 32 + h
                    rb = rbank[i]
                    er = cnt["ea"] % 2
                    cnt["ea"] += 1
                    self.act(eabc[:, er, :], self.ps[:, rb, :], AF.Exp, [("ps", rb)], [("eabc", er)])
                    xi = cnt["X"] % 2
                    cnt["X"] += 1
                    self.tt(Xm[:, xi, :].rearrange("p (a b) -> p a b", a=4),
                            self.ps[:, rb, :].rearrange("p (a b) -> p a b", a=4),
                            tok3[:, tbl * 4:tbl * 4 + 4, 128 + k:128 + k + 1].to_broadcast([128, 4, 128]),
                            ALU.min, [("ps", rb), ("tok3",)], [("X", xi)])
                    info[i] = (er, xi)

                def stageB(i):
                    (tbl, hp, hh2), d = dirs[i]
                    u = dirs[i][0]
                    tsl = slice(tbl * 512, (tbl + 1) * 512)
                    h = 4 * g + hp * 2 + hh2
                    k = d * 32 + h
                    er, xi = info[i]
                    ei = cnt["E"] % 2
                    cnt["E"] += 1
                    for q_ in range(4):
                        tile = tbl * 4 + q_
                        self.act(Ebuf[:, ei, q_ * 128:(q_ + 1) * 128], Xm[:, xi, q_ * 128:(q_ + 1) * 128],
                                 AF.Exp, [("X", xi), ("tok3",)], [("E", ei, q_)], bias=tok3[:, tile, k:k + 1])
                    co = cnt["co"] % 4
                    cnt["co"] += 1
                    self.tt(coff[:, co, :], CT[:, tsl], eabc[:, er, :], ALU.mult,
                            [("CT",), ("eabc", er)], [("coff", co)])
                    wi = cnt["W"] % 4
                    cnt["W"] += 1
                    self.tt(Wbuf[:, wi, :], Ebuf[:, ei, :], cbLU[:, d, :], ALU.mult,
                            [("E", ei, q_) for q_ in range(4)] + [("cbLU", d)], [("W", wi)])
                    wr, cr_ = st1.setdefault(u, ({}, {}))
                    wr[d] = wi
                    cr_[d] = co

                ybank = {}

                def stage2(u):
                    tbl, hp, hh2 = u
                    tsl = slice(tbl * 512, (tbl + 1) * 512)
                    hh = hp * 2 + hh2
                    po = hh2 * 64
                    wr, cr_ = st1[u]
                    if hh2 == 0:
                        ybank[(tbl, hp)] = self.psum_alloc("y")
                    yb = ybank[(tbl, hp)]
                    for q_ in range(4):
                        tile = tbl * 4 + q_
                        qs = slice(q_ * 128, (q_ + 1) * 128)
                        out = self.ps[po:po + 64, yb, qs]
                        xl = xbtok[:, tile, hh * 64:(hh + 1) * 64]
                        self.mm(out, xl, Wbuf[:, wr[0], qs], True, False,
                                [("xbtok", tile), ("W", wr[0])], [("ps", yb)])
                        self.mm(out, xl, Wbuf[:, wr[1], qs], False, False,
                                [("xbtok", tile), ("W", wr[1])], [("ps", yb)])
                        for d in range(2):
                            self.mm(out, sin[:, tile, d, hh * 64:(hh + 1) * 64], coff[:, cr_[d], qs],
                                    False, d == 1, [("sin", tile, d), ("coff", cr_[d])], [("ps", yb)])
                    if hh2 == 1:
                        yc = 2 * g + hp
                        self.stt(ysq, xcT[:, hp, tsl], dvec[:, yc:yc + 1], self.ps[:, yb, :],
                                 ALU.mult, ALU.add, [("xcT", hp), ("ssdv", 0), ("ps", yb)], [("sq", 0), ("sq", 1)])
                        yv, _ = self.yT_view(yc)
                        self.tt(yv[:, tsl], ysq, yv[:, tsl], ALU.mult,
                                [("sq", 0), ("sq", 1)] + self.ykeys(yc, tbl), self.ykeys(yc, tbl))

                stage_cb(0)
                AHEAD = 4
                for i in range(min(AHEAD, len(dirs))):
                    stageP(i)
                stageA(0)
                for i in range(len(dirs)):
                    if i + AHEAD < len(dirs):
                        stageP(i + AHEAD)
                    if i + 1 < len(dirs):
                        stageA(i + 1)
                    if i + 1 < len(dirs) and dirs[i + 1][0][0] != dirs[i][0][0] and dirs[i + 1][1] == 0:
                        pass
                    if dirs[i][1] == 0 and dirs[i][0][1] == 0 and dirs[i][0][2] == 0 and dirs[i][0][0] == 1:
                        stage_cb(1)
                    stageB(i)
                    if dirs[i][1] == 1:
                        stage2(dirs[i][0])
            for tbl in range(2):
                tsl = slice(tbl * 512, (tbl + 1) * 512)
                nb = self.psum_alloc("norm")
                for yc in range(16):
                    yv, _ = self.yT_view(yc)
                    sr = self.sqi % 2
                    self.sqi += 1
                    self.act(self.sq[:, sr, :], yv[:, tsl], AF.Square, self.ykeys(yc, tbl), [("sq", sr)])
                    self.mm(self.ps[:, nb, :], self.ones_bf[:, :], self.sq[:, sr, :], yc == 0, yc == 15,
                            [("sq", sr), ("ones",)], [("ps", nb)])
                r = self.uid % 2
                self.uid += 1
                rs = self.rstd[:, r, :]
                rk = ("rstd", r)
                self.act(rs, self.ps[:, nb, :], AF.Sqrt, [("ps", nb), ("eps",)], [rk], bias=self.epsc[:, 0:1],
                         scale=1.0 / 2048)
                self.op("dve", lambda e, rs=rs: e.reciprocal(rs, rs), [rk], [rk])
                for yc in range(16):
                    yv, _ = self.yT_view(yc)
                    self.stt(yv[:, tsl], yv[:, tsl], ngain[:, yc:yc + 1], rs, ALU.mult, ALU.mult,
                             self.ykeys(yc, tbl) + [rk, ("ssdv", 1)], self.ykeys(yc, tbl))
            for cbk in range(4):
                wt, wk = self.load_w(self.d_wout[j, cbk], 16, 256)
                for m in range(2):
                    oc = cbk * 2 + m
                    for tbl in range(2):
                        tsl = slice(tbl * 512, (tbl + 1) * 512)
                        pb = self.psum_alloc("rbc")
                        for yc in range(16):
                            yv, _ = self.yT_view(yc)
                            self.mm(self.ps[:, pb, :], wt[:, yc, m * 128:(m + 1) * 128], yv[:, tsl], yc == 0, yc == 15,
                                    [wk] + self.ykeys(yc, tbl), [("ps", pb)])
                        tb = grp * 2 + tbl
                        xs_ = self.xT[:, oc, tb * 512:(tb + 1) * 512]
                        self.stt(xs_, self.ps[:, pb, :], self.mod[:, l, 16 + oc, ci:ci + 1], xs_, ALU.mult, ALU.add,
                                 [("ps", pb), ("mod", l), ("xall",)], [("x", oc, tb)])
        S.barrier()

    def emit(self):
        nc = self.nc
        names = self.S.finalize()
        sems = {}
        for i, n in enumerate(names):
            sems[n] = self.es.enter_context(nc.semaphore("s%d" % i))
        S = self.S
        with nc.Block() as block:
            @block.tensor
            def _(e):
                S.replay("pe", e, sems)

            @block.scalar
            def _(e):
                S.replay("act", e, sems)

            @block.vector
            def _(e):
                S.replay("dve", e, sems)

            @block.gpsimd
            def _(e):
                S.replay("pool", e, sems)

            @block.sync
            def _(e):
                S.replay("sp", e, sems)
        self.es.close()
        return nc


def _chunk_vec(v):
    sh = v.shape
    c = sh[-1] // 128
    return np.ascontiguousarray(np.swapaxes(v.reshape(sh[:-1] + (c, 128)), -1, -2))


def _wblocks(w, cb):
    K, N = w.shape
    return np.ascontiguousarray(w.reshape(K // 128, 128, N // cb, cb).transpose(2, 1, 0, 3))


def prep_shared(inp, nl):
    sh = {}
    sh["ada_w"] = np.stack([_wblocks(inp["ada_w"][l], 512) for l in range(DEPTH)])
    sh["ada_b"] = np.stack([_chunk_vec(inp["ada_b"][l]) for l in range(DEPTH)])
    sh["gmix"] = np.stack([_chunk_vec(inp["norm_mix_g"][l]) for l in range(DEPTH)])
    sh["gffn"] = np.stack([_chunk_vec(inp["norm_ffn_g"][l]) for l in range(DEPTH)])
    sh["gfin"] = _chunk_vec(inp["final_norm_g"])
    idx = []
    for b in range(11):
        for part in (0, 1):
            for jj in (0, 1):
                j = 2 * b + jj
                idx.append(np.arange(part * DFF + j * 128, part * DFF + (j + 1) * 128))
    idx = np.concatenate(idx)
    sh["w_gu"] = np.stack([_wblocks(inp["ffn_w_gate_up"][l][:, idx], 512) for l in range(DEPTH)])
    sh["w_dn"] = np.stack([_wblocks(inp["ffn_w_down"][l], 256) for l in range(DEPTH)])
    sh["w_qkv"] = np.stack([_wblocks(inp["na_w_qkv"][j], 512) for j in range(2)])
    sh["w_o"] = np.stack([_wblocks(inp["na_w_o"][j], 512) for j in range(2)])
    a = np.arange(2)[:, None, None, None]
    kc = np.arange(64)[None, :, None, None]
    mm = np.arange(14)[None, None, :, None]
    c = np.arange(64)[None, None, None, :]
    dr = a - mm + 6 + 0 * kc + 0 * c
    dc = np.clip(kc - c + 15, 0, 30) + 0 * a + 0 * mm
    c0 = np.clip(c - 8, 0, 48)
    colok = (kc >= c0) & (kc < c0 + 16) & (a >= 0) & (mm >= 0)
    okf = colok
    oki = colok & (dr >= -4) & (dr <= 3)
    rpb = inp["na_rpb"]
    g = rpb[:, :, dr + 7, dc]
    neg = np.float32(-30000.0)
    bf = np.where(okf[None, None], g, neg).reshape(2, NH, 128, 896)
    bi_ = np.where(oki[None, None], g, neg).reshape(2, NH, 128, 896)
    sh["braw"] = np.ascontiguousarray(np.concatenate([bf, bi_], axis=-1).astype(np.float32))
    wA, wB, wdt, cvp, dtp, dvec, ngain, wout = [], [], [], [], [], [], [], []
    for j in range(2):
        w_in = inp["ssd_w_in"][j]
        a_, b_, c_ = [], [], []
        for g in range(8):
            colsA = np.concatenate([g * 256 + np.arange(256), 2048 + g * 256 + np.arange(256)])
            colsB = np.concatenate([4096 + g * 128 + np.arange(128), 5120 + g * 128 + np.arange(128)])
            a_.append(_wblocks(w_in[:, colsA], 512)[0])
            b_.append(_wblocks(w_in[:, colsB], 256)[0])
            chs = [g * 256 + np.arange(128), g * 256 + 128 + np.arange(128), 2048 + g * 128 + np.arange(128),
                   3072 + g * 128 + np.arange(128)]
            cp = np.zeros((128, 16), np.float32)
            for ci_, ch in enumerate(chs):
                cp[:, ci_ * 4:ci_ * 4 + 3] = inp["ssd_conv_w"][j][:, ch].T
                cp[:, ci_ * 4 + 3] = inp["ssd_conv_b"][j][ch]
            c_.append(cp)
        wA.append(np.stack(a_)); wB.append(np.stack(b_)); cvp.append(np.stack(c_))
        wdt.append(_wblocks(w_in[:, 6144:6208], 64)[0])
        dtp.append(np.stack([inp["ssd_dt_bias"][j].reshape(64), inp["ssd_a_log"][j].reshape(64)], axis=-1))
        dvec.append(_chunk_vec(np.repeat(inp["ssd_d"][j], 64)))
        ngain.append(_chunk_vec(inp["ssd_norm_g"][j]))
        wout.append(_wblocks(inp["ssd_w_out"][j], 256))
    sh["w_inA"] = np.stack(wA); sh["w_inB"] = np.stack(wB); sh["w_dt"] = np.stack(wdt)
    sh["convp"] = np.stack(cvp); sh["dtp"] = np.ascontiguousarray(np.stack(dtp).astype(np.float32))
    sh["dvec"] = np.stack(dvec); sh["ngain"] = np.stack(ngain); sh["w_out"] = np.stack(wout)
    ident = np.eye(128, dtype=np.float32)
    sh["consts"] = np.stack([ident, np.triu(np.ones((128, 128), np.float32)), np.tril(np.ones((128, 128), np.float32))])
    return sh


def prep_core(inp, core):
    b = core // 2
    xs = inp["x_sample"][b]
    xp = inp["x_prompt"][4 * core:4 * core + 4].reshape(4 * 256, D)
    xT = np.ascontiguousarray(np.concatenate([xs, xp], axis=0).T)
    cond = np.stack([inp["c"][b], inp["c_ctx"]], axis=-1)
    cond = np.ascontiguousarray(cond.reshape(KD, 128, 2).transpose(1, 0, 2))
    kctxT = np.ascontiguousarray(inp["cache_k"][b].transpose(0, 1, 3, 2).reshape(2, D, 512))
    vctx = np.ascontiguousarray(inp["cache_v"][b].transpose(0, 2, 1, 3).reshape(2, 512, D))
    state0 = np.ascontiguousarray(inp["state_ssm"][b].transpose(0, 1, 4, 2, 3).reshape(2, 2, 128, 2048))
    return {"xT": xT, "cond": cond, "kctxT": kctxT, "vctx": vctx, "state0": state0}


_CACHE = {}


def get_program(cfg):
    if cfg not in _CACHE:
        bld = Builder(*cfg)
        bld.build()
        _CACHE[cfg] = bld.emit()
    return _CACHE[cfg]


def kernel(**inputs):
    inp = {k: np.asarray(v) for k, v in inputs.items()}
    nl = _env_int("K_NL", DEPTH)
    cfg = (nl, bool(_env_int("K_NA", 1)), bool(_env_int("K_SSD", 1)), bool(_env_int("K_FFN", 1)))
    nc = get_program(cfg)
    shared = prep_shared(inp, nl)
    in_maps = []
    for c in range(NCORES):
        m = dict(shared)
        m.update(prep_core(inp, c))
        in_maps.append(m)
    res = run_bass_kernel_spmd(nc, in_maps, core_ids=list(range(NCORES)))
    R = res.results
    y_prompt = np.zeros((32, 256, D), np.float32)
    y_sample = np.zeros((4, 1024, D), np.float32)
    for c in range(NCORES):
        yT = R[c]["yT"]
        if c % 2 == 0:
            y_sample[c // 2] = yT[:, 0:1024].T
        y_prompt[4 * c:4 * c + 4] = yT[:, 1024:2048].T.reshape(4, 256, D)
    new_k = np.zeros((32, 2, NH, 256, 64), np.float32)
    new_v = np.zeros((32, 2, NH, 256, 64), np.float32)
    for c in range(NCORES):
        kT = R[c]["kT_out"]
        vv = R[c]["v_out"]
        new_k[4 * c:4 * c + 4] = kT.reshape(2, NH, 64, 4, 256).transpose(3, 0, 1, 4, 2)
        new_v[4 * c:4 * c + 4] = vv.reshape(2, 4, 256, NH, 64).transpose(1, 0, 3, 2, 4)
    new_s = np.zeros((32, 2, 2, 32, 64, 128), np.float32)
    for c in range(NCORES):
        st = R[c]["st_out"]
        new_s[4 * c:4 * c + 4] = st.reshape(2, 4, 2, 128, 32, 64).transpose(1, 0, 2, 4, 5, 3)
    return (y_prompt, y_sample, new_k, new_v, new_s)
```
